# Optimizing a Trainium2 kernel written in Bass

```python
import math
import jax
import jax.numpy as jnp
from jax import lax
import numpy as np

D_MODEL = 1024
BATCH = 4
SEQ = 4096
DEPTH = 4
DEC_BATCH = 128
DEC_SEQ = 1
PAST_LEN = 8192
PAGE_SIZE = 128

ATT_HEADS = 8
ATT_KV_HEADS = 2
HEAD_DIM = 64
WINDOW = 128
ATT_BLOCK = 128
M_INNER = D_MODEL // 2
M_HEADDIM = 64
M_HEADS = M_INNER // M_HEADDIM
M_GROUPS = 2
M_STATE = 64
CONV_W = 4
CONV_DIM = M_INNER + 2 * M_GROUPS * M_STATE
M_CHUNK = 128
HG_WIDTH = D_MODEL // 2
HG_HEADS = 4
HG_DK = HG_WIDTH // HG_HEADS
HG_DV = HG_WIDTH // HG_HEADS
HG_CHUNK = 64
D_FF = 2816
N_BRANCH = 3
ALPHA = (2.0 * DEPTH) ** 0.25
BETA = (8.0 * DEPTH) ** -0.25
LN_EPS = 1e-5
RMS_EPS = 1e-6
COL_SPLITS = (ATT_HEADS * HEAD_DIM, ATT_KV_HEADS * HEAD_DIM, ATT_KV_HEADS * HEAD_DIM,
              M_INNER, CONV_DIM, M_HEADS,
              HG_WIDTH, HG_WIDTH, HG_WIDTH, HG_WIDTH,
              N_BRANCH * D_MODEL)
N_IN = sum(COL_SPLITS)

kernel_name = 'hybrid_swa_ssd_hgrn2_deepnorm_step'


def _layernorm(x, g, b):
    xf = x.astype(jnp.float32)
    mu = jnp.mean(xf, axis=-1, keepdims=True)
    var = jnp.mean(jnp.square(xf - mu), axis=-1, keepdims=True)
    return ((xf - mu) * lax.rsqrt(var + LN_EPS) * g.astype(jnp.float32) + b.astype(jnp.float32)).astype(x.dtype)


def _rmsnorm(x, w):
    xf = x.astype(jnp.float32)
    return (xf * lax.rsqrt(jnp.mean(xf * xf, axis=-1, keepdims=True) + RMS_EPS) * w.astype(jnp.float32)).astype(x.dtype)


def _swiglu(x, wg, wu, wd):
    return (jax.nn.silu(x @ wg) * (x @ wu)) @ wd


def _blk(T, C):
    return C if T % C == 0 else T


def _split_cols(h):
    out, start = [], 0
    for size in COL_SPLITS:
        out.append(h[..., start:start + size])
        start += size
    return out


def _swa_attention(q, k, v, k_prev, v_prev, n_prev_valid, sinks):
    b, T = q.shape[0], q.shape[1]
    blk = _blk(T, ATT_BLOCK)
    nb = T // blk
    grp = ATT_HEADS // ATT_KV_HEADS
    kc = jnp.concatenate([k_prev.astype(k.dtype), k], axis=1)
    vc = jnp.concatenate([v_prev.astype(v.dtype), v], axis=1)
    idx = jnp.arange(nb)[:, None] * blk + jnp.arange(blk + WINDOW)[None, :]
    kb = kc[:, idx]
    vb = vc[:, idx]
    qb = q.reshape(b, nb, blk, ATT_KV_HEADS, grp, HEAD_DIM)
    s = jnp.einsum('bnqkgd,bnskd->bnkgqs', qb, kb).astype(jnp.float32) * (HEAD_DIM ** -0.5)
    dist = WINDOW + jnp.arange(blk)[:, None] - jnp.arange(blk + WINDOW)[None, :]
    valid = (dist >= 0) & (dist <= WINDOW) & (idx[:, None, :] >= WINDOW - n_prev_valid)
    slopes = (2.0 ** (-8.0 * jnp.arange(1, ATT_HEADS + 1, dtype=jnp.float32) / ATT_HEADS)).reshape(ATT_KV_HEADS, grp, 1, 1)
    s = jnp.where(valid[None, :, None, None], s - slopes * dist.astype(jnp.float32), -jnp.inf)
    sink = sinks.astype(jnp.float32).reshape(ATT_KV_HEADS, grp, 1, 1)
    m = jnp.maximum(jnp.max(s, axis=-1, keepdims=True), sink)
    p = jnp.exp(s - m)
    p = p / (jnp.sum(p, axis=-1, keepdims=True) + jnp.exp(sink - m))
    o = jnp.einsum('bnkgqs,bnskd->bnqkgd', p.astype(v.dtype), vb)
    return o.reshape(b, T, ATT_HEADS * HEAD_DIM), kc[:, -WINDOW:], vc[:, -WINDOW:]


def _ssd(xh, dt, A, Bm, Cm, h0):
    b, T = xh.shape[0], xh.shape[1]
    L = _blk(T, M_CHUNK)
    nc = T // L
    hpg = M_HEADS // M_GROUPS
    x = xh.astype(jnp.float32).reshape(b, nc, L, M_GROUPS, hpg, M_HEADDIM)
    dtc = dt.reshape(b, nc, L, M_GROUPS, hpg)
    Bc = Bm.astype(jnp.float32).reshape(b, nc, L, M_GROUPS, M_STATE)
    Cc = Cm.astype(jnp.float32).reshape(b, nc, L, M_GROUPS, M_STATE)
    a = jnp.cumsum(dtc * A.reshape(M_GROUPS, hpg), axis=2)
    causal = jnp.tril(jnp.ones((L, L), bool))[:, :, None, None]
    decay = jnp.exp(jnp.where(causal, a[:, :, :, None] - a[:, :, None], -jnp.inf))
    w = jnp.einsum('bctgn,bcsgn->bctsg', Cc, Bc)[..., None] * decay * dtc[:, :, None]
    y = jnp.einsum('bctsgh,bcsghp->bctghp', w, x)
    cs = jnp.einsum('bcsgh,bcsgn,bcsghp->bcghpn', jnp.exp(a[:, :, -1:] - a) * dtc, Bc, x)

    def step(hc, inp):
        dec, s_c = inp
        return dec[..., None, None] * hc + s_c, hc

    hT, hs = lax.scan(step, h0.astype(jnp.float32).reshape(b, M_GROUPS, hpg, M_HEADDIM, M_STATE),
                      (jnp.moveaxis(jnp.exp(a[:, :, -1]), 1, 0), jnp.moveaxis(cs, 1, 0)))
    y = y + jnp.einsum('bctgn,cbghpn->bctghp', Cc, hs) * jnp.exp(a)[..., None]
    return y.reshape(b, T, M_HEADS, M_HEADDIM), hT.reshape(b, M_HEADS, M_HEADDIM, M_STATE)


def _mamba2(z, xbc, dt_raw, conv_prev, h0, conv_w, conv_b, dt_bias, a_log, d_skip, norm_w):
    b, T = xbc.shape[0], xbc.shape[1]
    xcat = jnp.concatenate([conv_prev.astype(xbc.dtype), xbc], axis=1)
    conv = lax.conv_general_dilated(xcat, conv_w.astype(xbc.dtype)[:, None, :], window_strides=(1,),
                                    padding='VALID', dimension_numbers=('NWC', 'WIO', 'NWC'),
                                    feature_group_count=CONV_DIM)
    act = jax.nn.silu(conv + conv_b)
    xs = act[..., :M_INNER].reshape(b, T, M_HEADS, M_HEADDIM)
    Bm = act[..., M_INNER:M_INNER + M_GROUPS * M_STATE].reshape(b, T, M_GROUPS, M_STATE)
    Cm = act[..., M_INNER + M_GROUPS * M_STATE:].reshape(b, T, M_GROUPS, M_STATE)
    dt = jax.nn.softplus(dt_raw.astype(jnp.float32) + dt_bias.astype(jnp.float32))
    A = -jnp.exp(a_log.astype(jnp.float32))
    y, hT = _ssd(xs, dt, A, Bm, Cm, h0)
    y = y + d_skip.astype(jnp.float32)[:, None] * xs.astype(jnp.float32)
    y = y.reshape(b, T, M_INNER) * jax.nn.silu(z.astype(jnp.float32))
    return _rmsnorm(y, norm_w).astype(z.dtype), xcat[:, -(CONV_W - 1):], hT


def _hgrn2(q, f_logit, i, lb, S0):
    b, T = q.shape[0], q.shape[1]
    qh = jax.nn.silu(q.astype(jnp.float32)).reshape(b, T, HG_HEADS, HG_DK)
    fl = f_logit.astype(jnp.float32).reshape(b, T, HG_HEADS, HG_DK)
    lbh = lb.reshape(HG_HEADS, HG_DK)
    log_f = jnp.logaddexp(jnp.log(lbh), jnp.log1p(-lbh) + jax.nn.log_sigmoid(fl))
    kh = (1.0 - lbh) * jax.nn.sigmoid(-fl)
    ih = i.astype(jnp.float32).reshape(b, T, HG_HEADS, HG_DV)
    L = _blk(T, HG_CHUNK)
    nc = T // L

    def chunks(arr):
        return jnp.moveaxis(arr.reshape(b, nc, L, HG_HEADS, arr.shape[-1]), 1, 0)

    causal = jnp.tril(jnp.ones((L, L), bool))[None, :, :, None, None]

    def step(S, inp):
        qc, kc, ic, gc = inp
        bc = jnp.cumsum(gc, axis=1)
        dec = jnp.exp(jnp.where(causal, bc[:, :, None] - bc[:, None, :], -jnp.inf))
        att = jnp.einsum('bthk,btshk->bhts', qc, dec * kc[:, None])
        o = jnp.einsum('bhts,bshv->bthv', att, ic) + jnp.einsum('bthk,bhkv->bthv', qc * jnp.exp(bc), S)
        S = jnp.exp(bc[:, -1])[..., None] * S + jnp.einsum('bshk,bshv->bhkv', kc * jnp.exp(bc[:, -1:] - bc), ic)
        return S, o

    S_T, o = lax.scan(step, S0.astype(jnp.float32), (chunks(qh), chunks(kh), chunks(ih), chunks(log_f)))
    return jnp.moveaxis(o, 0, 1).reshape(b, T, HG_HEADS, HG_DV), S_T


def _mixer(x, prev, n_prev_valid, lb, p):
    k_prev, v_prev, conv_prev, ssm_prev, hg_prev = prev
    b, T = x.shape[0], x.shape[1]
    h = x @ p['w_in'] + p['b_in']
    q, k, v, mz, mxbc, mdt, hq, hf, hi, hg, gates = _split_cols(h)
    ya, nk, nv = _swa_attention(q, k.reshape(b, T, ATT_KV_HEADS, HEAD_DIM), v.reshape(b, T, ATT_KV_HEADS, HEAD_DIM),
                                k_prev, v_prev, n_prev_valid, p['att_sinks'])
    ym, nconv, nssm = _mamba2(mz, mxbc, mdt, conv_prev, ssm_prev, p['conv_w'], p['conv_b'], p['dt_bias'],
                              p['a_log'], p['d_skip'], p['ssm_norm_w'])
    oh, nhg = _hgrn2(hq, hf, hi, lb, hg_prev)
    yh = (_rmsnorm(oh, p['hg_norm_w']).reshape(b, T, HG_WIDTH) * jax.nn.silu(hg.astype(jnp.float32))).astype(x.dtype)
    g = jax.nn.sigmoid(gates)
    g_att, g_ssm, g_hg = g[..., :D_MODEL], g[..., D_MODEL:2 * D_MODEL], g[..., 2 * D_MODEL:]
    merged = g_att * (ya @ p['w_br_att']) + g_ssm * (ym @ p['w_br_ssm']) + g_hg * (yh @ p['w_br_hg'])
    return merged @ p['w_out'], (nk, nv, nconv, nssm.astype(x.dtype), nhg.astype(x.dtype))


def _layer(x, prev, n_prev_valid, lb, p):
    x = _layernorm(ALPHA * x + 0.5 * _swiglu(x, p['ffn1_wg'], p['ffn1_wu'], p['ffn1_wd']), p['ln1_g'], p['ln1_b'])
    y, new_state = _mixer(x, prev, n_prev_valid, lb, p)
    x = _layernorm(ALPHA * x + y, p['ln2_g'], p['ln2_b'])
    x = _layernorm(ALPHA * x + 0.5 * _swiglu(x, p['ffn2_wg'], p['ffn2_wu'], p['ffn2_wd']), p['ln3_g'], p['ln3_b'])
    return x, new_state


def setup_inputs(seed: int = 0) -> dict:
    key = jax.random.key(seed)
    keys = iter(jax.random.split(key, 48))

    def nrm(shape, scale):
        return scale * jax.random.normal(next(keys), shape, jnp.float32)

    def gain(shape):
        return 1.0 + nrm(shape, 0.1)

    att_w = ATT_HEADS * HEAD_DIM
    inp = {}
    inp['x_prompt'] = nrm((BATCH, SEQ, D_MODEL), 1.0)
    inp['x_sample'] = nrm((DEC_BATCH, DEC_SEQ, D_MODEL), 1.0)
    inp['cache_swa_k'] = nrm((DEPTH, DEC_BATCH, WINDOW, ATT_KV_HEADS, HEAD_DIM), 1.0)
    inp['cache_swa_v'] = nrm((DEPTH, DEC_BATCH, WINDOW, ATT_KV_HEADS, HEAD_DIM), 1.0)
    inp['state_conv'] = nrm((DEPTH, DEC_BATCH, CONV_W - 1, CONV_DIM), 1.0)
    inp['state_ssm'] = nrm((DEPTH, DEC_BATCH, M_HEADS, M_HEADDIM, M_STATE), 0.5)
    inp['state_hgrn'] = nrm((DEPTH, DEC_BATCH, HG_HEADS, HG_DK, HG_DV), 0.5)
    inp['ln1_g'] = gain((DEPTH, D_MODEL))
    inp['ln1_b'] = nrm((DEPTH, D_MODEL), 0.02)
    inp['ffn1_wg'] = nrm((DEPTH, D_MODEL, D_FF), D_MODEL ** -0.5)
    inp['ffn1_wu'] = nrm((DEPTH, D_MODEL, D_FF), D_MODEL ** -0.5)
    inp['ffn1_wd'] = nrm((DEPTH, D_FF, D_MODEL), BETA * D_FF ** -0.5)
    inp['w_in'] = nrm((DEPTH, D_MODEL, N_IN), D_MODEL ** -0.5)
    inp['b_in'] = nrm((DEPTH, N_IN), 0.02)
    inp['att_sinks'] = nrm((DEPTH, ATT_HEADS), 0.5)
    inp['conv_w'] = nrm((DEPTH, CONV_W, CONV_DIM), CONV_W ** -0.5)
    inp['conv_b'] = nrm((DEPTH, CONV_DIM), 0.02)
    dt0 = jnp.exp(jax.random.uniform(next(keys), (DEPTH, M_HEADS), jnp.float32, math.log(1e-3), math.log(1e-1)))
    inp['dt_bias'] = dt0 + jnp.log(-jnp.expm1(-dt0))
    inp['a_log'] = jnp.log(jax.random.uniform(next(keys), (DEPTH, M_HEADS), jnp.float32, 1.0, 16.0))
    inp['d_skip'] = gain((DEPTH, M_HEADS))
    inp['ssm_norm_w'] = gain((DEPTH, M_INNER))
    inp['hg_lb_logits'] = nrm((DEPTH, HG_WIDTH), 1.0)
    inp['hg_norm_w'] = gain((DEPTH, HG_DV))
    inp['w_br_att'] = nrm((DEPTH, att_w, D_MODEL), BETA * att_w ** -0.5)
    inp['w_br_ssm'] = nrm((DEPTH, M_INNER, D_MODEL), BETA * M_INNER ** -0.5)
    inp['w_br_hg'] = nrm((DEPTH, HG_WIDTH, D_MODEL), BETA * HG_WIDTH ** -0.5)
    inp['w_out'] = nrm((DEPTH, D_MODEL, D_MODEL), BETA * D_MODEL ** -0.5)
    inp['ln2_g'] = gain((DEPTH, D_MODEL))
    inp['ln2_b'] = nrm((DEPTH, D_MODEL), 0.02)
    inp['ffn2_wg'] = nrm((DEPTH, D_MODEL, D_FF), D_MODEL ** -0.5)
    inp['ffn2_wu'] = nrm((DEPTH, D_MODEL, D_FF), D_MODEL ** -0.5)
    inp['ffn2_wd'] = nrm((DEPTH, D_FF, D_MODEL), BETA * D_FF ** -0.5)
    inp['ln3_g'] = gain((DEPTH, D_MODEL))
    inp['ln3_b'] = nrm((DEPTH, D_MODEL), 0.02)
    return inp


def reference(x_prompt, x_sample, cache_swa_k, cache_swa_v, state_conv, state_ssm, state_hgrn,
              ln1_g, ln1_b, ffn1_wg, ffn1_wu, ffn1_wd, w_in, b_in, att_sinks, conv_w, conv_b,
              dt_bias, a_log, d_skip, ssm_norm_w, hg_lb_logits, hg_norm_w, w_br_att, w_br_ssm,
              w_br_hg, w_out, ln2_g, ln2_b, ffn2_wg, ffn2_wu, ffn2_wd, ln3_g, ln3_b):
    lb_all = jnp.cumsum(jax.nn.softmax(hg_lb_logits.astype(jnp.float32), axis=0), axis=0)
    lb_all = lb_all - lb_all[0]
    bp, dtype = x_prompt.shape[0], x_prompt.dtype
    zero_prev = (jnp.zeros((bp, WINDOW, ATT_KV_HEADS, HEAD_DIM), dtype),
                 jnp.zeros((bp, WINDOW, ATT_KV_HEADS, HEAD_DIM), dtype),
                 jnp.zeros((bp, CONV_W - 1, CONV_DIM), dtype),
                 jnp.zeros((bp, M_HEADS, M_HEADDIM, M_STATE), dtype),
                 jnp.zeros((bp, HG_HEADS, HG_DK, HG_DV), dtype))
    n_past_valid = min(WINDOW, PAST_LEN)
    y_prompt, y_sample = x_prompt, x_sample
    p_states, s_states = [], []
    for l in range(DEPTH):
        p = {'ln1_g': ln1_g[l], 'ln1_b': ln1_b[l], 'ffn1_wg': ffn1_wg[l], 'ffn1_wu': ffn1_wu[l],
             'ffn1_wd': ffn1_wd[l], 'w_in': w_in[l], 'b_in': b_in[l], 'att_sinks': att_sinks[l],
             'conv_w': conv_w[l], 'conv_b': conv_b[l], 'dt_bias': dt_bias[l], 'a_log': a_log[l],
             'd_skip': d_skip[l], 'ssm_norm_w': ssm_norm_w[l], 'hg_norm_w': hg_norm_w[l],
             'w_br_att': w_br_att[l], 'w_br_ssm': w_br_ssm[l], 'w_br_hg': w_br_hg[l], 'w_out': w_out[l],
             'ln2_g': ln2_g[l], 'ln2_b': ln2_b[l], 'ffn2_wg': ffn2_wg[l], 'ffn2_wu': ffn2_wu[l],
             'ffn2_wd': ffn2_wd[l], 'ln3_g': ln3_g[l], 'ln3_b': ln3_b[l]}
        y_prompt, ps = _layer(y_prompt, zero_prev, 0, lb_all[l], p)
        s_prev = (cache_swa_k[l], cache_swa_v[l], state_conv[l], state_ssm[l], state_hgrn[l])
        y_sample, ss = _layer(y_sample, s_prev, n_past_valid, lb_all[l], p)
        p_states.append(ps)
        s_states.append(ss)
    p_swa_k, p_swa_v, p_conv, p_ssm, p_hgrn = [jnp.stack(t) for t in zip(*p_states)]
    s_swa_k, s_swa_v, s_conv, s_ssm, s_hgrn = [jnp.stack(t) for t in zip(*s_states)]
    return (y_prompt, y_sample, p_swa_k, p_swa_v, p_conv, p_ssm, p_hgrn, s_swa_k, s_swa_v, s_conv, s_ssm, s_hgrn)
```

```python
import numpy as np
import concourse.bass as bass
import concourse.mybir as mybir
from concourse.bass_utils import run_bass_kernel_spmd
from contextlib import ExitStack

F32 = mybir.dt.float32
BF16 = mybir.dt.bfloat16
AF = mybir.ActivationFunctionType
ALU = mybir.AluOpType
AX = mybir.AxisListType

ENGS = ['pe', 'act', 'dve', 'pool', 'sp']

DFF = 2816
NCORES = 8
NS = 16
ALPHA = 8.0 ** 0.25
LN_EPS = 1e-5
RMS_EPS = 1e-6
NEG = -1.0e30


class Res:
    __slots__ = ('name', 'w', 'r', 'const')

    def __init__(self, name, const=False):
        self.name = name
        self.w = None
        self.r = {}
        self.const = const


class Tracker:
    def __init__(self):
        self.streams = {e: [] for e in ENGS}
        self.seen = {e: {} for e in ENGS}
        self.dcount = {}
        self.gates = []
        self.gated = {e: 0 for e in ENGS}

    def emit(self, eng, fn, reads=(), writes=(), dma=None):
        st = self.streams[eng]
        idx = len(st)
        deps = []
        reads = list(reads)
        if self.gated[eng] < len(self.gates):
            reads = reads + self.gates[self.gated[eng]:]
            self.gated[eng] = len(self.gates)
        for r in reads:
            if r.const:
                continue
            if r.w is not None:
                deps.append((0, r.w))
        for w in writes:
            if w.w is not None:
                deps.append((1, w.w))
            for t in w.r.values():
                deps.append((2, t))
        waits = []
        seen = self.seen[eng]
        for kind, t in deps:
            if t[0] == 'E':
                _, e2, i2 = t
                if e2 == eng and dma is None:
                    if eng == 'pe' or idx - i2 > 2:
                        continue
                key = ('E', e2)
                if seen.get(key, -1) >= i2:
                    continue
                seen[key] = i2
                waits.append(t)
                self.streams[e2][i2][2] = True
            else:
                _, name, val = t
                key = ('D', name)
                if seen.get(key, -1) >= val:
                    continue
                cur = self.dcount[name]
                seen[key] = cur
                waits.append(('D', name, cur))
        if dma is not None:
            dma = dma + '_' + eng
            self.dcount[dma] = self.dcount.get(dma, 0) + 16
            tok = ('D', dma, self.dcount[dma])
        else:
            tok = ('E', eng, idx)
        st.append([fn, waits, False, dma, 0])
        k = tok[1] if tok[0] == 'E' else ('D', tok[1])
        for r in reads:
            if not r.const:
                r.r[k] = tok
        for w in writes:
            w.w = tok
            w.r = {}
        return tok

    def replay(self, nc, es, final_wait_eng='sp'):
        engsem = {e: es.enter_context(nc.semaphore('s_' + e)) for e in ENGS}
        dsem = {n: es.enter_context(nc.semaphore('d_' + n)) for n in self.dcount}
        for e in ENGS:
            c = 0
            for op in self.streams[e]:
                if op[2]:
                    c += 1
                    op[4] = c
        streams = self.streams
        dcount = self.dcount
        block = es.enter_context(nc.Block())

        def run(ename, h):
            for op in streams[ename]:
                fn, waits, flag, dma, val = op
                for t in waits:
                    if t[0] == 'E':
                        h.wait_ge(engsem[t[1]], streams[t[1]][t[2]][4])
                    else:
                        h.wait_ge(dsem[t[1]], t[2])
                ins = fn(h)
                if dma is not None:
                    ins.then_inc(dsem[dma], 16)
                elif flag:
                    ins.then_inc(engsem[ename], 1)
            if ename == final_wait_eng:
                for n, c in dcount.items():
                    h.wait_ge(dsem[n], c)

        @block.tensor
        def _(h):
            run('pe', h)

        @block.scalar
        def _(h):
            run('act', h)

        @block.vector
        def _(h):
            run('dve', h)

        @block.gpsimd
        def _(h):
            run('pool', h)

        @block.sync
        def _(h):
            run('sp', h)


class Arena:
    def __init__(self, nc, es, name, nrows, width, dtype, const=False):
        self.t = es.enter_context(nc.sbuf_tensor('sb_' + name, [128, nrows, width], dtype))
        self.res = [Res(f'{name}{i}', const) for i in range(nrows)]
        self.nrows = nrows
        self.width = width

    def rows(self, a, b):
        return self.res[a:b]


def bc(ap, shape):
    return ap.to_broadcast(list(shape))


WBLK = 4096
FF1, FF2, WIN0, DTREP, BR0, WOUT0 = 0, 18, 36, 51, 52, 55
NB_LAYER = 57
CQ, CK, CV, CZ, CX, CDT, CHQ, CHF, CHI, CHG, CG = 0, 512, 640, 768, 1280, 2048, 2056, 2568, 3080, 3592, 4104


def _blk_cols(w):
    C = w.shape[1]
    nb = C // 512
    v = w.reshape(8, 128, nb, 512)
    return np.ascontiguousarray(v.transpose(2, 1, 0, 3)).reshape(nb, 128, WBLK)


def _blk_wd(wd):
    w = np.zeros((24 * 128, 1024), np.float32)
    w[:DFF] = wd
    v = w.reshape(3, 8, 128, 2, 512)
    return np.ascontiguousarray(v.transpose(3, 0, 2, 1, 4)).reshape(6, 128, WBLK)


def _win_cols(w):
    c = np.zeros((1024, 15 * 512), np.float32)
    c[:, 0:512] = w[:, CQ:CQ + 512]
    c[:, 512:640] = w[:, CK:CK + 128]
    c[:, 640:768] = w[:, CV:CV + 128]
    c[:, 768:776] = w[:, CDT:CDT + 8]
    c[:, 1024:1536] = w[:, CZ:CZ + 512]
    c[:, 1536:2048] = w[:, CX:CX + 512]
    c[:, 2048:2304] = w[:, CX + 512:CX + 768]
    c[:, 2560:3072] = w[:, CHQ:CHQ + 512]
    c[:, 3072:3584] = w[:, CHF:CHF + 512]
    c[:, 3584:4096] = w[:, CHI:CHI + 512]
    c[:, 4096:4608] = w[:, CHG:CHG + 512]
    for ci in range(24):
        j, br = divmod(ci, 3)
        c[:, 4608 + ci * 128: 4608 + ci * 128 + 128] = w[:, CG + br * 1024 + j * 128: CG + br * 1024 + j * 128 + 128]
    return c


def layout_layer_weights(inp, l):
    blks = []
    for names in (('ffn1_wg', 'ffn1_wu', 'ffn1_wd'), ('ffn2_wg', 'ffn2_wu', 'ffn2_wd')):
        for nm in names[:2]:
            w = np.zeros((1024, 3072), np.float32)
            w[:, :DFF] = inp[nm][l]
            blks.append(_blk_cols(w))
        blks.append(_blk_wd(np.asarray(inp[names[2]][l], np.float32)))
    win = np.asarray(inp['w_in'][l], np.float32)
    blks.append(_blk_cols(_win_cols(win)))
    blks.append(_blk_cols(np.ascontiguousarray(np.repeat(win[:, CDT:CDT + 8], 64, axis=1))))
    for nm in ('w_br_att', 'w_br_ssm', 'w_br_hg'):
        W = np.asarray(inp[nm][l], np.float32)
        blks.append(np.ascontiguousarray(W.reshape(4, 128, 1024).transpose(1, 0, 2)).reshape(1, 128, WBLK))
    W = np.asarray(inp['w_out'][l], np.float32)
    blks.append(np.ascontiguousarray(W.reshape(2, 4, 128, 1024).transpose(0, 2, 1, 3)).reshape(2, 128, WBLK))
    out = np.concatenate(blks, axis=0)
    assert out.shape[0] == NB_LAYER
    return out


PFIELDS = [('ln1_g', 8), ('ln1_b', 8), ('ln2_g', 8), ('ln2_b', 8), ('ln3_g', 8), ('ln3_b', 8),
           ('bq', 8), ('bk', 2), ('bz', 4), ('bxbc', 6), ('bdt', 1), ('bhq', 4), ('bhf', 4), ('bhg', 4), ('bgate', 24),
           ('convw', 24), ('convb', 6), ('dtb', 1), ('alog', 1), ('dskip', 4), ('ssmw', 4), ('hgw', 1),
           ('sink', 8), ('bdtrep', 4), ('dtbrep', 4), ('alogrep', 4),
           ('A', 1), ('esink', 8), ('lb', 4), ('omlb', 4), ('Arep', 4), ('dtbt', 1), ('dtbtrep', 4)]
POFF = {}
_p = 0
for _n, _w in PFIELDS:
    POFF[_n] = _p
    _p += _w
PW = _p


def layout_params(inp, depth):
    def fm(v):
        return np.ascontiguousarray(np.asarray(v, np.float32).reshape(-1, 128).T)

    def rep64(v):
        v = np.asarray(v, np.float32)
        return np.ascontiguousarray(np.repeat(v.reshape(4, 2), 64, axis=1).T)
    P = np.zeros((128, depth, PW), np.float32)
    for l in range(depth):
        b = np.asarray(inp['b_in'][l], np.float32)

        def put(name, arr):
            P[:arr.shape[0], l, POFF[name]:POFF[name] + arr.shape[1]] = arr
        for nm in ('ln1_g', 'ln1_b', 'ln2_g', 'ln2_b', 'ln3_g', 'ln3_b'):
            put(nm, fm(inp[nm][l]))
        put('bq', b[CQ:CQ + 512].reshape(8, 64).T)
        put('bk', b[CK:CK + 128].reshape(2, 64).T)
        put('bz', fm(b[CZ:CZ + 512]))
        put('bxbc', fm(b[CX:CX + 768]))
        put('bdt', b[CDT:CDT + 8].reshape(8, 1))
        put('bhq', fm(b[CHQ:CHQ + 512]))
        put('bhf', fm(b[CHF:CHF + 512]))
        put('bhg', fm(b[CHG:CHG + 512]))
        bg = b[CG:CG + 3072]
        put('bgate', np.stack([bg[(ci % 3) * 1024 + (ci // 3) * 128: (ci % 3) * 1024 + (ci // 3) * 128 + 128] for ci in range(24)], axis=1))
        cw = np.asarray(inp['conv_w'][l], np.float32)
        put('convw', np.ascontiguousarray(cw.reshape(4, 6, 128).transpose(2, 1, 0)).reshape(128, 24))
        put('convb', fm(inp['conv_b'][l]))
        put('dtb', np.asarray(inp['dt_bias'][l], np.float32).reshape(8, 1))
        put('alog', np.asarray(inp['a_log'][l], np.float32).reshape(8, 1))
        put('dskip', rep64(inp['d_skip'][l]))
        put('ssmw', fm(inp['ssm_norm_w'][l]))
        put('hgw', np.asarray(inp['hg_norm_w'][l], np.float32).reshape(128, 1))
        put('sink', np.broadcast_to(np.asarray(inp['att_sinks'][l], np.float32)[None, :], (128, 8)))
        put('bdtrep', rep64(b[CDT:CDT + 8]))
        put('dtbrep', rep64(inp['dt_bias'][l]))
        put('alogrep', rep64(inp['a_log'][l]))
    lg = np.asarray(inp['hg_lb_logits'], np.float32)[:depth]
    lblog = np.ascontiguousarray(lg.reshape(depth, 4, 128).transpose(2, 1, 0))
    brow = np.zeros((depth, 128, 640), np.float32)
    for l in range(depth):
        b = np.asarray(inp['b_in'][l], np.float32)
        brow[l, :, 0:128] = np.broadcast_to(b[CV:CV + 128][None, :], (128, 128))
        brow[l, :, 128:640] = np.broadcast_to(b[CHI:CHI + 512][None, :], (128, 512))
    return P.reshape(128, depth * PW), lblog.reshape(128, 4 * depth), brow


CF_ID, CF_ONE, CF_TRI, CF_ONE512 = 0, 128, 256, 384
CB_ID, CB_ONE, CB_BIAS, CB_HGM = 0, 128, 256, 2304
CFW, CBW = 896, 2368


def layout_consts():
    cf = np.zeros((128, CFW), np.float32)
    cf[:, 0:128] = np.eye(128, dtype=np.float32)
    cf[:, 128:256] = 1.0
    s = np.arange(128)[:, None]
    t = np.arange(128)[None, :]
    cf[:, 256:384] = np.where(s <= t, 0.0, NEG)
    cf[:, 384:896] = 1.0
    cb = np.zeros((128, CBW), np.float32)
    cb[:, 0:128] = np.eye(128, dtype=np.float32)
    cb[:, 128:256] = 1.0
    bias = np.zeros((128, 2, 8, 128), np.float32)
    for kind in range(2):
        dist = (128 if kind == 0 else 0) + t - s
        valid = (dist >= 0) & (dist <= 128)
        for h in range(8):
            slope = 2.0 ** (-(h + 1))
            bias[:, kind, h, :] = np.where(valid, -slope * dist, NEG)
    cb[:, 256:256 + 2048] = bias.reshape(128, 2048)
    s64 = (np.arange(128) % 64)[:, None]
    t64 = np.arange(64)[None, :]
    cb[:, 2304:2368] = (s64 <= t64).astype(np.float32)
    return cf, cb


class Builder:
    def __init__(self, cfg):
        self.cfg = cfg
        self.nt = cfg['n_tiles']
        self.depth = cfg['depth']
        self.sample = cfg.get('sample', True)
        self.T = self.nt * 512
        self.nc = bass.Bass('TRN2', target_bir_lowering=False)
        self.es = ExitStack()
        self.tr = Tracker()
        self.bank_i = 0
        self.held = [False] * 8
        self.wslot_i = 0
        self.rot = {}

    def din(self, name, shape):
        return self.nc.dram_tensor(name, list(shape), F32, kind='ExternalInput').ap()

    def dout(self, name, shape):
        return self.nc.dram_tensor(name, list(shape), F32, kind='ExternalOutput').ap()

    def bank(self, hold=False):
        for _ in range(9):
            b = self.bank_i
            self.bank_i = (b + 1) % 8
            if not self.held[b]:
                break
        else:
            raise RuntimeError('no psum bank')
        if hold:
            self.held[b] = True
        return b

    def release(self, b):
        self.held[b] = False

    def rotate(self, key, n):
        v = self.rot.get(key, 0)
        self.rot[key] = (v + 1) % n
        return v

    def E(self, eng, fn, r=(), w=()):
        self.tr.emit(eng, fn, reads=r, writes=w)

    def act(self, out, in_, func, r, w, bias=None, scale=None):
        kw = {}
        if bias is not None:
            kw['bias'] = bias
        if scale is not None:
            kw['scale'] = scale
        self.tr.emit('act', lambda h: h.activation(out=out, in_=in_, func=func, **kw), reads=r, writes=w)

    def tt(self, out, in0, in1, op, r, w):
        self.tr.emit('dve', lambda h: h.tensor_tensor(out=out, in0=in0, in1=in1, op=op), reads=r, writes=w)

    def ts(self, out, in0, s1, s2, op0, op1, r, w):
        if s2 is None:
            self.tr.emit('dve', lambda h: h.tensor_scalar(out=out, in0=in0, scalar1=s1, scalar2=None, op0=op0), reads=r, writes=w)
        else:
            self.tr.emit('dve', lambda h: h.tensor_scalar(out=out, in0=in0, scalar1=s1, scalar2=s2, op0=op0, op1=op1), reads=r, writes=w)

    def stt(self, out, in0, scalar, in1, op0, op1, r, w):
        self.tr.emit('dve', lambda h: h.scalar_tensor_tensor(out=out, in0=in0, scalar=scalar, in1=in1, op0=op0, op1=op1), reads=r, writes=w)

    def cpv(self, out, in_, r, w):
        self.tr.emit('dve', lambda h: h.tensor_copy(out=out, in_=in_), reads=r, writes=w)

    def cpa(self, out, in_, r, w):
        self.tr.emit('act', lambda h: h.activation(out=out, in_=in_, func=AF.Copy), reads=r, writes=w)

    def mm(self, out, pairs, r, bk):
        pairs = list(pairs)

        def fn(h):
            n = len(pairs)
            ins = None
            for i, (a, b) in enumerate(pairs):
                ins = h.matmul(out, lhsT=a, rhs=b, start=(i == 0), stop=(i == n - 1))
            return ins
        self.tr.emit('pe', fn, reads=r, writes=[self.psr[bk]])

    def dma(self, q, out, in_, r, w, sem):
        self.tr.emit(q, lambda h: h.dma_start(out=out, in_=in_), reads=r, writes=w, dma=sem)

    def par(self, l, name, col=0, rows=128, n=1):
        o = l * PW + POFF[name] + col
        return self.pt.t[0:rows, 0, o:o + n]

    def _wfetch(self, l, blk, dst, dres, sem):
        key = (l, blk)
        if key in self.cvt:
            src = self.wbf[l, blk]
            self.tr.emit('sp', lambda h: h.dma_start(out=dst, in_=src), reads=[self.cvt[key]], writes=[dres], dma=sem)
        else:
            src = self.wts[l, blk]
            self.tr.emit('pool', lambda h: h.dma_start(out=dst, in_=src, max_dma_last_dim=8192), reads=(), writes=[dres], dma=sem)
            r = Res(f'wbf{l}_{blk}')
            out = self.wbf[l, blk]
            self.tr.emit('sp', lambda h: h.dma_start(out=out, in_=dst), reads=[dres], writes=[r], dma='wst')
            self.cvt[key] = r

    def wload(self, l, blk, dst_fn=None, src=None):
        s = self.wslot_i
        self.wslot_i = (s + 1) % self.nslots
        if src is None:
            self._wfetch(l, blk, self.wring.t[:, s, :], self.wring.res[s], f'w{s}')
            return s
        dst = dst_fn(self.wring.t, s)
        self.tr.emit('pool', lambda h: h.dma_start(out=dst, in_=src, max_dma_last_dim=8192), reads=(), writes=[self.wring.res[s]], dma=f'w{s}')
        return s

    def wpin_load(self, l, blk, i):
        self._wfetch(l, blk, self.wpin.t[:, i, :], self.wpin.res[i], f'p{i}')

    def proj(self, s, col, M, N):
        bk = self.bank()
        wr, xb = self.wring, self.xb
        out = self.ps[bk][0:M, 0:N]
        for kc in range(8):
            def fn(h, kc=kc):
                return h.matmul(out, lhsT=wr.t[:, s, kc * 512 + col: kc * 512 + col + M], rhs=xb.t[:, kc, 0:N], start=(kc == 0), stop=(kc == 7))
            self.tr.emit('pe', fn, reads=[wr.res[s]] + xb.rows(kc, kc + 1), writes=[self.psr[bk]])
        return bk

    def ffn(self, l, base, N):
        fa, hb, wr, PS = self.fa, self.ha, self.wring, self.ps
        for jb in range(6):
            sg = self.wload(l, base + jb)
            su = self.wload(l, base + 6 + jb)
            for jj in range(4):
                j = jb * 4 + jj
                if j >= 22:
                    break
                bg = self.proj(sg, jj * 128, 128, N)
                bu = self.proj(su, jj * 128, 128, N)
                trow = 8 + self.rotate('ffn', 4)
                self.act(fa.t[:, trow, 0:N], PS[bg][:, 0:N], AF.Silu, [self.psr[bg]], fa.rows(trow, trow + 1))
                self.tt(hb.t[:, j, 0:N], fa.t[:, trow, 0:N], PS[bu][:, 0:N], ALU.mult,
                        [self.psr[bu]] + fa.rows(trow, trow + 1), hb.rows(j, j + 1))
        self.ln_begin()
        for half in range(2):
            banks = [self.bank(hold=True) for _ in range(4)]
            for p in range(3):
                s = self.wload(l, base + 12 + half * 3 + p)
                njj = 8 if p < 2 else 6
                for dc4 in range(4):
                    def fn(h, p=p, s=s, dc4=dc4, njj=njj, bk=banks[dc4]):
                        ins = None
                        for jj in range(njj):
                            j = p * 8 + jj
                            ins = h.matmul(PS[bk][:, 0:N], lhsT=wr.t[:, s, jj * 512 + dc4 * 128: jj * 512 + dc4 * 128 + 128],
                                           rhs=hb.t[:, j, 0:N], start=(j == 0), stop=(j == 21))
                        return ins
                    self.tr.emit('pe', fn, reads=[wr.res[s]] + hb.rows(p * 8, p * 8 + njj), writes=[self.psr[banks[dc4]]])
            for dc4 in range(4):
                bk = banks[dc4]
                dc = half * 4 + dc4
                self.stt(fa.t[:, dc, 0:N], self.xres.t[:, dc, 0:N], 2.0 * ALPHA, PS[bk][:, 0:N], ALU.mult, ALU.add,
                         [self.psr[bk]] + self.xres.rows(dc, dc + 1), fa.rows(dc, dc + 1))
                self.release(bk)
                self.ln_feed(dc, N)

    def ln_begin(self):
        self.ln_banks = (self.bank(hold=True), self.bank(hold=True))

    def ln_feed(self, c, N):
        fa, ha, PS = self.fa, self.ha, self.ps
        onesb = self.cb.t[:, 0, CB_ONE:CB_ONE + 128]
        bm, bq = self.ln_banks
        zr = 22 + self.rotate('lnz', 4)
        qr = 26 + self.rotate('lnq', 4)
        self.cpa(ha.t[:, zr, 0:N], fa.t[:, c, 0:N], fa.rows(c, c + 1), ha.rows(zr, zr + 1))
        self.act(ha.t[:, qr, 0:N], fa.t[:, c, 0:N], AF.Square, fa.rows(c, c + 1), ha.rows(qr, qr + 1))

        def fm(h):
            return h.matmul(PS[bm][:, 0:N], lhsT=onesb, rhs=ha.t[:, zr, 0:N], start=(c == 0), stop=(c == 7))

        def fq(h):
            return h.matmul(PS[bq][:, 0:N], lhsT=onesb, rhs=ha.t[:, qr, 0:N], start=(c == 0), stop=(c == 7))
        self.tr.emit('pe', fm, reads=ha.rows(zr, zr + 1), writes=[self.psr[bm]])
        self.tr.emit('pe', fq, reads=ha.rows(qr, qr + 1), writes=[self.psr[bq]])

    def layernorm(self, l, gname, bname, N, zscale):
        fa, PS = self.fa, self.ps
        bm, bq = self.ln_banks
        s2 = zscale * zscale
        self.ts(fa.t[:, 20, 0:N], PS[bm][:, 0:N], 1.0 / 1024.0, None, ALU.mult, None, [self.psr[bm]], fa.rows(20, 21))
        self.tt(fa.t[:, 21, 0:N], fa.t[:, 20, 0:N], fa.t[:, 20, 0:N], ALU.mult, fa.rows(20, 21), fa.rows(21, 22))
        self.stt(fa.t[:, 21, 0:N], PS[bq][:, 0:N], 1.0 / 1024.0, fa.t[:, 21, 0:N], ALU.mult, ALU.subtract,
                 [self.psr[bq]] + fa.rows(21, 22), fa.rows(21, 22))
        self.release(bm)
        self.release(bq)
        self.ts(fa.t[:, 21, 0:N], fa.t[:, 21, 0:N], LN_EPS / s2, None, ALU.add, None, fa.rows(21, 22), fa.rows(21, 22))
        self.act(fa.t[:, 21, 0:N], fa.t[:, 21, 0:N], AF.Ln, fa.rows(21, 22), fa.rows(21, 22))
        self.act(fa.t[:, 21, 0:N], fa.t[:, 21, 0:N], AF.Exp, fa.rows(21, 22), fa.rows(21, 22), scale=-0.5)
        self.tt(fa.t[:, 20, 0:N], fa.t[:, 20, 0:N], fa.t[:, 21, 0:N], ALU.mult, fa.rows(20, 22), fa.rows(20, 21))
        for c in range(8):
            self.tt(fa.t[:, c, 0:N], fa.t[:, c, 0:N], fa.t[:, 21, 0:N], ALU.mult, fa.rows(c, c + 1) + fa.rows(21, 22), fa.rows(c, c + 1))
            self.tt(fa.t[:, c, 0:N], fa.t[:, c, 0:N], fa.t[:, 20, 0:N], ALU.subtract, fa.rows(c, c + 1) + fa.rows(20, 21), fa.rows(c, c + 1))
            self.act(self.xb.t[:, c, 0:N], fa.t[:, c, 0:N], AF.Identity, fa.rows(c, c + 1), self.xb.rows(c, c + 1),
                     bias=self.par(l, bname, c), scale=self.par(l, gname, c))
            self.act(self.xres.t[:, c, 0:N], fa.t[:, c, 0:N], AF.Identity, fa.rows(c, c + 1), self.xres.rows(c, c + 1),
                     bias=self.par(l, bname, c), scale=self.par(l, gname, c))

    def rstd_from_sum(self, out, out_res, src, src_res, n, eps):
        self.ts(out, src, 1.0 / n, eps, ALU.mult, ALU.add, src_res, out_res)
        self.act(out, out, AF.Ln, out_res, out_res)
        self.act(out, out, AF.Exp, out_res, out_res, scale=-0.5)

    def merge(self, l, N):
        fa, ha, PS, wr, wp = self.fa, self.ha, self.ps, self.wring, self.wpin
        YA0, YM0, YH0, MG0 = 12, 16, 20, 0
        for i in range(3):
            self.wpin_load(l, BR0 + i, i)
        ysrc = (YA0, YM0, YH0)
        gslot = {}
        for j in range(8):
            gb = []
            for br in range(3):
                ci = 3 * j + br
                blk = ci // 4
                if blk not in gslot:
                    gslot[blk] = self.wload(l, WIN0 + 9 + blk)
                gb.append(self.proj(gslot[blk], (ci % 4) * 128, 128, N))
            pb = []
            for br in range(3):
                bk = self.bank()
                self.mm(PS[bk][:, 0:N],
                        [(wp.t[:, br, kc * 1024 + j * 128: kc * 1024 + j * 128 + 128], ha.t[:, ysrc[br] + kc, 0:N]) for kc in range(4)],
                        [wp.res[br]] + ha.rows(ysrc[br], ysrc[br] + 4), bk)
                pb.append(bk)
            pp = self.rotate('mg', 2)
            sg = 8 + 3 * pp
            for br in range(3):
                self.act(fa.t[:, sg + br, 0:N], PS[gb[br]][:, 0:N], AF.Sigmoid, [self.psr[gb[br]]], fa.rows(sg + br, sg + br + 1),
                         bias=self.par(l, 'bgate', 3 * j + br))
            t0, t1 = 14 + 2 * pp, 15 + 2 * pp
            self.tt(fa.t[:, t0, 0:N], fa.t[:, sg, 0:N], PS[pb[0]][:, 0:N], ALU.mult, [self.psr[pb[0]]] + fa.rows(sg, sg + 1), fa.rows(t0, t0 + 1))
            self.tt(fa.t[:, t1, 0:N], fa.t[:, sg + 1, 0:N], PS[pb[1]][:, 0:N], ALU.mult, [self.psr[pb[1]]] + fa.rows(sg + 1, sg + 2), fa.rows(t1, t1 + 1))
            self.tt(fa.t[:, t0, 0:N], fa.t[:, t0, 0:N], fa.t[:, t1, 0:N], ALU.add, fa.rows(t0, t0 + 1) + fa.rows(t1, t1 + 1), fa.rows(t0, t0 + 1))
            self.tt(fa.t[:, t1, 0:N], fa.t[:, sg + 2, 0:N], PS[pb[2]][:, 0:N], ALU.mult, [self.psr[pb[2]]] + fa.rows(sg + 2, sg + 3), fa.rows(t1, t1 + 1))
            self.tt(ha.t[:, MG0 + j, 0:N], fa.t[:, t0, 0:N], fa.t[:, t1, 0:N], ALU.add, fa.rows(t0, t0 + 1) + fa.rows(t1, t1 + 1), ha.rows(MG0 + j, MG0 + j + 1))
        so = [self.wload(l, WOUT0), self.wload(l, WOUT0 + 1)]
        self.ln_begin()
        for dc in range(8):
            bk = self.bank()
            self.mm(PS[bk][:, 0:N],
                    [(wr.t[:, so[kc // 4], (kc % 4) * 1024 + dc * 128:(kc % 4) * 1024 + dc * 128 + 128], ha.t[:, MG0 + kc, 0:N]) for kc in range(8)],
                    [wr.res[so[0]], wr.res[so[1]]] + ha.rows(MG0, MG0 + 8), bk)
            self.stt(fa.t[:, dc, 0:N], self.xres.t[:, dc, 0:N], ALPHA, PS[bk][:, 0:N], ALU.mult, ALU.add,
                     [self.psr[bk]] + self.xres.rows(dc, dc + 1), fa.rows(dc, dc + 1))
            self.ln_feed(dc, N)

    def ssd_post(self, l, N, ysrc, ysrc_res, xs_ap, xs_res, zs_ap, zs_res, yb_ap, yb_res):
        fa, PS = self.fa, self.ps
        onesb = self.cb.t[:, 0, CB_ONE:CB_ONE + 128]
        bM = self.bank(hold=True)
        for i in range(4):
            self.stt(xs_ap(i), xs_ap(i), self.par(l, 'dskip', i), ysrc(i), ALU.mult, ALU.add, ysrc_res(i) + xs_res(i), xs_res(i))
            self.tt(xs_ap(i), xs_ap(i), zs_ap(i), ALU.mult, xs_res(i) + zs_res(i), xs_res(i))
            sq = 30 + self.rotate('sq', 2)
            self.act(self.ha.t[:, sq, 0:N], xs_ap(i), AF.Square, xs_res(i), self.ha.rows(sq, sq + 1))

            def fn(h, i=i, sq=sq):
                return h.matmul(PS[bM][:, 0:N], lhsT=onesb, rhs=self.ha.t[:, sq, 0:N], start=(i == 0), stop=(i == 3))
            self.tr.emit('pe', fn, reads=self.ha.rows(sq, sq + 1), writes=[self.psr[bM]])
        self.rstd_from_sum(fa.t[:, 22, 0:N], fa.rows(22, 23), PS[bM][:, 0:N], [self.psr[bM]], 512.0, RMS_EPS)
        self.release(bM)
        for i in range(4):
            self.stt(yb_ap(i), xs_ap(i), self.par(l, 'ssmw', i), fa.t[:, 22, 0:N], ALU.mult, ALU.mult,
                     xs_res(i) + fa.rows(22, 23), yb_res(i))

    def hgrn_post(self, l, N, h, osrc, osrc_res, hgs_ap, hgs_res, yh_ap, yh_res):
        fa, PS = self.fa, self.ps
        onesb = self.cb.t[:, 0, CB_ONE:CB_ONE + 128]
        sq = 30 + self.rotate('sq', 2)
        self.act(self.ha.t[:, sq, 0:N], osrc, AF.Square, osrc_res, self.ha.rows(sq, sq + 1))
        bM = self.bank()
        self.mm(PS[bM][:, 0:N], [(onesb, self.ha.t[:, sq, 0:N])], self.ha.rows(sq, sq + 1), bM)
        self.rstd_from_sum(fa.t[:, 22, 0:N], fa.rows(22, 23), PS[bM][:, 0:N], [self.psr[bM]], 128.0, RMS_EPS)
        self.tt(fa.t[:, 23, 0:N], osrc, fa.t[:, 22, 0:N], ALU.mult, osrc_res + fa.rows(22, 23), fa.rows(23, 24))
        self.stt(yh_ap, fa.t[:, 23, 0:N], self.par(l, 'hgw', 0), hgs_ap, ALU.mult, ALU.mult, fa.rows(23, 24) + hgs_res, yh_res)

    def mixer_prompt(self, l, ti):
        N = 512
        fa, ha, PS, wr, xb = self.fa, self.ha, self.ps, self.wring, self.xb
        cf, cb = self.cf, self.cb
        last = (ti == self.nt - 1)
        identf = cf.t[:, 0, CF_ID:CF_ID + 128]
        onesf = cf.t[:, 0, CF_ONE:CF_ONE + 128]
        identb = cb.t[:, 0, CB_ID:CB_ID + 128]
        onesb = cb.t[:, 0, CB_ONE:CB_ONE + 128]
        QT0, PT0, YA0, YM0, YH0 = 0, 8, 12, 16, 20
        kT, vtok, kcar, vcar = self.kT, self.vtok, self.kcar, self.vcar
        brow = self.brow

        self.dma('pool', brow.t[:, 0, :], self.browd[l], [], brow.rows(0, 1), 'brow')

        s0 = self.wload(l, WIN0 + 0)
        for h in range(8):
            bk = self.proj(s0, h * 64, 64, N)
            self.act(ha.t[0:64, QT0 + h, 0:N], PS[bk][0:64, 0:N], AF.Identity, [self.psr[bk]], ha.rows(QT0 + h, QT0 + h + 1),
                     bias=self.par(l, 'bq', h, rows=64))
        s1 = self.wload(l, WIN0 + 1)
        for g in range(2):
            bk = self.proj(s1, g * 64, 64, N)
            self.act(kT.t[0:64, g, 0:N], PS[bk][0:64, 0:N], AF.Identity, [self.psr[bk]], kT.rows(g, g + 1),
                     bias=self.par(l, 'bk', g, rows=64))
            if last:
                self.act(self.kfin.t[0:64, g, :], PS[bk][0:64, 384:512], AF.Identity, [self.psr[bk]], self.kfin.rows(g, g + 1),
                         bias=self.par(l, 'bk', g, rows=64))
        if last:
            self.dma('pool', self.p_k[l], self.kfin.t[0:64, :, :], self.kfin.rows(0, 2), [], 'out')
        bv = self.bank()
        for blk in range(4):
            self.mm(PS[bv][:, blk * 128: blk * 128 + 128],
                    [(xb.t[:, kc, blk * 128: blk * 128 + 128], wr.t[:, s1, kc * 512 + 128: kc * 512 + 256]) for kc in range(8)],
                    [wr.res[s1]] + xb.rows(0, 8), bv)
        self.tt(vtok.t[:, 0, :].rearrange('p (b d) -> p b d', b=4), PS[bv][:, :].rearrange('p (b d) -> p b d', b=4),
                bc(brow.t[:, 0, 0:128].unsqueeze(1), [128, 4, 128]), ALU.add, [self.psr[bv]] + brow.rows(0, 1), vtok.rows(0, 1))
        if last:
            self.tt(self.vfin.t[:, 0, :], PS[bv][:, 384:512], brow.t[:, 0, 0:128], ALU.add, [self.psr[bv]] + brow.rows(0, 1), self.vfin.rows(0, 1))
            self.dma('pool', self.p_v[l], self.vfin.t[:, 0, :], self.vfin.rows(0, 1), [], 'out')
        bdt = self.bank(hold=True)
        self.mm(PS[bdt][0:8, 0:N], [(wr.t[:, s1, kc * 512 + 256: kc * 512 + 264], xb.t[:, kc, 0:N]) for kc in range(8)],
                [wr.res[s1]] + xb.rows(0, 8), bdt)

        stop = self.cfg.get('stop', 9)
        if stop <= 1:
            self.release(bdt)
            return
        BIAS = cb.t[:, 0, CB_BIAS:CB_BIAS + 2048]
        def att_s(qb, g):
            kinds = ([0] if (qb > 0 or ti > 0) else []) + [1]
            pp = self.rotate('att', 2)
            ptrow = {}
            for kind in kinds:
                if kind == 0 and qb == 0:
                    klhs = kcar.t[0:64, l * 2 + g, :]
                    kres = kcar.rows(l * 2 + g, l * 2 + g + 1)
                else:
                    kb = qb - 1 + kind
                    klhs = kT.t[0:64, g, kb * 128: kb * 128 + 128]
                    kres = kT.rows(g, g + 1)
                bS = self.bank()
                self.mm(PS[bS][:, :], [(klhs, ha.t[0:64, QT0 + 4 * g: QT0 + 4 * g + 4, qb * 128: qb * 128 + 128])],
                        kres + ha.rows(QT0 + 4 * g, QT0 + 4 * g + 4), bS)
                sb = 2 * pp + kind
                self.stt(fa.t[:, sb, :], PS[bS][:, :], 0.125, BIAS[:, (kind * 8 + 4 * g) * 128:(kind * 8 + 4 * g + 4) * 128],
                         ALU.mult, ALU.add, [self.psr[bS]], fa.rows(sb, sb + 1))
                pr = PT0 + 2 * pp + kind
                self.act(ha.t[:, pr, :], fa.t[:, sb, :], AF.Exp, fa.rows(sb, sb + 1), ha.rows(pr, pr + 1))
                ptrow[kind] = pr
            return kinds, pp, ptrow

        def att_pv(qb, g, ctx):
            kinds, pp, ptrow = ctx
            bO = self.bank()
            bD = self.bank()
            for p2 in range(2):
                pairs = []
                rr = []
                for kind in kinds:
                    if kind == 0 and qb == 0:
                        vl = vcar.t[:, l, g * 64: g * 64 + 64]
                        rr += vcar.rows(l, l + 1)
                    else:
                        vb = qb - 1 + kind
                        vl = vtok.t[:, 0, vb * 128 + g * 64: vb * 128 + g * 64 + 64]
                        rr += vtok.rows(0, 1)
                    pt = ha.t[:, ptrow[kind], :].rearrange('p (h t) -> p h t', h=4)[:, p2::2, :]
                    pairs.append((vl, pt))
                    rr += ha.rows(ptrow[kind], ptrow[kind] + 1)
                self.mm(PS[bO][p2 * 64: p2 * 64 + 64, 0:256], pairs, rr, bO)
            self.mm(PS[bD][:, :], [(onesb, ha.t[:, ptrow[kind], :]) for kind in kinds],
                    [r for kind in kinds for r in ha.rows(ptrow[kind], ptrow[kind] + 1)], bD)
            dn = 4 + pp
            self.tt(fa.t[:, dn, :].rearrange('p (h t) -> p h t', h=4), PS[bD][:, :].rearrange('p (h t) -> p h t', h=4),
                    bc(self.par(l, 'esink', 4 * g, n=4).unsqueeze(2), [128, 4, 128]), ALU.add, [self.psr[bD]], fa.rows(dn, dn + 1))
            self.act(fa.t[:, dn, :], fa.t[:, dn, :], AF.Ln, fa.rows(dn, dn + 1), fa.rows(dn, dn + 1))
            self.act(fa.t[:, dn, :], fa.t[:, dn, :], AF.Exp, fa.rows(dn, dn + 1), fa.rows(dn, dn + 1), scale=-1.0)
            for p2 in range(2):
                self.tt(ha.t[p2 * 64: p2 * 64 + 64, YA0 + 2 * g: YA0 + 2 * g + 2, qb * 128: qb * 128 + 128],
                        PS[bO][p2 * 64: p2 * 64 + 64, 0:256].rearrange('p (i t) -> p i t', i=2),
                        fa.t[p2 * 64: p2 * 64 + 64, dn, :].rearrange('p (h t) -> p h t', h=4)[:, p2::2, :],
                        ALU.mult, [self.psr[bO]] + fa.rows(dn, dn + 1), ha.rows(YA0 + 2 * g, YA0 + 2 * g + 2))

        steps = [(qb, g) for qb in range(4) for g in range(2)]
        ctx = att_s(*steps[0])
        for i, st in enumerate(steps):
            nxt = att_s(*steps[i + 1]) if i + 1 < len(steps) else None
            att_pv(st[0], st[1], ctx)
            ctx = nxt
        self.cpv(kcar.t[0:64, 2 * l: 2 * l + 2, :], kT.t[0:64, :, 384:512], kT.rows(0, 2), kcar.rows(2 * l, 2 * l + 2))
        self.cpv(vcar.t[:, l, :], vtok.t[:, 0, 384:512], vtok.rows(0, 1), vcar.rows(l, l + 1))

        if stop <= 2:
            self.release(bdt)
            return
        ZS0, ACT0 = 0, 4
        s2 = self.wload(l, WIN0 + 2)
        for c in range(4):
            bk = self.proj(s2, c * 128, 128, N)
            self.act(fa.t[:, ZS0 + c, :], PS[bk][:, :], AF.Silu, [self.psr[bk]], fa.rows(ZS0 + c, ZS0 + c + 1), bias=self.par(l, 'bz', c))
        s3 = self.wload(l, WIN0 + 3)
        s4 = self.wload(l, WIN0 + 4)
        xh, convc = self.xh, self.convc
        for c in range(6):
            bk = self.proj(s3 if c < 4 else s4, (c % 4) * 128, 128, N)
            pp = self.rotate('xh', 2)
            self.act(xh.t[:, pp, 3:515], PS[bk][:, :], AF.Identity, [self.psr[bk]], xh.rows(pp, pp + 1), bias=self.par(l, 'bxbc', c))
            cc = l * 6 + c
            self.cpv(xh.t[:, pp, 0:3], convc.t[:, 0, cc * 3: cc * 3 + 3], convc.rows(0, 1), xh.rows(pp, pp + 1))
            ar = 10 + self.rotate('cacc', 2)
            self.ts(fa.t[:, ar, :], xh.t[:, pp, 0:512], self.par(l, 'convw', c * 4), None, ALU.mult, None, xh.rows(pp, pp + 1), fa.rows(ar, ar + 1))
            for w in range(1, 4):
                self.stt(fa.t[:, ar, :], xh.t[:, pp, w:w + 512], self.par(l, 'convw', c * 4 + w), fa.t[:, ar, :], ALU.mult, ALU.add,
                         xh.rows(pp, pp + 1) + fa.rows(ar, ar + 1), fa.rows(ar, ar + 1))
            self.cpv(convc.t[:, 0, cc * 3: cc * 3 + 3], xh.t[:, pp, 512:515], xh.rows(pp, pp + 1), convc.rows(0, 1))
            self.act(fa.t[:, ACT0 + c, :], fa.t[:, ar, :], AF.Silu, fa.rows(ar, ar + 1), fa.rows(ACT0 + c, ACT0 + c + 1), bias=self.par(l, 'convb', c))
        if last:
            self.dma('pool', self.p_conv[l], convc.t[:, 0, l * 18: l * 18 + 18], convc.rows(0, 1), [], 'out')
        DT, DA, AT, RH = 12, 13, 14, 15
        self.act(fa.t[0:8, DT, :], PS[bdt][0:8, :], AF.Exp, [self.psr[bdt]], fa.rows(DT, DT + 1), bias=self.par(l, 'dtbt', 0, rows=8))
        self.release(bdt)
        self.act(fa.t[0:8, DT, :], fa.t[0:8, DT, :], AF.Ln, fa.rows(DT, DT + 1), fa.rows(DT, DT + 1), bias=1.0)
        self.ts(fa.t[0:8, DA, :], fa.t[0:8, DT, :], self.par(l, 'A', 0, rows=8), None, ALU.mult, None, fa.rows(DT, DT + 1), fa.rows(DA, DA + 1))
        for c in range(4):
            self.E('dve', lambda h, c=c: h.tensor_tensor_scan(out=fa.t[0:8, AT, c * 128:(c + 1) * 128], data0=cf.t[0:8, 0, CF_ONE:CF_ONE + 128],
                                                             data1=fa.t[0:8, DA, c * 128:(c + 1) * 128], initial=0.0, op0=ALU.mult, op1=ALU.add),
                   fa.rows(DA, DA + 1), fa.rows(AT, AT + 1))
        atdt, sm = self.atdt, self.sm
        bk = self.bank()
        for blk in range(4):
            self.mm(PS[bk][:, blk * 16: blk * 16 + 8], [(fa.t[0:8, AT, blk * 128: blk * 128 + 128], identf[0:8, 0:8])], fa.rows(AT, AT + 1), bk)
            self.mm(PS[bk][:, blk * 16 + 8: blk * 16 + 16], [(fa.t[0:8, DT, blk * 128: blk * 128 + 128], identf[0:8, 0:8])], fa.rows(DT, DT + 1), bk)
        self.cpv(atdt.t[:, 0, 0:64], PS[bk][:, 0:64], [self.psr[bk]], atdt.rows(0, 1))
        atv = atdt.t[:, 0, 0:64].rearrange('p (c k) -> p c k', c=4)
        self.tt(sm.t[0:8, 0, 0:32].rearrange('p (c h) -> p c h', c=4),
                bc(fa.t[0:8, AT, 127:512:128].unsqueeze(2), [8, 4, 8]), bc(identf[0:8, 0:8].unsqueeze(1), [8, 4, 8]), ALU.mult,
                fa.rows(AT, AT + 1), sm.rows(0, 1))
        bk = self.bank()
        self.mm(PS[bk][:, 0:32], [(onesf[0:8, :], sm.t[0:8, 0, 0:32])], sm.rows(0, 1), bk)
        self.cpv(sm.t[:, 1, 0:32], PS[bk][:, 0:32], [self.psr[bk]], sm.rows(1, 2))
        aend = sm.t[:, 1, 0:32].rearrange('p (c h) -> p c h', c=4)
        wtok = sm.t[:, 2, 0:32].rearrange('p (c h) -> p c h', c=4)
        self.tt(wtok, aend, atv[:, :, 0:8], ALU.subtract, sm.rows(1, 2) + atdt.rows(0, 1), sm.rows(2, 3))
        self.act(sm.t[:, 2, 0:32], sm.t[:, 2, 0:32], AF.Exp, sm.rows(2, 3), sm.rows(2, 3))
        self.tt(wtok, wtok, atv[:, :, 8:16], ALU.mult, sm.rows(2, 3) + atdt.rows(0, 1), sm.rows(2, 3))
        for g in range(2):
            self.act(sm.t[g * 64: g * 64 + 64, 3, 0:16].rearrange('p (c j) -> p c j', c=4), aend[g * 64: g * 64 + 64, :, 4 * g: 4 * g + 4],
                     AF.Exp, sm.rows(1, 2), sm.rows(3, 4))
        Fdec = sm.t[:, 3, 0:16].rearrange('p (c j) -> p c j', c=4)
        XST0, BTOK, BC0, SHD, WT0, BW0 = 2, 6, 0, 7, 9, 25
        for blk in range(4):
            bk = self.bank()
            for c in range(4):
                self.mm(PS[bk][:, c * 128: c * 128 + 128], [(fa.t[:, ACT0 + c, blk * 128: blk * 128 + 128], identf)], fa.rows(ACT0 + c, ACT0 + c + 1), bk)
            self.cpa(ha.t[:, XST0 + blk, :], PS[bk][:, :], [self.psr[bk]], ha.rows(XST0 + blk, XST0 + blk + 1))
        bk = self.bank()
        for blk in range(4):
            self.mm(PS[bk][:, blk * 128: blk * 128 + 128], [(fa.t[:, ACT0 + 4, blk * 128: blk * 128 + 128], identf)], fa.rows(ACT0 + 4, ACT0 + 5), bk)
        self.cpa(ha.t[:, BTOK, :], PS[bk][:, :], [self.psr[bk]], ha.rows(BTOK, BTOK + 1))
        self.cpa(ha.t[:, BC0, :], fa.t[:, ACT0 + 4, :], fa.rows(ACT0 + 4, ACT0 + 5), ha.rows(BC0, BC0 + 1))
        self.cpa(ha.t[:, BC0 + 1, :], fa.t[:, ACT0 + 5, :], fa.rows(ACT0 + 5, ACT0 + 6), ha.rows(BC0 + 1, BC0 + 2))
        for c in range(4):
            for g in range(2):
                self.tt(ha.t[:, BW0 + c, g * 256: g * 256 + 256].rearrange('p (j n) -> p j n', j=4),
                        bc(ha.t[:, BTOK, c * 128 + g * 64: c * 128 + g * 64 + 64].unsqueeze(1), [128, 4, 64]),
                        bc(wtok[:, c, 4 * g: 4 * g + 4].unsqueeze(2), [128, 4, 64]), ALU.mult,
                        ha.rows(BTOK, BTOK + 1) + sm.rows(2, 3), ha.rows(BW0 + c, BW0 + c + 1))
        bS = [self.bank(hold=True), self.bank(hold=True)]
        for c in range(4):
            for h in range(8):
                g, j = divmod(h, 4)
                self.mm(PS[bS[c // 2]][g * 64: g * 64 + 64, (c % 2) * 256 + j * 64:(c % 2) * 256 + j * 64 + 64],
                        [(ha.t[:, BW0 + c, h * 64: h * 64 + 64], ha.t[:, XST0 + c, h * 64: h * 64 + 64])],
                        ha.rows(BW0 + c, BW0 + c + 1) + ha.rows(XST0 + c, XST0 + c + 1), bS[c // 2])
        hT = self.hT
        shd = ha.t[:, SHD:SHD + 2, :].rearrange('p r (c x) -> p (r c) x', c=2)
        hst = hT.t[:, l, :]
        for c in range(4):
            self.cpa(shd[:, c, :], hst, hT.rows(l, l + 1), ha.rows(SHD, SHD + 2))
            self.tt(hst.rearrange('p (j x) -> p j x', j=4), hst.rearrange('p (j x) -> p j x', j=4),
                    bc(Fdec[:, c, :].unsqueeze(2), [128, 4, 64]), ALU.mult, hT.rows(l, l + 1) + sm.rows(3, 4), hT.rows(l, l + 1))
            self.tt(hst, hst, PS[bS[c // 2]][:, (c % 2) * 256:(c % 2) * 256 + 256], ALU.add, hT.rows(l, l + 1) + [self.psr[bS[c // 2]]], hT.rows(l, l + 1))
        self.release(bS[0])
        self.release(bS[1])
        if last:
            self.dma('pool', self.p_ssm[l], hst, hT.rows(l, l + 1), [], 'out')
        bG = [self.bank(hold=True), self.bank(hold=True)]
        for g in range(2):
            for c in range(4):
                self.mm(PS[bG[g]][:, c * 128: c * 128 + 128],
                        [(ha.t[g * 64: g * 64 + 64, BC0, c * 128: c * 128 + 128], ha.t[g * 64: g * 64 + 64, BC0 + 1, c * 128: c * 128 + 128])],
                        ha.rows(BC0, BC0 + 2), bG[g])
        tri = cf.t[:, 0, CF_TRI:CF_TRI + 128]
        bY = None
        ybanks = []
        def ssd_pre(h):
            g, j = divmod(h, 4)
            self.ts(fa.t[0:8, RH, :], fa.t[0:8, AT, :], identf[0:8, h:h + 1], None, ALU.mult, None, fa.rows(AT, AT + 1), fa.rows(RH, RH + 1))
            ba = self.bank()
            self.mm(PS[ba][:, :], [(onesf[0:8, :], fa.t[0:8, RH, :])], fa.rows(RH, RH + 1), ba)
            pp = self.rotate('ssdh', 2)
            DF, EA = 16 + pp, 18 + pp
            dfv = fa.t[:, DF, :].rearrange('p (c t) -> p c t', c=4)
            self.tt(dfv, PS[ba][:, :].rearrange('p (c t) -> p c t', c=4), bc(atv[:, :, h:h + 1], [128, 4, 128]), ALU.subtract,
                    [self.psr[ba]] + atdt.rows(0, 1), fa.rows(DF, DF + 1))
            self.tt(dfv, dfv, bc(tri.unsqueeze(1), [128, 4, 128]), ALU.add, fa.rows(DF, DF + 1), fa.rows(DF, DF + 1))
            self.act(fa.t[:, DF, :], fa.t[:, DF, :], AF.Exp, fa.rows(DF, DF + 1), fa.rows(DF, DF + 1))
            self.act(fa.t[:, EA, :], PS[ba][:, :], AF.Exp, [self.psr[ba]], fa.rows(EA, EA + 1))
            self.tt(dfv, dfv, bc(atv[:, :, 8 + h:9 + h], [128, 4, 128]), ALU.mult, fa.rows(DF, DF + 1) + atdt.rows(0, 1), fa.rows(DF, DF + 1))
            wt = WT0 + pp
            self.tt(ha.t[:, wt, :], fa.t[:, DF, :], PS[bG[g]][:, :], ALU.mult, fa.rows(DF, DF + 1) + [self.psr[bG[g]]], ha.rows(wt, wt + 1))
            cs = (11, 24)[pp]
            self.tt(ha.t[g * 64: g * 64 + 64, cs, :], ha.t[g * 64: g * 64 + 64, BC0 + 1, :], fa.t[g * 64: g * 64 + 64, EA, :], ALU.mult,
                    ha.rows(BC0 + 1, BC0 + 2) + fa.rows(EA, EA + 1), ha.rows(cs, cs + 1))
            return wt, cs

        pre = ssd_pre(0)
        for h in range(8):
            g, j = divmod(h, 4)
            wt, cs = pre
            if h + 1 < 8:
                pre = ssd_pre(h + 1)
            if h % 2 == 0:
                bY = self.bank(hold=True)
            for c in range(4):
                self.mm(PS[bY][(h % 2) * 64:(h % 2) * 64 + 64, c * 128: c * 128 + 128],
                        [(ha.t[:, XST0 + c, h * 64: h * 64 + 64], ha.t[:, wt, c * 128: c * 128 + 128]),
                         (shd[g * 64: g * 64 + 64, c, j * 64: j * 64 + 64], ha.t[g * 64: g * 64 + 64, cs, c * 128: c * 128 + 128])],
                        ha.rows(XST0 + c, XST0 + c + 1) + ha.rows(wt, wt + 1) + ha.rows(SHD, SHD + 2) + ha.rows(cs, cs + 1), bY)
            if h % 2 == 1:
                ybanks.append(bY)
        self.release(bG[0])
        self.release(bG[1])
        yb = ybanks
        self.ssd_post(l, N,
                      lambda i: PS[yb[i]][:, :], lambda i: [self.psr[yb[i]]],
                      lambda i: fa.t[:, ACT0 + i, :], lambda i: fa.rows(ACT0 + i, ACT0 + i + 1),
                      lambda i: fa.t[:, ZS0 + i, :], lambda i: fa.rows(ZS0 + i, ZS0 + i + 1),
                      lambda i: ha.t[:, YM0 + i, :], lambda i: ha.rows(YM0 + i, YM0 + i + 1))
        for b in yb:
            self.release(b)

        if stop <= 3:
            return
        QS0, F0, KH0, G0, E0 = 0, 4, 8, 12, 16
        QB0, KB0, KTOK0, ITOK0, ATM0 = 0, 4, 8, 24, 28
        s5 = self.wload(l, WIN0 + 5)
        s6 = self.wload(l, WIN0 + 6)
        s7 = self.wload(l, WIN0 + 7)

        def hg_proj(h):
            bk = self.proj(s5, h * 128, 128, N)
            self.act(fa.t[:, QS0 + h, :], PS[bk][:, :], AF.Silu, [self.psr[bk]], fa.rows(QS0 + h, QS0 + h + 1), bias=self.par(l, 'bhq', h))
            bk = self.proj(s6, h * 128, 128, N)
            self.act(fa.t[:, F0 + h, :], PS[bk][:, :], AF.Sigmoid, [self.psr[bk]], fa.rows(F0 + h, F0 + h + 1), bias=self.par(l, 'bhf', h))

        def hi_proj(blk):
            bk = self.bank()
            self.mm(PS[bk][:, :],
                    [(xb.t[:, kc, blk * 128: blk * 128 + 128], wr.t[:, s7, kc * 512: kc * 512 + 512]) for kc in range(8)],
                    [wr.res[s7]] + xb.rows(0, 8), bk)
            self.tt(ha.t[:, ITOK0 + blk, :], PS[bk][:, :], brow.t[:, 0, 128:640], ALU.add, [self.psr[bk]] + brow.rows(0, 1), ha.rows(ITOK0 + blk, ITOK0 + blk + 1))
        hg_proj(0)
        dfac = self.dfac
        for h in range(4):
            if h + 1 < 4:
                hg_proj(h + 1)
            hi_proj(h)
            fr, kr, gr, er, qr = fa.rows(F0 + h, F0 + h + 1), fa.rows(KH0 + h, KH0 + h + 1), fa.rows(G0 + h, G0 + h + 1), fa.rows(E0 + h, E0 + h + 1), fa.rows(QS0 + h, QS0 + h + 1)
            self.ts(fa.t[:, F0 + h, :], fa.t[:, F0 + h, :], self.par(l, 'omlb', h), self.par(l, 'lb', h), ALU.mult, ALU.add, fr, fr)
            self.ts(fa.t[:, KH0 + h, :], fa.t[:, F0 + h, :], -1.0, 1.0, ALU.mult, ALU.add, fr, kr)
            self.act(fa.t[:, F0 + h, :], fa.t[:, F0 + h, :], AF.Ln, fr, fr)
            self.E('dve', lambda hh, h=h: hh.tensor_tensor_scan(out=fa.t[:, G0 + h, :], data0=cf.t[:, 0, CF_ONE512:CF_ONE512 + 512],
                                                               data1=fa.t[:, F0 + h, :], initial=0.0, op0=ALU.mult, op1=ALU.add), fr, gr)
            Gv = fa.t[:, G0 + h, :]
            self.tt(fa.t[:, E0 + h, :].rearrange('p (c t) -> p c t', c=8), Gv.rearrange('p (c t) -> p c t', c=8),
                    bc(fa.t[:, G0 + h, 31:512:64].unsqueeze(2), [128, 8, 64]), ALU.subtract, gr, er)
            self.act(fa.t[:, F0 + h, :], fa.t[:, E0 + h, :], AF.Exp, er, fr)
            self.act(fa.t[:, E0 + h, :], fa.t[:, E0 + h, :], AF.Exp, er, er, scale=-1.0)
            self.tt(ha.t[:, QB0 + h, :], fa.t[:, QS0 + h, :], fa.t[:, F0 + h, :], ALU.mult, qr + fr, ha.rows(QB0 + h, QB0 + h + 1))
            self.tt(ha.t[:, KB0 + h, :], fa.t[:, KH0 + h, :], fa.t[:, E0 + h, :], ALU.mult, kr + er, ha.rows(KB0 + h, KB0 + h + 1))
            dr = dfac.rows(0, 1)
            d = dfac.t[:, 0, :].rearrange('p (k h c) -> p k h c', k=3, h=4)
            Gend = fa.t[:, G0 + h, 63:512:64]
            Gref = fa.t[:, G0 + h, 31:512:64]
            self.tt(d[:, 0, h, 1:8], Gend[:, 1:8], Gend[:, 0:7], ALU.subtract, gr, dr)
            self.cpv(d[:, 0, h, 0:1], Gend[:, 0:1], gr, dr)
            self.tt(d[:, 1, h, :], Gend, Gref, ALU.subtract, gr, dr)
            self.tt(d[:, 2, h, 1:8], Gref[:, 1:8], Gend[:, 0:7], ALU.subtract, gr, dr)
            self.cpv(d[:, 2, h, 0:1], Gref[:, 0:1], gr, dr)
        self.act(dfac.t[:, 0, :], dfac.t[:, 0, :], AF.Exp, dfac.rows(0, 1), dfac.rows(0, 1))
        dv = dfac.t[:, 0, :].rearrange('p (k h c) -> p k h c', k=3, h=4)
        hgm = cb.t[:, 0, CB_HGM:CB_HGM + 64]
        for hp in range(2):
            bA = self.bank()
            for h2 in range(2):
                h = hp * 2 + h2
                for c in range(8):
                    self.mm(PS[bA][(c % 2) * 64:(c % 2) * 64 + 64, h2 * 256 + (c // 2) * 64: h2 * 256 + (c // 2) * 64 + 64],
                            [(ha.t[:, KB0 + h, c * 64: c * 64 + 64], ha.t[:, QB0 + h, c * 64: c * 64 + 64])],
                            ha.rows(KB0 + h, KB0 + h + 1) + ha.rows(QB0 + h, QB0 + h + 1), bA)
            self.tt(ha.t[:, ATM0 + hp, :].rearrange('p (a t) -> p a t', a=8), PS[bA][:, :].rearrange('p (a t) -> p a t', a=8),
                    bc(hgm.unsqueeze(1), [128, 8, 64]), ALU.mult, [self.psr[bA]], ha.rows(ATM0 + hp, ATM0 + hp + 1))
        for blk in range(4):
            bk = self.bank()
            for h in range(4):
                self.mm(PS[bk][:, h * 128: h * 128 + 128], [(ha.t[:, KB0 + h, blk * 128: blk * 128 + 128], identb)], ha.rows(KB0 + h, KB0 + h + 1), bk)
            self.cpa(ha.t[:, KTOK0 + blk, :], PS[bk][:, :], [self.psr[bk]], ha.rows(KTOK0 + blk, KTOK0 + blk + 1))
        hgS = self.hgS
        Sv = hgS.t[:, l, :].rearrange('p (h v) -> p h v', h=4)
        Sr = hgS.rows(l, l + 1)
        ub = []
        for c in range(8):
            blk, pb = c // 2, (c % 2) * 64
            bU = self.bank(hold=True)
            for h in range(4):
                self.mm(PS[bU][:, h * 128: h * 128 + 128],
                        [(ha.t[pb:pb + 64, KTOK0 + blk, h * 128: h * 128 + 128], ha.t[pb:pb + 64, ITOK0 + blk, h * 128: h * 128 + 128])],
                        ha.rows(KTOK0 + blk, KTOK0 + blk + 1) + ha.rows(ITOK0 + blk, ITOK0 + blk + 1), bU)
            ub.append(bU)
            if c >= 3:
                self._hg_chain(c - 3, ub, Sv, Sr, dv)
        for c in range(5, 8):
            self._hg_chain(c, ub, Sv, Sr, dv)
        if last:
            self.dma('pool', self.p_hg[l], hgS.t[:, l, :], Sr, [], 'out')
        HGS0 = 8
        s8 = self.wload(l, WIN0 + 8)
        for h in range(4):
            bk = self.proj(s8, h * 128, 128, N)
            self.act(fa.t[:, HGS0 + h, :], PS[bk][:, :], AF.Silu, [self.psr[bk]], fa.rows(HGS0 + h, HGS0 + h + 1), bias=self.par(l, 'bhg', h))
        for h in range(4):
            bO = self.bank(hold=True)
            hp, h2 = divmod(h, 2)
            for c in range(8):
                blk, pb = c // 2, (c % 2) * 64
                self.mm(PS[bO][:, c * 64: c * 64 + 64],
                        [(ha.t[pb:pb + 64, ITOK0 + blk, h * 128: h * 128 + 128],
                          ha.t[pb:pb + 64, ATM0 + hp, h2 * 256 + (c // 2) * 64: h2 * 256 + (c // 2) * 64 + 64]),
                         (ha.t[:, 4 + c, h * 128: h * 128 + 128], ha.t[:, QB0 + h, c * 64: c * 64 + 64])],
                        ha.rows(ITOK0 + blk, ITOK0 + blk + 1) + ha.rows(ATM0 + hp, ATM0 + hp + 1) + ha.rows(4 + c, 5 + c) + ha.rows(QB0 + h, QB0 + h + 1), bO)
            self.hgrn_post(l, N, h, PS[bO][:, :], [self.psr[bO]], fa.t[:, HGS0 + h, :], fa.rows(HGS0 + h, HGS0 + h + 1),
                           ha.t[:, YH0 + h, :], ha.rows(YH0 + h, YH0 + h + 1))
            self.release(bO)

    def _hg_chain(self, c, ub, Sv, Sr, dv):
        fa, ha, PS = self.fa, self.ha, self.ps
        bU = ub[c]
        shr = ha.rows(4 + c, 5 + c)
        dr = self.dfac.rows(0, 1)
        self.tt(ha.t[:, 4 + c, :].rearrange('p (h v) -> p h v', h=4), Sv, bc(dv[:, 2, :, c:c + 1], [128, 4, 128]), ALU.mult, Sr + dr, shr)
        self.tt(fa.t[:, 23, :].rearrange('p (h v) -> p h v', h=4), PS[bU][:, :].rearrange('p (h v) -> p h v', h=4),
                bc(dv[:, 1, :, c:c + 1], [128, 4, 128]), ALU.mult, [self.psr[bU]] + dr, fa.rows(23, 24))
        self.release(bU)
        self.tt(Sv, Sv, bc(dv[:, 0, :, c:c + 1], [128, 4, 128]), ALU.mult, Sr + dr, Sr)
        self.tt(Sv, Sv, fa.t[:, 23, :].rearrange('p (h v) -> p h v', h=4), ALU.add, Sr + fa.rows(23, 24), Sr)

    def mixer_sample(self, l):
        N = NS
        fa, ha, PS, wr, xb = self.fa, self.ha, self.ps, self.wring, self.xb
        cf, cb = self.cf, self.cb
        identf = cf.t[:, 0, CF_ID:CF_ID + 128]
        onesf = cf.t[:, 0, CF_ONE:CF_ONE + 128]
        onesb = cb.t[:, 0, CB_ONE:CB_ONE + 128]
        QT0, PT0, YA0, YM0, YH0 = 0, 8, 12, 16, 20
        brow = self.brow
        self.dma('pool', brow.t[:, 0, :], self.browd[l], [], brow.rows(0, 1), 'brow')

        s0 = self.wload(l, WIN0 + 0)
        for h in range(8):
            bk = self.proj(s0, h * 64, 64, N)
            self.act(ha.t[0:64, QT0 + h, 0:N], PS[bk][0:64, 0:N], AF.Identity, [self.psr[bk]], ha.rows(QT0 + h, QT0 + h + 1),
                     bias=self.par(l, 'bq', h, rows=64))
        s1 = self.wload(l, WIN0 + 1)
        kT, kfin = self.kT, self.kfin
        for g in range(2):
            bk = self.proj(s1, g * 64, 64, N)
            self.act(kT.t[0:64, g, 0:N], PS[bk][0:64, 0:N], AF.Identity, [self.psr[bk]], kT.rows(g, g + 1), bias=self.par(l, 'bk', g, rows=64))
            self.act(kfin.t[0:64, g, 0:N], PS[bk][0:64, 0:N], AF.Identity, [self.psr[bk]], kfin.rows(g, g + 1), bias=self.par(l, 'bk', g, rows=64))
        self.dma('pool', self.s_knew[l], kfin.t[0:64, :, 0:N], kfin.rows(0, 2), [], 'out')
        bv = self.bank()
        self.mm(PS[bv][0:N, 0:128],
                [(xb.t[:, kc, 0:N], wr.t[:, s1, kc * 512 + 128: kc * 512 + 256]) for kc in range(8)],
                [wr.res[s1]] + xb.rows(0, 8), bv)
        vfin, vtok = self.vfin, self.vtok
        self.tt(vfin.t[0:N, 0, :], PS[bv][0:N, 0:128], brow.t[0:N, 0, 0:128], ALU.add, [self.psr[bv]] + brow.rows(0, 1), vfin.rows(0, 1))
        self.tt(vtok.t[0:N, 0, 0:128], PS[bv][0:N, 0:128], brow.t[0:N, 0, 0:128], ALU.add, [self.psr[bv]] + brow.rows(0, 1), vtok.rows(0, 1))
        self.dma('pool', self.s_vnew[l], vfin.t[0:N, 0, :], vfin.rows(0, 1), [], 'out')
        self.dma('pool', self.s_kshift[l], self.kc_nat[l, :, 1:128, :], [], [], 'out')
        self.dma('pool', self.s_vshift[l], self.vc_nat[l, :, 1:128, :], [], [], 'out')
        sK = self.wload(l, 0, dst_fn=lambda t, s: t[0:64, s, :], src=self.kc_T[l])
        sV = self.wload(l, 0, dst_fn=lambda t, s: t[:, s, 0:2048].rearrange('p (b d) -> p b d', b=NS),
                        src=self.vc_nat[l].rearrange('b k d -> k b d'))
        bS = self.bank()
        for b in range(NS):
            for g in range(2):
                self.mm(PS[bS][:, b * 8 + 4 * g: b * 8 + 4 * g + 4],
                        [(wr.t[0:64, sK, (b * 2 + g) * 128:(b * 2 + g) * 128 + 128], ha.t[0:64, QT0 + 4 * g: QT0 + 4 * g + 4, b:b + 1])],
                        [wr.res[sK]] + ha.rows(QT0 + 4 * g, QT0 + 4 * g + 4), bS)
        BIAS = cb.t[:, 0, CB_BIAS:CB_BIAS + 2048].rearrange('p (k h t) -> p k h t', k=2, h=8)
        self.stt(fa.t[:, 0, 0:128].rearrange('p (b h) -> p b h', b=NS), PS[bS][:, 0:128].rearrange('p (b h) -> p b h', b=NS), 0.125,
                 bc(BIAS[:, 0, :, 0:1].rearrange('p h o -> p o h'), [128, NS, 8]), ALU.mult, ALU.add, [self.psr[bS]], fa.rows(0, 1))
        self.act(ha.t[:, PT0, 0:128], fa.t[:, 0, 0:128], AF.Exp, fa.rows(0, 1), ha.rows(PT0, PT0 + 1))
        for g in range(2):
            self.tt(fa.t[0:64, 1, g * 64: g * 64 + 64].rearrange('p (j b) -> p j b', j=4), ha.t[0:64, QT0 + 4 * g: QT0 + 4 * g + 4, 0:N],
                    bc(kT.t[0:64, g, 0:N].unsqueeze(1), [64, 4, N]), ALU.mult, ha.rows(QT0 + 4 * g, QT0 + 4 * g + 4) + kT.rows(g, g + 1), fa.rows(1, 2))
        bN = self.bank()
        self.mm(PS[bN][0:N, 0:128], [(onesf[0:64, 0:N], fa.t[0:64, 1, 0:128])], fa.rows(1, 2), bN)
        self.act(fa.t[0:N, 2, 0:128], PS[bN][0:N, 0:128], AF.Exp, [self.psr[bN]], fa.rows(2, 3), scale=0.125)
        self.tt(ha.t[0:N, PT0 + 1, 0:128].rearrange('p (b h) -> p b h', b=NS), fa.t[0:N, 2, 0:128].rearrange('p (h b) -> p b h', h=8),
                bc(identf[0:N, 0:N].unsqueeze(2), [N, NS, 8]), ALU.mult, fa.rows(2, 3), ha.rows(PT0 + 1, PT0 + 2))
        bO = self.bank()
        for b in range(NS):
            for g in range(2):
                for p2 in range(2):
                    c0 = (g * NS + b) * 2
                    h0 = b * 8 + 4 * g + p2
                    self.mm(PS[bO][p2 * 64: p2 * 64 + 64, c0:c0 + 2],
                            [(wr.t[:, sV, b * 128 + g * 64: b * 128 + g * 64 + 64], ha.t[:, PT0, h0:h0 + 3:2]),
                             (vtok.t[0:N, 0, g * 64: g * 64 + 64], ha.t[0:N, PT0 + 1, h0:h0 + 3:2])],
                            [wr.res[sV]] + ha.rows(PT0, PT0 + 2) + vtok.rows(0, 1), bO)
        bD = self.bank()
        self.mm(PS[bD][:, 0:128], [(onesb, ha.t[:, PT0, 0:128]), (onesb[0:N, :], ha.t[0:N, PT0 + 1, 0:128])], ha.rows(PT0, PT0 + 2), bD)
        self.tt(fa.t[:, 3, 0:128].rearrange('p (b h) -> p b h', b=NS), PS[bD][:, 0:128].rearrange('p (b h) -> p b h', b=NS),
                bc(self.par(l, 'esink', 0, n=8).unsqueeze(1), [128, NS, 8]), ALU.add, [self.psr[bD]], fa.rows(3, 4))
        self.act(fa.t[:, 3, 0:128], fa.t[:, 3, 0:128], AF.Ln, fa.rows(3, 4), fa.rows(3, 4))
        self.act(fa.t[:, 3, 0:128], fa.t[:, 3, 0:128], AF.Exp, fa.rows(3, 4), fa.rows(3, 4), scale=-1.0)
        for p2 in range(2):
            for g in range(2):
                self.tt(ha.t[p2 * 64: p2 * 64 + 64, YA0 + 2 * g: YA0 + 2 * g + 2, 0:N],
                        PS[bO][p2 * 64: p2 * 64 + 64, g * 2 * NS:(g + 1) * 2 * NS].rearrange('p (b i) -> p i b', i=2),
                        fa.t[p2 * 64: p2 * 64 + 64, 3, 0:128].rearrange('p (b h) -> p h b', h=8)[:, 4 * g + p2: 4 * g + p2 + 3: 2, :],
                        ALU.mult, [self.psr[bO]] + fa.rows(3, 4), ha.rows(YA0 + 2 * g, YA0 + 2 * g + 2))

        ZS, ACTR, DTR, DECR, DTXR, YSR, CVT = 4, 5, 6, 7, 8, 9, 10
        s2 = self.wload(l, WIN0 + 2)
        for c in range(4):
            bk = self.proj(s2, c * 128, 128, N)
            self.act(fa.t[:, ZS, c * N:(c + 1) * N], PS[bk][:, 0:N], AF.Silu, [self.psr[bk]], fa.rows(ZS, ZS + 1), bias=self.par(l, 'bz', c))
        xhs = self.xhs
        xv = xhs.t[:, 0, :].rearrange('p (c b w) -> p c b w', c=6, b=NS)
        self.dma('pool', xv[:, :, :, 0:3], self.conv_fm[l], [], xhs.rows(0, 1), 'ld')
        s3 = self.wload(l, WIN0 + 3)
        s4 = self.wload(l, WIN0 + 4)
        for c in range(6):
            bk = self.proj(s3 if c < 4 else s4, (c % 4) * 128, 128, N)
            self.act(xv[:, c, :, 3], PS[bk][:, 0:N], AF.Identity, [self.psr[bk]], xhs.rows(0, 1), bias=self.par(l, 'bxbc', c))
        self.dma('pool', self.s_conv[l], xv[:, :, :, 1:4], xhs.rows(0, 1), [], 'out')
        for c in range(6):
            self.tt(fa.t[:, CVT, c * 64:(c + 1) * 64].rearrange('p (b w) -> p b w', b=NS), xv[:, c, :, :],
                    bc(self.par(l, 'convw', c * 4, n=4).unsqueeze(1), [128, NS, 4]), ALU.mult, xhs.rows(0, 1), fa.rows(CVT, CVT + 1))
        self.E('dve', lambda h: h.tensor_reduce(out=fa.t[:, CVT + 1, 0:96], in_=fa.t[:, CVT, 0:384].rearrange('p (x w) -> p x w', w=4), axis=AX.X, op=ALU.add),
               fa.rows(CVT, CVT + 1), fa.rows(CVT + 1, CVT + 2))
        for c in range(6):
            self.act(fa.t[:, ACTR, c * N:(c + 1) * N], fa.t[:, CVT + 1, c * N:(c + 1) * N], AF.Silu, fa.rows(CVT + 1, CVT + 2), fa.rows(ACTR, ACTR + 1),
                     bias=self.par(l, 'convb', c))
        sd = self.wload(l, DTREP)
        for c in range(4):
            bk = self.proj(sd, c * 128, 128, N)
            self.act(fa.t[:, DTR, c * N:(c + 1) * N], PS[bk][:, 0:N], AF.Exp, [self.psr[bk]], fa.rows(DTR, DTR + 1), bias=self.par(l, 'dtbtrep', c))
        self.act(fa.t[:, DTR, 0:64], fa.t[:, DTR, 0:64], AF.Ln, fa.rows(DTR, DTR + 1), fa.rows(DTR, DTR + 1), bias=1.0)
        for c in range(4):
            self.act(fa.t[:, DECR, c * N:(c + 1) * N], fa.t[:, DTR, c * N:(c + 1) * N], AF.Exp, fa.rows(DTR, DTR + 1), fa.rows(DECR, DECR + 1),
                     scale=self.par(l, 'Arep', c))
        self.tt(fa.t[:, DTXR, 0:64], fa.t[:, DTR, 0:64], fa.t[:, ACTR, 0:64], ALU.mult, fa.rows(DTR, DTR + 1) + fa.rows(ACTR, ACTR + 1), fa.rows(DTXR, DTXR + 1))
        HS, T2, BB, CB_, DG = 12, 16, 20, 22, 0
        hsv = fa.t[:, HS:HS + 4, :].rearrange('p r (a n) -> p (r a) n', n=64)
        hsr = fa.rows(HS, HS + 4)
        dgv = fa.t[:, DG:DG + 2, :].rearrange('p r (b n) -> p (r b) n', n=64)
        bbv = fa.t[:, BB:BB + 2, :].rearrange('p r (b n) -> p (r b) n', n=64)
        cbv = fa.t[:, CB_:CB_ + 2, :].rearrange('p r (b n) -> p (r b) n', n=64)
        t2v = fa.t[:, T2:T2 + 4, :].rearrange('p r (a n) -> p (r a) n', n=64)
        for g in range(2):
            for cp in range(2):
                c = 2 * g + cp
                self.dma('pool', hsv[:, cp * NS:(cp + 1) * NS, :], self.ssm_in[l, :, 2 * c: 2 * c + 2].rearrange('b h p n -> (h p) b n'), [], hsr, 'ld')
            for (row, dst) in ((4, BB), (5, CB_)):
                self.tt(dgv, bc(fa.t[:, ACTR, row * N:(row + 1) * N].unsqueeze(2), [128, NS, 64]),
                        bc(identf[:, g * 64: g * 64 + 64].unsqueeze(1), [128, NS, 64]), ALU.mult, fa.rows(ACTR, ACTR + 1), fa.rows(DG, DG + 2))
                for half in range(2):
                    bk = self.bank()
                    self.mm(PS[bk][:, :], [(onesf, fa.t[:, DG + half, :])], fa.rows(DG, DG + 2), bk)
                    self.cpa(fa.t[:, dst + half, :], PS[bk][:, :], [self.psr[bk]], fa.rows(dst + half, dst + half + 1))
            for cp in range(2):
                c = 2 * g + cp
                sl = slice(cp * NS, (cp + 1) * NS)
                self.tt(hsv[:, sl, :], hsv[:, sl, :], bc(fa.t[:, DECR, c * N:(c + 1) * N].unsqueeze(2), [128, NS, 64]), ALU.mult,
                        hsr + fa.rows(DECR, DECR + 1), hsr)
                self.tt(t2v[:, sl, :], bbv, bc(fa.t[:, DTXR, c * N:(c + 1) * N].unsqueeze(2), [128, NS, 64]), ALU.mult,
                        fa.rows(BB, BB + 2) + fa.rows(DTXR, DTXR + 1), fa.rows(T2, T2 + 4))
                self.tt(hsv[:, sl, :], hsv[:, sl, :], t2v[:, sl, :], ALU.add, hsr + fa.rows(T2, T2 + 4), hsr)
                self.dma('pool', self.s_ssm[l, :, 2 * c: 2 * c + 2].rearrange('b h p n -> (h p) b n'), hsv[:, sl, :], hsr, [], 'out')
                self.tt(t2v[:, sl, :], hsv[:, sl, :], cbv, ALU.mult, hsr + fa.rows(CB_, CB_ + 2), fa.rows(T2, T2 + 4))
                self.E('dve', lambda h, c=c, sl=sl: h.tensor_reduce(out=fa.t[:, YSR, c * N:(c + 1) * N], in_=t2v[:, sl, :], axis=AX.X, op=ALU.add),
                       fa.rows(T2, T2 + 4), fa.rows(YSR, YSR + 1))
        self.ssd_post(l, N,
                      lambda i: fa.t[:, YSR, i * N:(i + 1) * N], lambda i: fa.rows(YSR, YSR + 1),
                      lambda i: fa.t[:, ACTR, i * N:(i + 1) * N], lambda i: fa.rows(ACTR, ACTR + 1),
                      lambda i: fa.t[:, ZS, i * N:(i + 1) * N], lambda i: fa.rows(ZS, ZS + 1),
                      lambda i: ha.t[:, YM0 + i, 0:N], lambda i: ha.rows(YM0 + i, YM0 + i + 1))

        QSR, FR, KHR, HGR, ITK = 4, 5, 6, 7, 8
        s5 = self.wload(l, WIN0 + 5)
        for h in range(4):
            bk = self.proj(s5, h * 128, 128, N)
            self.act(fa.t[:, QSR, h * N:(h + 1) * N], PS[bk][:, 0:N], AF.Silu, [self.psr[bk]], fa.rows(QSR, QSR + 1), bias=self.par(l, 'bhq', h))
        s6 = self.wload(l, WIN0 + 6)
        for h in range(4):
            bk = self.proj(s6, h * 128, 128, N)
            self.act(fa.t[:, FR, h * N:(h + 1) * N], PS[bk][:, 0:N], AF.Sigmoid, [self.psr[bk]], fa.rows(FR, FR + 1), bias=self.par(l, 'bhf', h))
            self.ts(fa.t[:, FR, h * N:(h + 1) * N], fa.t[:, FR, h * N:(h + 1) * N], self.par(l, 'omlb', h), self.par(l, 'lb', h), ALU.mult, ALU.add,
                    fa.rows(FR, FR + 1), fa.rows(FR, FR + 1))
        self.ts(fa.t[:, KHR, 0:64], fa.t[:, FR, 0:64], -1.0, 1.0, ALU.mult, ALU.add, fa.rows(FR, FR + 1), fa.rows(KHR, KHR + 1))
        s8 = self.wload(l, WIN0 + 8)
        for h in range(4):
            bk = self.proj(s8, h * 128, 128, N)
            self.act(fa.t[:, HGR, h * N:(h + 1) * N], PS[bk][:, 0:N], AF.Silu, [self.psr[bk]], fa.rows(HGR, HGR + 1), bias=self.par(l, 'bhg', h))
        s7 = self.wload(l, WIN0 + 7)
        bk = self.bank()
        self.mm(PS[bk][0:N, :],
                [(xb.t[:, kc, 0:N], wr.t[:, s7, kc * 512: kc * 512 + 512]) for kc in range(8)],
                [wr.res[s7]] + xb.rows(0, 8), bk)
        self.tt(fa.t[0:N, ITK, :], PS[bk][0:N, :], brow.t[0:N, 0, 128:640], ALU.add, [self.psr[bk]] + brow.rows(0, 1), fa.rows(ITK, ITK + 1))
        bO = self.bank(hold=True)
        HS, T2, DG = 12, 16, 0
        for grp in range(4):
            hsr = fa.rows(HS, HS + 4)
            self.dma('pool', fa.t[:, HS:HS + 4, :].rearrange('p b (h v) -> p b h v', h=4),
                     self.hg_in[l, 4 * grp: 4 * grp + 4].rearrange('b h k v -> k b h v'), [], hsr, 'ld')
            self.tt(fa.t[0:N, DG:DG + 4, :], bc(fa.t[0:N, ITK, :].unsqueeze(1), [N, 4, 512]),
                    bc(identf[0:N, 4 * grp: 4 * grp + 4].unsqueeze(2), [N, 4, 512]), ALU.mult, fa.rows(ITK, ITK + 1), fa.rows(DG, DG + 4))
            for bb in range(4):
                b = 4 * grp + bb
                bk = self.bank()
                self.mm(PS[bk][:, :], [(onesf[0:N, :], fa.t[0:N, DG + bb, :])], fa.rows(DG, DG + 4), bk)
                khb = fa.t[:, KHR, b:64:N]
                fb = fa.t[:, FR, b:64:N]
                sv = fa.t[:, HS + bb, :].rearrange('p (h v) -> p h v', h=4)
                sr = fa.rows(HS + bb, HS + bb + 1)
                tv = fa.t[:, T2 + bb, :].rearrange('p (h v) -> p h v', h=4)
                tr_ = fa.rows(T2 + bb, T2 + bb + 1)
                self.tt(tv, PS[bk][:, :].rearrange('p (h v) -> p h v', h=4), bc(khb.unsqueeze(2), [128, 4, 128]), ALU.mult,
                        [self.psr[bk]] + fa.rows(KHR, KHR + 1), tr_)
                self.tt(sv, sv, bc(fb.unsqueeze(2), [128, 4, 128]), ALU.mult, sr + fa.rows(FR, FR + 1), sr)
                self.tt(sv, sv, tv, ALU.add, sr + tr_, sr)
                for h in range(4):
                    self.mm(PS[bO][:, h * N + b: h * N + b + 1],
                            [(fa.t[:, HS + bb, h * 128: h * 128 + 128], fa.t[:, QSR, h * N + b: h * N + b + 1])],
                            sr + fa.rows(QSR, QSR + 1), bO)
            self.dma('pool', self.s_hg[l, 4 * grp: 4 * grp + 4].rearrange('b h k v -> k b h v'),
                     fa.t[:, HS:HS + 4, :].rearrange('p b (h v) -> p b h v', h=4), hsr, [], 'out')
        for h in range(4):
            self.hgrn_post(l, N, h, PS[bO][:, h * N:(h + 1) * N], [self.psr[bO]], fa.t[:, HGR, h * N:(h + 1) * N], fa.rows(HGR, HGR + 1),
                           ha.t[:, YH0 + h, 0:N], ha.rows(YH0 + h, YH0 + h + 1))
        self.release(bO)

    def setup_params(self):
        depth = self.depth
        pr = self.pt.rows(0, 1)
        parts = self.cfg.get('setup_parts', (1, 2, 3))
        for l in range(depth if 1 in parts else 0):
            for nm, src, rows, n in (('A', 'alog', 8, 1), ('Arep', 'alogrep', 128, 4)):
                self.act(self.par(l, nm, 0, rows, n), self.par(l, src, 0, rows, n), AF.Exp, pr, pr)
                self.ts(self.par(l, nm, 0, rows, n), self.par(l, nm, 0, rows, n), -1.0, None, ALU.mult, None, pr, pr)
            self.act(self.par(l, 'esink', 0, 128, 8), self.par(l, 'sink', 0, 128, 8), AF.Exp, pr, pr)
            self.tt(self.par(l, 'dtbt', 0, 8, 1), self.par(l, 'bdt', 0, 8, 1), self.par(l, 'dtb', 0, 8, 1), ALU.add, pr, pr)
            self.tt(self.par(l, 'dtbtrep', 0, 128, 4), self.par(l, 'bdtrep', 0, 128, 4), self.par(l, 'dtbrep', 0, 128, 4), ALU.add, pr, pr)
        lg = self.lbl.t[:, 0, 0:4 * depth]
        lr = self.lbl.rows(0, 1)
        lr1 = self.lbl.rows(1, 2)
        lgv = lg.rearrange('p (h d) -> p h d', h=4)
        if 2 not in parts:
            return
        self.act(lg, lg, AF.Exp, lr, lr)
        self.E('dve', lambda h: h.tensor_reduce(out=self.lbl.t[:, 1, 0:4], in_=lgv, axis=AX.X, op=ALU.add), lr, lr1)
        self.act(self.lbl.t[:, 1, 0:4], self.lbl.t[:, 1, 0:4], AF.Ln, lr1, lr1)
        self.act(self.lbl.t[:, 1, 0:4], self.lbl.t[:, 1, 0:4], AF.Exp, lr1, lr1, scale=-1.0)
        self.tt(lgv, lgv, bc(self.lbl.t[:, 1, 0:4].unsqueeze(2), [128, 4, depth]), ALU.mult, lr + lr1, lr)
        for l in range(depth):
            lbp = self.par(l, 'lb', 0, 128, 4)
            if l == 0:
                self.E('dve', lambda h, lbp=lbp: h.memset(lbp, 0.0), [], pr)
            else:
                self.tt(lbp, self.par(l - 1, 'lb', 0, 128, 4), lgv[:, :, l], ALU.add, pr + lr, pr)
            self.ts(self.par(l, 'omlb', 0, 128, 4), lbp, -1.0, 1.0, ALU.mult, ALU.add, pr, pr)
        if 3 not in parts:
            return
        ga, gd = Res('g_pa'), Res('g_pd')
        ga.w = self.pt.res[0].w
        gd.w = self.lbl.res[0].w
        self.tr.gates += [ga, gd]

    def build(self):
        nc, es, tr = self.nc, self.es, self.tr
        T, depth = self.T, self.depth
        self.xin = self.din('xin', [128, 8, T])
        self.wts = self.din('wts', [depth, NB_LAYER, 128, WBLK])
        self.wbf = self.nc.dram_tensor('wbf', [depth, NB_LAYER, 128, WBLK], BF16).ap()
        self.cvt = {}
        self.pard = self.din('par', [128, depth * PW])
        self.lbld = self.din('lblog', [128, 4 * depth])
        self.browd = self.din('brow', [depth, 128, 640])
        self.cfd = self.din('cf', [128, CFW])
        self.cbd = self.din('cb', [128, CBW])
        self.yout = self.dout('yout', [128, 8, T])
        self.p_k = self.dout('p_k', [depth, 64, 2, 128])
        self.p_v = self.dout('p_v', [depth, 128, 128])
        self.p_conv = self.dout('p_conv', [depth, 128, 18])
        self.p_ssm = self.dout('p_ssm', [depth, 128, 256])
        self.p_hg = self.dout('p_hg', [depth, 128, 512])
        if self.sample:
            self.xs_in = self.din('xs_in', [128, 8, NS])
            self.kc_T = self.din('kc_T', [depth, 64, NS * 2 * 128])
            self.kc_nat = self.din('kc_nat', [depth, NS, 128, 128])
            self.vc_nat = self.din('vc_nat', [depth, NS, 128, 128])
            self.conv_fm = self.din('conv_fm', [depth, 128, 6, NS, 3])
            self.ssm_in = self.din('ssm_in', [depth, NS, 8, 64, 64])
            self.hg_in = self.din('hg_in', [depth, NS, 4, 128, 128])
            self.ys_out = self.dout('ys_out', [128, 8, NS])
            self.s_knew = self.dout('s_knew', [depth, 64, 2, NS])
            self.s_vnew = self.dout('s_vnew', [depth, NS, 128])
            self.s_kshift = self.dout('s_kshift', [depth, NS, 127, 128])
            self.s_vshift = self.dout('s_vshift', [depth, NS, 127, 128])
            self.s_conv = self.dout('s_conv', [depth, 128, 6, NS, 3])
            self.s_ssm = self.dout('s_ssm', [depth, NS, 8, 64, 64])
            self.s_hg = self.dout('s_hg', [depth, NS, 4, 128, 128])

        self.nslots = 4
        self.wring = Arena(nc, es, 'wring', self.nslots, WBLK, BF16)
        self.wpin = Arena(nc, es, 'wpin', 3, WBLK, BF16)
        self.xres = Arena(nc, es, 'xres', 8, 512, F32)
        self.xb = Arena(nc, es, 'xb', 8, 512, BF16)
        self.fa = Arena(nc, es, 'fa', 24, 512, F32)
        self.ha = Arena(nc, es, 'ha', 32, 512, BF16)
        self.pt = Arena(nc, es, 'pt', 1, depth * PW, F32)
        self.lbl = Arena(nc, es, 'lbl', 2, 4 * depth, F32)
        self.cf = Arena(nc, es, 'cf', 1, CFW, F32, const=True)
        self.cb = Arena(nc, es, 'cb', 1, CBW, BF16, const=True)
        self.kT = Arena(nc, es, 'kT', 2, 512, BF16)
        self.vtok = Arena(nc, es, 'vtok', 1, 512, BF16)
        self.kcar = Arena(nc, es, 'kcar', 2 * depth, 128, BF16)
        self.vcar = Arena(nc, es, 'vcar', depth, 128, BF16)
        self.kfin = Arena(nc, es, 'kfin', 2, 128, F32)
        self.vfin = Arena(nc, es, 'vfin', 1, 128, F32)
        self.brow = Arena(nc, es, 'brw', 1, 640, F32)
        self.xh = Arena(nc, es, 'xh', 2, 515, F32)
        self.convc = Arena(nc, es, 'convc', 1, depth * 18, F32)
        self.atdt = Arena(nc, es, 'atdt', 1, 64, F32)
        self.sm = Arena(nc, es, 'sm', 4, 32, F32)
        self.dfac = Arena(nc, es, 'dfac', 1, 96, F32)
        self.hT = Arena(nc, es, 'hT', depth, 256, F32)
        self.hgS = Arena(nc, es, 'hgS', depth, 512, F32)
        self.xhs = Arena(nc, es, 'xhs', 1, 6 * NS * 4, F32)
        self.ps = [es.enter_context(nc.psum_tensor(f'ps{i}', [128, 512], F32)) for i in range(8)]
        self.psr = [Res(f'ps{i}') for i in range(8)]

        g0 = Res('g_cf')
        g1 = Res('g_cb')
        tr.emit('sp', lambda h: h.dma_start(out=self.cf.t[:, 0, :], in_=self.cfd), writes=[g0], dma='cst')
        tr.emit('pool', lambda h: h.dma_start(out=self.cb.t[:, 0, :], in_=self.cbd, max_dma_last_dim=4096), writes=[g1], dma='cstb')
        tr.gates += [g0, g1]
        self.dma('pool', self.pt.t[:, 0, :], self.pard, [], self.pt.rows(0, 1), 'cst')
        self.dma('pool', self.lbl.t[:, 0, :], self.lbld, [], self.lbl.rows(0, 1), 'cst')
        if self.cfg.get('stop', 9) > -2:
            self.setup_params()
        self.E('dve', lambda h: h.memset(self.convc.t[:, 0, :], 0.0), [], self.convc.rows(0, 1))
        for l in range(depth):
            self.E('dve', lambda h, l=l: h.memset(self.hT.t[:, l, :], 0.0), [], self.hT.rows(l, l + 1))
            self.E('dve', lambda h, l=l: h.memset(self.hgS.t[:, l, :], 0.0), [], self.hgS.rows(l, l + 1))

        ntile = self.nt + (1 if self.sample else 0)
        for ti in range(ntile):
            prompt = ti < self.nt
            N = 512 if prompt else NS
            t0 = ti * 512
            if prompt:
                self.dma('pool', self.xres.t[:, :, :], self.xin[:, :, t0:t0 + 512], [], self.xres.rows(0, 8), 'xin')
            else:
                self.dma('pool', self.xres.t[:, :, 0:NS], self.xs_in, [], self.xres.rows(0, 8), 'xin')
            for c in range(8):
                self.cpa(self.xb.t[:, c, 0:N], self.xres.t[:, c, 0:N], self.xres.rows(c, c + 1), self.xb.rows(c, c + 1))
            for l in range(depth):
                self.ffn(l, FF1, N)
                self.layernorm(l, 'ln1_g', 'ln1_b', N, 0.5)
                stop = self.cfg.get('stop', 9)
                if stop > 0:
                    if prompt:
                        self.mixer_prompt(l, ti)
                    else:
                        self.mixer_sample(l)
                if stop > 0 or stop == -1:
                    self.merge(l, N)
                    self.layernorm(l, 'ln2_g', 'ln2_b', N, 1.0)
                self.ffn(l, FF2, N)
                self.layernorm(l, 'ln3_g', 'ln3_b', N, 0.5)
            if prompt:
                self.dma('pool', self.yout[:, :, t0:t0 + 512], self.xres.t[:, :, :], self.xres.rows(0, 8), [], 'out')
            else:
                self.dma('pool', self.ys_out, self.xres.t[:, :, 0:NS], self.xres.rows(0, 8), [], 'out')
        tr.replay(nc, es)
        return nc


def host_prepare(inp, cfg):
    depth = cfg['depth']
    T = cfg['n_tiles'] * 512
    par, lblog, brow = layout_params(inp, depth)
    cf, cb = layout_consts()
    wts = np.stack([layout_layer_weights(inp, l) for l in range(depth)])
    xp = np.asarray(inp['x_prompt'], np.float32)
    BP = xp.shape[0]
    in_maps = []
    for c in range(NCORES):
        b = c % BP
        xT = np.ascontiguousarray(xp[b, :T].T)
        m = {'xin': np.ascontiguousarray(xT.reshape(8, 128, T).transpose(1, 0, 2)), 'wts': wts, 'par': par, 'lblog': lblog,
             'brow': brow, 'cf': cf, 'cb': cb}
        if cfg.get('sample', True):
            sl = slice(c * NS, (c + 1) * NS)
            xs = np.asarray(inp['x_sample'], np.float32)[sl, 0, :]
            m['xs_in'] = np.ascontiguousarray(xs.T.reshape(8, 128, NS).transpose(1, 0, 2))
            kc = np.asarray(inp['cache_swa_k'], np.float32)[:depth, sl]
            m['kc_T'] = np.ascontiguousarray(kc.transpose(0, 4, 1, 3, 2)).reshape(depth, 64, NS * 2 * 128)
            m['kc_nat'] = np.ascontiguousarray(kc.reshape(depth, NS, 128, 128))
            m['vc_nat'] = np.ascontiguousarray(np.asarray(inp['cache_swa_v'], np.float32)[:depth, sl].reshape(depth, NS, 128, 128))
            cv = np.asarray(inp['state_conv'], np.float32)[:depth, sl]
            m['conv_fm'] = np.ascontiguousarray(cv.reshape(depth, NS, 3, 6, 128).transpose(0, 4, 3, 1, 2))
            m['ssm_in'] = np.ascontiguousarray(np.asarray(inp['state_ssm'], np.float32)[:depth, sl])
            m['hg_in'] = np.ascontiguousarray(np.asarray(inp['state_hgrn'], np.float32)[:depth, sl])
        in_maps.append(m)
    return in_maps


def assemble(res, cfg, BP):
    depth = cfg['depth']
    T = cfg['n_tiles'] * 512
    f = np.float32
    y_prompt = np.stack([res[b]['yout'].transpose(1, 0, 2).reshape(1024, T).T for b in range(BP)]).astype(f)
    p_k = np.stack([np.stack([res[b]['p_k'][l].transpose(2, 1, 0) for b in range(BP)]) for l in range(depth)]).astype(f)
    p_v = np.stack([np.stack([res[b]['p_v'][l].reshape(128, 2, 64) for b in range(BP)]) for l in range(depth)]).astype(f)
    p_conv = np.stack([np.stack([res[b]['p_conv'][l].reshape(128, 6, 3).transpose(2, 1, 0).reshape(3, 768) for b in range(BP)])
                       for l in range(depth)]).astype(f)
    p_ssm = np.stack([np.stack([res[b]['p_ssm'][l].reshape(2, 64, 4, 64).transpose(0, 2, 3, 1).reshape(8, 64, 64) for b in range(BP)])
                      for l in range(depth)]).astype(f)
    p_hg = np.stack([np.stack([res[b]['p_hg'][l].reshape(128, 4, 128).transpose(1, 0, 2) for b in range(BP)]) for l in range(depth)]).astype(f)
    outs = [y_prompt, None, p_k, p_v, p_conv, p_ssm, p_hg]
    if cfg.get('sample', True):
        ys = np.concatenate([res[c]['ys_out'].transpose(1, 0, 2).reshape(1024, NS).T for c in range(NCORES)], axis=0)[:, None, :].astype(f)
        outs[1] = ys
        sk = np.zeros((depth, NCORES * NS, 128, 2, 64), f)
        sv = np.zeros((depth, NCORES * NS, 128, 2, 64), f)
        sc = np.zeros((depth, NCORES * NS, 3, 768), f)
        for c in range(NCORES):
            sl = slice(c * NS, (c + 1) * NS)
            r = res[c]
            sk[:, sl, :127] = r['s_kshift'].reshape(depth, NS, 127, 2, 64)
            sk[:, sl, 127] = r['s_knew'].transpose(0, 3, 2, 1)
            sv[:, sl, :127] = r['s_vshift'].reshape(depth, NS, 127, 2, 64)
            sv[:, sl, 127] = r['s_vnew'].reshape(depth, NS, 2, 64)
            sc[:, sl] = r['s_conv'].transpose(0, 3, 4, 2, 1).reshape(depth, NS, 3, 768)
        s_ssm = np.concatenate([res[c]['s_ssm'] for c in range(NCORES)], axis=1).astype(f)
        s_hg = np.concatenate([res[c]['s_hg'] for c in range(NCORES)], axis=1).astype(f)
        outs += [sk, sv, sc, s_ssm, s_hg]
    return outs


def run(inp, cfg):
    in_maps = host_prepare(inp, cfg)
    b = Builder(cfg)
    nc = b.build()
    res = run_bass_kernel_spmd(nc, in_maps, core_ids=list(range(NCORES)))
    return assemble(res.results, cfg, np.asarray(inp['x_prompt']).shape[0])


def kernel(**inputs):
    cfg = {'n_tiles': 8, 'depth': 4, 'sample': True}
    return tuple(run(inputs, cfg))
```

```python
import numpy as np
import concourse.bass as bass
import concourse.mybir as mybir
from concourse.bass_utils import run_bass_kernel_spmd
from contextlib import ExitStack

F32 = mybir.dt.float32
BF16 = mybir.dt.bfloat16
AF = mybir.ActivationFunctionType
ALU = mybir.AluOpType
AX = mybir.AxisListType

ENGS = ['pe', 'act', 'dve', 'pool', 'sp']

DFF = 2816
NCORES = 8
NS = 16
PROMPT_CORES = (0, 2, 4, 6)
ALPHA = 8.0 ** 0.25
LN_EPS = 1e-5
RMS_EPS = 1e-6
NEG = -1.0e30


class Res:
    __slots__ = ('name', 'w', 'r', 'const')

    def __init__(self, name, const=False):
        self.name = name
        self.w = None
        self.r = {}
        self.const = const


class Tracker:
    def __init__(self):
        self.streams = {e: [] for e in ENGS}
        self.seen = {e: {} for e in ENGS}
        self.dcount = {}
        self.gates = []
        self.gated = {e: 0 for e in ENGS}

    def emit(self, eng, fn, reads=(), writes=(), dma=None):
        st = self.streams[eng]
        idx = len(st)
        deps = []
        reads = list(reads)
        if self.gated[eng] < len(self.gates):
            reads = reads + self.gates[self.gated[eng]:]
            self.gated[eng] = len(self.gates)
        for r in reads:
            if r.const:
                continue
            if r.w is not None:
                deps.append((0, r.w))
        for w in writes:
            if w.w is not None:
                deps.append((1, w.w))
            for t in w.r.values():
                deps.append((2, t))
        waits = []
        seen = self.seen[eng]
        for kind, t in deps:
            if t[0] == 'E':
                _, e2, i2 = t
                if e2 == eng and dma is None:
                    if eng == 'pe' or idx - i2 > 2:
                        continue
                key = ('E', e2)
                if seen.get(key, -1) >= i2:
                    continue
                seen[key] = i2
                waits.append(t)
                self.streams[e2][i2][2] = True
            else:
                _, name, val = t
                key = ('D', name)
                if seen.get(key, -1) >= val:
                    continue
                cur = self.dcount[name]
                seen[key] = cur
                waits.append(('D', name, cur))
        if dma is not None:
            dma = dma + '_' + eng
            self.dcount[dma] = self.dcount.get(dma, 0) + 16
            tok = ('D', dma, self.dcount[dma])
        else:
            tok = ('E', eng, idx)
        st.append([fn, waits, False, dma, 0])
        k = tok[1] if tok[0] == 'E' else ('D', tok[1])
        for r in reads:
            if not r.const:
                r.r[k] = tok
        for w in writes:
            w.w = tok
            w.r = {}
        return tok

    def replay(self, nc, es, final_wait_eng='sp'):
        engsem = {e: es.enter_context(nc.semaphore('s_' + e)) for e in ENGS}
        dsem = {n: es.enter_context(nc.semaphore('d_' + n)) for n in self.dcount}
        for e in ENGS:
            c = 0
            for op in self.streams[e]:
                if op[2]:
                    c += 1
                    op[4] = c
        streams = self.streams
        dcount = self.dcount
        block = es.enter_context(nc.Block())

        def run(ename, h):
            for op in streams[ename]:
                fn, waits, flag, dma, val = op
                for t in waits:
                    if t[0] == 'E':
                        h.wait_ge(engsem[t[1]], streams[t[1]][t[2]][4])
                    else:
                        h.wait_ge(dsem[t[1]], t[2])
                ins = fn(h)
                if dma is not None:
                    ins.then_inc(dsem[dma], 16)
                elif flag:
                    ins.then_inc(engsem[ename], 1)
            if ename == final_wait_eng:
                for n, c in dcount.items():
                    h.wait_ge(dsem[n], c)

        @block.tensor
        def _(h):
            run('pe', h)

        @block.scalar
        def _(h):
            run('act', h)

        @block.vector
        def _(h):
            run('dve', h)

        @block.gpsimd
        def _(h):
            run('pool', h)

        @block.sync
        def _(h):
            run('sp', h)


class Arena:
    def __init__(self, nc, es, name, nrows, width, dtype, const=False):
        self.t = es.enter_context(nc.sbuf_tensor('sb_' + name, [128, nrows, width], dtype))
        self.res = [Res(f'{name}{i}', const) for i in range(nrows)]
        self.nrows = nrows
        self.width = width

    def rows(self, a, b):
        return self.res[a:b]


def bc(ap, shape):
    return ap.to_broadcast(list(shape))


WBLK = 4096
FF1, FF2, WIN0, DTREP, BR0, WOUT0 = 0, 18, 36, 51, 52, 55
NB_LAYER = 57
CQ, CK, CV, CZ, CX, CDT, CHQ, CHF, CHI, CHG, CG = 0, 512, 640, 768, 1280, 2048, 2056, 2568, 3080, 3592, 4104


def _blk_cols(w):
    C = w.shape[1]
    nb = C // 512
    v = w.reshape(8, 128, nb, 512)
    return np.ascontiguousarray(v.transpose(2, 1, 0, 3)).reshape(nb, 128, WBLK)


def _blk_wd(wd):
    w = np.zeros((24 * 128, 1024), np.float32)
    w[:DFF] = wd
    v = w.reshape(3, 8, 128, 2, 512)
    return np.ascontiguousarray(v.transpose(3, 0, 2, 1, 4)).reshape(6, 128, WBLK)


def _win_cols(w):
    c = np.zeros((1024, 15 * 512), np.float32)
    c[:, 0:512] = w[:, CQ:CQ + 512]
    c[:, 512:640] = w[:, CK:CK + 128]
    c[:, 640:768] = w[:, CV:CV + 128]
    c[:, 768:776] = w[:, CDT:CDT + 8]
    c[:, 1024:1536] = w[:, CZ:CZ + 512]
    c[:, 1536:2048] = w[:, CX:CX + 512]
    c[:, 2048:2304] = w[:, CX + 512:CX + 768]
    c[:, 2560:3072] = w[:, CHQ:CHQ + 512]
    c[:, 3072:3584] = w[:, CHF:CHF + 512]
    c[:, 3584:4096] = w[:, CHI:CHI + 512]
    c[:, 4096:4608] = w[:, CHG:CHG + 512]
    for ci in range(24):
        j, br = divmod(ci, 3)
        c[:, 4608 + ci * 128: 4608 + ci * 128 + 128] = w[:, CG + br * 1024 + j * 128: CG + br * 1024 + j * 128 + 128]
    return c


def layout_layer_weights(inp, l):
    blks = []
    for names in (('ffn1_wg', 'ffn1_wu', 'ffn1_wd'), ('ffn2_wg', 'ffn2_wu', 'ffn2_wd')):
        for nm in names[:2]:
            w = np.zeros((1024, 3072), np.float32)
            w[:, :DFF] = inp[nm][l]
            blks.append(_blk_cols(w))
        blks.append(_blk_wd(np.asarray(inp[names[2]][l], np.float32)))
    win = np.asarray(inp['w_in'][l], np.float32)
    blks.append(_blk_cols(_win_cols(win)))
    blks.append(_blk_cols(np.ascontiguousarray(np.repeat(win[:, CDT:CDT + 8], 64, axis=1))))
    for nm in ('w_br_att', 'w_br_ssm', 'w_br_hg'):
        W = np.asarray(inp[nm][l], np.float32)
        blks.append(np.ascontiguousarray(W.reshape(4, 128, 1024).transpose(1, 0, 2)).reshape(1, 128, WBLK))
    W = np.asarray(inp['w_out'][l], np.float32)
    blks.append(np.ascontiguousarray(W.reshape(2, 4, 128, 1024).transpose(0, 2, 1, 3)).reshape(2, 128, WBLK))
    out = np.concatenate(blks, axis=0)
    assert out.shape[0] == NB_LAYER
    return out


PFIELDS = [('ln1_g', 8), ('ln1_b', 8), ('ln2_g', 8), ('ln2_b', 8), ('ln3_g', 8), ('ln3_b', 8),
           ('bq', 8), ('bk', 2), ('bz', 4), ('bxbc', 6), ('bdt', 1), ('bhq', 4), ('bhf', 4), ('bhg', 4), ('bgate', 24),
           ('convw', 24), ('convb', 6), ('dtb', 1), ('alog', 1), ('dskip', 4), ('ssmw', 4), ('hgw', 1),
           ('sink', 8), ('bdtrep', 4), ('dtbrep', 4), ('alogrep', 4),
           ('A', 1), ('esink', 8), ('lb', 4), ('omlb', 4), ('Arep', 4), ('dtbt', 1), ('dtbtrep', 4)]
POFF = {}
_p = 0
for _n, _w in PFIELDS:
    POFF[_n] = _p
    _p += _w
PW = _p


def layout_params(inp, depth):
    def fm(v):
        return np.ascontiguousarray(np.asarray(v, np.float32).reshape(-1, 128).T)

    def rep64(v):
        v = np.asarray(v, np.float32)
        return np.ascontiguousarray(np.repeat(v.reshape(4, 2), 64, axis=1).T)
    P = np.zeros((128, depth, PW), np.float32)
    for l in range(depth):
        b = np.asarray(inp['b_in'][l], np.float32)

        def put(name, arr):
            P[:arr.shape[0], l, POFF[name]:POFF[name] + arr.shape[1]] = arr
        for nm in ('ln1_g', 'ln1_b', 'ln2_g', 'ln2_b', 'ln3_g', 'ln3_b'):
            put(nm, fm(inp[nm][l]))
        put('bq', b[CQ:CQ + 512].reshape(8, 64).T)
        put('bk', b[CK:CK + 128].reshape(2, 64).T)
        put('bz', fm(b[CZ:CZ + 512]))
        put('bxbc', fm(b[CX:CX + 768]))
        put('bdt', b[CDT:CDT + 8].reshape(8, 1))
        put('bhq', fm(b[CHQ:CHQ + 512]))
        put('bhf', fm(b[CHF:CHF + 512]))
        put('bhg', fm(b[CHG:CHG + 512]))
        bg = b[CG:CG + 3072]
        put('bgate', np.stack([bg[(ci % 3) * 1024 + (ci // 3) * 128: (ci % 3) * 1024 + (ci // 3) * 128 + 128] for ci in range(24)], axis=1))
        cw = np.asarray(inp['conv_w'][l], np.float32)
        put('convw', np.ascontiguousarray(cw.reshape(4, 6, 128).transpose(2, 1, 0)).reshape(128, 24))
        put('convb', fm(inp['conv_b'][l]))
        put('dtb', np.asarray(inp['dt_bias'][l], np.float32).reshape(8, 1))
        put('alog', np.asarray(inp['a_log'][l], np.float32).reshape(8, 1))
        put('dskip', rep64(inp['d_skip'][l]))
        put('ssmw', fm(inp['ssm_norm_w'][l]))
        put('hgw', np.asarray(inp['hg_norm_w'][l], np.float32).reshape(128, 1))
        put('sink', np.broadcast_to(np.asarray(inp['att_sinks'][l], np.float32)[None, :], (128, 8)))
        put('bdtrep', rep64(b[CDT:CDT + 8]))
        put('dtbrep', rep64(inp['dt_bias'][l]))
        put('alogrep', rep64(inp['a_log'][l]))
    lg = np.asarray(inp['hg_lb_logits'], np.float32)[:depth]
    lblog = np.ascontiguousarray(lg.reshape(depth, 4, 128).transpose(2, 1, 0))
    brow = np.zeros((depth, 128, 640), np.float32)
    for l in range(depth):
        b = np.asarray(inp['b_in'][l], np.float32)
        brow[l, :, 0:128] = np.broadcast_to(b[CV:CV + 128][None, :], (128, 128))
        brow[l, :, 128:640] = np.broadcast_to(b[CHI:CHI + 512][None, :], (128, 512))
    return P.reshape(128, depth * PW), lblog.reshape(128, 4 * depth), brow


CF_ID, CF_ONE, CF_TRI, CF_ONE512 = 0, 128, 256, 384
CB_ID, CB_ONE, CB_BIAS, CB_HGM = 0, 128, 256, 2304
CFW, CBW = 896, 2368


def layout_consts():
    cf = np.zeros((128, CFW), np.float32)
    cf[:, 0:128] = np.eye(128, dtype=np.float32)
    cf[:, 128:256] = 1.0
    s = np.arange(128)[:, None]
    t = np.arange(128)[None, :]
    cf[:, 256:384] = np.where(s <= t, 0.0, NEG)
    cf[:, 384:896] = 1.0
    cb = np.zeros((128, CBW), np.float32)
    cb[:, 0:128] = np.eye(128, dtype=np.float32)
    cb[:, 128:256] = 1.0
    bias = np.zeros((128, 2, 8, 128), np.float32)
    for kind in range(2):
        dist = (128 if kind == 0 else 0) + t - s
        valid = (dist >= 0) & (dist <= 128)
        for h in range(8):
            slope = 2.0 ** (-(h + 1))
            bias[:, kind, h, :] = np.where(valid, -slope * dist, NEG)
    cb[:, 256:256 + 2048] = bias.reshape(128, 2048)
    s64 = (np.arange(128) % 64)[:, None]
    t64 = np.arange(64)[None, :]
    cb[:, 2304:2368] = (s64 <= t64).astype(np.float32)
    return cf, cb


class Builder:
    def __init__(self, cfg):
        self.cfg = cfg
        self.nt = cfg['n_tiles']
        self.depth = cfg['depth']
        self.sample = cfg.get('sample', True)
        self.T = self.nt * 512
        self.nc = bass.Bass('TRN2', target_bir_lowering=False)
        self.es = ExitStack()
        self.tr = Tracker()
        self.bank_i = 0
        self.held = [False] * 8
        self.wslot_i = 0
        self.rot = {}

    def din(self, name, shape):
        return self.nc.dram_tensor(name, list(shape), F32, kind='ExternalInput').ap()

    def dout(self, name, shape):
        return self.nc.dram_tensor(name, list(shape), F32, kind='ExternalOutput').ap()

    def bank(self, hold=False):
        for _ in range(9):
            b = self.bank_i
            self.bank_i = (b + 1) % 8
            if not self.held[b]:
                break
        else:
            raise RuntimeError('no psum bank')
        if hold:
            self.held[b] = True
        return b

    def release(self, b):
        self.held[b] = False

    def rotate(self, key, n):
        v = self.rot.get(key, 0)
        self.rot[key] = (v + 1) % n
        return v

    def E(self, eng, fn, r=(), w=()):
        self.tr.emit(eng, fn, reads=r, writes=w)

    def act(self, out, in_, func, r, w, bias=None, scale=None):
        kw = {}
        if bias is not None:
            kw['bias'] = bias
        if scale is not None:
            kw['scale'] = scale
        self.tr.emit('act', lambda h: h.activation(out=out, in_=in_, func=func, **kw), reads=r, writes=w)

    def tt(self, out, in0, in1, op, r, w):
        self.tr.emit('dve', lambda h: h.tensor_tensor(out=out, in0=in0, in1=in1, op=op), reads=r, writes=w)

    def ts(self, out, in0, s1, s2, op0, op1, r, w):
        if s2 is None:
            self.tr.emit('dve', lambda h: h.tensor_scalar(out=out, in0=in0, scalar1=s1, scalar2=None, op0=op0), reads=r, writes=w)
        else:
            self.tr.emit('dve', lambda h: h.tensor_scalar(out=out, in0=in0, scalar1=s1, scalar2=s2, op0=op0, op1=op1), reads=r, writes=w)

    def stt(self, out, in0, scalar, in1, op0, op1, r, w):
        self.tr.emit('dve', lambda h: h.scalar_tensor_tensor(out=out, in0=in0, scalar=scalar, in1=in1, op0=op0, op1=op1), reads=r, writes=w)

    def cpv(self, out, in_, r, w):
        self.tr.emit('dve', lambda h: h.tensor_copy(out=out, in_=in_), reads=r, writes=w)

    def cpa(self, out, in_, r, w):
        self.tr.emit('act', lambda h: h.activation(out=out, in_=in_, func=AF.Copy), reads=r, writes=w)

    def mm(self, out, pairs, r, bk):
        pairs = list(pairs)

        def fn(h):
            n = len(pairs)
            ins = None
            for i, (a, b) in enumerate(pairs):
                ins = h.matmul(out, lhsT=a, rhs=b, start=(i == 0), stop=(i == n - 1))
            return ins
        self.tr.emit('pe', fn, reads=r, writes=[self.psr[bk]])

    def dma(self, q, out, in_, r, w, sem):
        self.tr.emit(q, lambda h: h.dma_start(out=out, in_=in_), reads=r, writes=w, dma=sem)

    def par(self, l, name, col=0, rows=128, n=1):
        o = l * PW + POFF[name] + col
        return self.pt.t[0:rows, 0, o:o + n]

    def _wfetch(self, l, blk, dst, dres, sem):
        key = (l, blk)
        if key in self.cvt:
            src = self.wbf[l, blk]
            self.tr.emit('sp', lambda h: h.dma_start(out=dst, in_=src), reads=[self.cvt[key]], writes=[dres], dma=sem)
        else:
            src = self.wts[l, blk]
            self.tr.emit('pool', lambda h: h.dma_start(out=dst, in_=src, max_dma_last_dim=8192), reads=(), writes=[dres], dma=sem)
            r = Res(f'wbf{l}_{blk}')
            out = self.wbf[l, blk]
            self.tr.emit('sp', lambda h: h.dma_start(out=out, in_=dst), reads=[dres], writes=[r], dma='wst')
            self.cvt[key] = r

    def wload(self, l, blk, dst_fn=None, src=None):
        s = self.wslot_i
        self.wslot_i = (s + 1) % self.nslots
        if src is None:
            self._wfetch(l, blk, self.wring.t[:, s, :], self.wring.res[s], f'w{s}')
            return s
        dst = dst_fn(self.wring.t, s)
        self.tr.emit('pool', lambda h: h.dma_start(out=dst, in_=src, max_dma_last_dim=8192), reads=(), writes=[self.wring.res[s]], dma=f'w{s}')
        return s

    def wpin_load(self, l, blk, i):
        self._wfetch(l, blk, self.wpin.t[:, i, :], self.wpin.res[i], f'p{i}')

    def proj(self, s, col, M, N):
        bk = self.bank()
        wr, xb = self.wring, self.xb
        self.mm(self.ps[bk][0:M, 0:N],
                [(wr.t[:, s, kc * 512 + col: kc * 512 + col + M], xb.t[:, kc, 0:N]) for kc in range(8)],
                [wr.res[s]] + xb.rows(0, 8), bk)
        return bk

    def ffn(self, l, base, N):
        fa, hb, wr, PS = self.fa, self.ha, self.wring, self.ps
        for jb in range(6):
            sg = self.wload(l, base + jb)
            su = self.wload(l, base + 6 + jb)
            for jj in range(4):
                j = jb * 4 + jj
                if j >= 22:
                    break
                bg = self.proj(sg, jj * 128, 128, N)
                bu = self.proj(su, jj * 128, 128, N)
                trow = 8 + self.rotate('ffn', 4)
                self.act(fa.t[:, trow, 0:N], PS[bg][:, 0:N], AF.Silu, [self.psr[bg]], fa.rows(trow, trow + 1))
                self.tt(hb.t[:, j, 0:N], fa.t[:, trow, 0:N], PS[bu][:, 0:N], ALU.mult,
                        [self.psr[bu]] + fa.rows(trow, trow + 1), hb.rows(j, j + 1))
        self.ln_begin()
        for half in range(2):
            banks = [self.bank(hold=True) for _ in range(4)]
            for p in range(3):
                s = self.wload(l, base + 12 + half * 3 + p)
                njj = 8 if p < 2 else 6
                for dc4 in range(4):
                    def fn(h, p=p, s=s, dc4=dc4, njj=njj, bk=banks[dc4]):
                        ins = None
                        for jj in range(njj):
                            j = p * 8 + jj
                            ins = h.matmul(PS[bk][:, 0:N], lhsT=wr.t[:, s, jj * 512 + dc4 * 128: jj * 512 + dc4 * 128 + 128],
                                           rhs=hb.t[:, j, 0:N], start=(j == 0), stop=(j == 21))
                        return ins
                    self.tr.emit('pe', fn, reads=[wr.res[s]] + hb.rows(p * 8, p * 8 + njj), writes=[self.psr[banks[dc4]]])
            for dc4 in range(4):
                bk = banks[dc4]
                dc = half * 4 + dc4
                self.stt(fa.t[:, dc, 0:N], self.xres.t[:, dc, 0:N], 2.0 * ALPHA, PS[bk][:, 0:N], ALU.mult, ALU.add,
                         [self.psr[bk]] + self.xres.rows(dc, dc + 1), fa.rows(dc, dc + 1))
                self.release(bk)
                self.ln_feed(dc, N)

    def ln_begin(self):
        self.ln_banks = (self.bank(hold=True), self.bank(hold=True))

    def ln_feed(self, c, N):
        fa, ha, PS = self.fa, self.ha, self.ps
        onesb = self.cb.t[:, 0, CB_ONE:CB_ONE + 128]
        bm, bq = self.ln_banks
        zr = 22 + self.rotate('lnz', 4)
        qr = 26 + self.rotate('lnq', 4)
        self.cpa(ha.t[:, zr, 0:N], fa.t[:, c, 0:N], fa.rows(c, c + 1), ha.rows(zr, zr + 1))
        self.act(ha.t[:, qr, 0:N], fa.t[:, c, 0:N], AF.Square, fa.rows(c, c + 1), ha.rows(qr, qr + 1))

        def fm(h):
            return h.matmul(PS[bm][:, 0:N], lhsT=onesb, rhs=ha.t[:, zr, 0:N], start=(c == 0), stop=(c == 7))

        def fq(h):
            return h.matmul(PS[bq][:, 0:N], lhsT=onesb, rhs=ha.t[:, qr, 0:N], start=(c == 0), stop=(c == 7))
        self.tr.emit('pe', fm, reads=ha.rows(zr, zr + 1), writes=[self.psr[bm]])
        self.tr.emit('pe', fq, reads=ha.rows(qr, qr + 1), writes=[self.psr[bq]])

    def layernorm(self, l, gname, bname, N, zscale):
        fa, PS = self.fa, self.ps
        bm, bq = self.ln_banks
        s2 = zscale * zscale
        self.ts(fa.t[:, 20, 0:N], PS[bm][:, 0:N], 1.0 / 1024.0, None, ALU.mult, None, [self.psr[bm]], fa.rows(20, 21))
        self.tt(fa.t[:, 21, 0:N], fa.t[:, 20, 0:N], fa.t[:, 20, 0:N], ALU.mult, fa.rows(20, 21), fa.rows(21, 22))
        self.stt(fa.t[:, 21, 0:N], PS[bq][:, 0:N], 1.0 / 1024.0, fa.t[:, 21, 0:N], ALU.mult, ALU.subtract,
                 [self.psr[bq]] + fa.rows(21, 22), fa.rows(21, 22))
        self.release(bm)
        self.release(bq)
        self.ts(fa.t[:, 21, 0:N], fa.t[:, 21, 0:N], LN_EPS / s2, None, ALU.add, None, fa.rows(21, 22), fa.rows(21, 22))
        self.act(fa.t[:, 21, 0:N], fa.t[:, 21, 0:N], AF.Ln, fa.rows(21, 22), fa.rows(21, 22))
        self.act(fa.t[:, 21, 0:N], fa.t[:, 21, 0:N], AF.Exp, fa.rows(21, 22), fa.rows(21, 22), scale=-0.5)
        self.tt(fa.t[:, 20, 0:N], fa.t[:, 20, 0:N], fa.t[:, 21, 0:N], ALU.mult, fa.rows(20, 22), fa.rows(20, 21))
        for c in range(8):
            self.tt(fa.t[:, c, 0:N], fa.t[:, c, 0:N], fa.t[:, 21, 0:N], ALU.mult, fa.rows(c, c + 1) + fa.rows(21, 22), fa.rows(c, c + 1))
            self.tt(fa.t[:, c, 0:N], fa.t[:, c, 0:N], fa.t[:, 20, 0:N], ALU.subtract, fa.rows(c, c + 1) + fa.rows(20, 21), fa.rows(c, c + 1))
            self.act(self.xb.t[:, c, 0:N], fa.t[:, c, 0:N], AF.Identity, fa.rows(c, c + 1), self.xb.rows(c, c + 1),
                     bias=self.par(l, bname, c), scale=self.par(l, gname, c))
            self.act(self.xres.t[:, c, 0:N], fa.t[:, c, 0:N], AF.Identity, fa.rows(c, c + 1), self.xres.rows(c, c + 1),
                     bias=self.par(l, bname, c), scale=self.par(l, gname, c))

    def rstd_from_sum(self, out, out_res, src, src_res, n, eps):
        self.ts(out, src, 1.0 / n, eps, ALU.mult, ALU.add, src_res, out_res)
        self.act(out, out, AF.Ln, out_res, out_res)
        self.act(out, out, AF.Exp, out_res, out_res, scale=-0.5)

    def merge(self, l, N):
        fa, ha, PS, wr, wp = self.fa, self.ha, self.ps, self.wring, self.wpin
        YA0, YM0, YH0, MG0 = 12, 16, 20, 0
        for i in range(3):
            self.wpin_load(l, BR0 + i, i)
        ysrc = (YA0, YM0, YH0)
        gslot = {}
        for j in range(8):
            gb = []
            for br in range(3):
                ci = 3 * j + br
                blk = ci // 4
                if blk not in gslot:
                    gslot[blk] = self.wload(l, WIN0 + 9 + blk)
                gb.append(self.proj(gslot[blk], (ci % 4) * 128, 128, N))
            pb = []
            for br in range(3):
                bk = self.bank()
                self.mm(PS[bk][:, 0:N],
                        [(wp.t[:, br, kc * 1024 + j * 128: kc * 1024 + j * 128 + 128], ha.t[:, ysrc[br] + kc, 0:N]) for kc in range(4)],
                        [wp.res[br]] + ha.rows(ysrc[br], ysrc[br] + 4), bk)
                pb.append(bk)
            pp = self.rotate('mg', 2)
            sg = 8 + 3 * pp
            for br in range(3):
                self.act(fa.t[:, sg + br, 0:N], PS[gb[br]][:, 0:N], AF.Sigmoid, [self.psr[gb[br]]], fa.rows(sg + br, sg + br + 1),
                         bias=self.par(l, 'bgate', 3 * j + br))
            t0, t1 = 14 + 2 * pp, 15 + 2 * pp
            self.tt(fa.t[:, t0, 0:N], fa.t[:, sg, 0:N], PS[pb[0]][:, 0:N], ALU.mult, [self.psr[pb[0]]] + fa.rows(sg, sg + 1), fa.rows(t0, t0 + 1))
            self.tt(fa.t[:, t1, 0:N], fa.t[:, sg + 1, 0:N], PS[pb[1]][:, 0:N], ALU.mult, [self.psr[pb[1]]] + fa.rows(sg + 1, sg + 2), fa.rows(t1, t1 + 1))
            self.tt(fa.t[:, t0, 0:N], fa.t[:, t0, 0:N], fa.t[:, t1, 0:N], ALU.add, fa.rows(t0, t0 + 1) + fa.rows(t1, t1 + 1), fa.rows(t0, t0 + 1))
            self.tt(fa.t[:, t1, 0:N], fa.t[:, sg + 2, 0:N], PS[pb[2]][:, 0:N], ALU.mult, [self.psr[pb[2]]] + fa.rows(sg + 2, sg + 3), fa.rows(t1, t1 + 1))
            self.tt(ha.t[:, MG0 + j, 0:N], fa.t[:, t0, 0:N], fa.t[:, t1, 0:N], ALU.add, fa.rows(t0, t0 + 1) + fa.rows(t1, t1 + 1), ha.rows(MG0 + j, MG0 + j + 1))
        so = [self.wload(l, WOUT0), self.wload(l, WOUT0 + 1)]
        self.ln_begin()
        for dc in range(8):
            bk = self.bank()
            self.mm(PS[bk][:, 0:N],
                    [(wr.t[:, so[kc // 4], (kc % 4) * 1024 + dc * 128:(kc % 4) * 1024 + dc * 128 + 128], ha.t[:, MG0 + kc, 0:N]) for kc in range(8)],
                    [wr.res[so[0]], wr.res[so[1]]] + ha.rows(MG0, MG0 + 8), bk)
            self.stt(fa.t[:, dc, 0:N], self.xres.t[:, dc, 0:N], ALPHA, PS[bk][:, 0:N], ALU.mult, ALU.add,
                     [self.psr[bk]] + self.xres.rows(dc, dc + 1), fa.rows(dc, dc + 1))
            self.ln_feed(dc, N)

    def ssd_post(self, l, N, ysrc, ysrc_res, xs_ap, xs_res, zs_ap, zs_res, yb_ap, yb_res):
        fa, PS = self.fa, self.ps
        onesb = self.cb.t[:, 0, CB_ONE:CB_ONE + 128]
        bM = self.bank(hold=True)
        for i in range(4):
            self.stt(xs_ap(i), xs_ap(i), self.par(l, 'dskip', i), ysrc(i), ALU.mult, ALU.add, ysrc_res(i) + xs_res(i), xs_res(i))
            self.tt(xs_ap(i), xs_ap(i), zs_ap(i), ALU.mult, xs_res(i) + zs_res(i), xs_res(i))
            sq = 30 + self.rotate('sq', 2)
            self.act(self.ha.t[:, sq, 0:N], xs_ap(i), AF.Square, xs_res(i), self.ha.rows(sq, sq + 1))

            def fn(h, i=i, sq=sq):
                return h.matmul(PS[bM][:, 0:N], lhsT=onesb, rhs=self.ha.t[:, sq, 0:N], start=(i == 0), stop=(i == 3))
            self.tr.emit('pe', fn, reads=self.ha.rows(sq, sq + 1), writes=[self.psr[bM]])
        self.rstd_from_sum(fa.t[:, 22, 0:N], fa.rows(22, 23), PS[bM][:, 0:N], [self.psr[bM]], 512.0, RMS_EPS)
        self.release(bM)
        for i in range(4):
            self.stt(yb_ap(i), xs_ap(i), self.par(l, 'ssmw', i), fa.t[:, 22, 0:N], ALU.mult, ALU.mult,
                     xs_res(i) + fa.rows(22, 23), yb_res(i))

    def hgrn_post(self, l, N, h, osrc, osrc_res, hgs_ap, hgs_res, yh_ap, yh_res):
        fa, PS = self.fa, self.ps
        onesb = self.cb.t[:, 0, CB_ONE:CB_ONE + 128]
        sq = 30 + self.rotate('sq', 2)
        self.act(self.ha.t[:, sq, 0:N], osrc, AF.Square, osrc_res, self.ha.rows(sq, sq + 1))
        bM = self.bank()
        self.mm(PS[bM][:, 0:N], [(onesb, self.ha.t[:, sq, 0:N])], self.ha.rows(sq, sq + 1), bM)
        self.rstd_from_sum(fa.t[:, 22, 0:N], fa.rows(22, 23), PS[bM][:, 0:N], [self.psr[bM]], 128.0, RMS_EPS)
        self.tt(fa.t[:, 23, 0:N], osrc, fa.t[:, 22, 0:N], ALU.mult, osrc_res + fa.rows(22, 23), fa.rows(23, 24))
        self.stt(yh_ap, fa.t[:, 23, 0:N], self.par(l, 'hgw', 0), hgs_ap, ALU.mult, ALU.mult, fa.rows(23, 24) + hgs_res, yh_res)

    def mixer_prompt(self, l, ti):
        N = 512
        fa, ha, PS, wr, xb = self.fa, self.ha, self.ps, self.wring, self.xb
        cf, cb = self.cf, self.cb
        last = (ti == self.nt - 1)
        identf = cf.t[:, 0, CF_ID:CF_ID + 128]
        onesf = cf.t[:, 0, CF_ONE:CF_ONE + 128]
        identb = cb.t[:, 0, CB_ID:CB_ID + 128]
        onesb = cb.t[:, 0, CB_ONE:CB_ONE + 128]
        QT0, PT0, YA0, YM0, YH0 = 0, 8, 12, 16, 20
        kT, vtok, kcar, vcar = self.kT, self.vtok, self.kcar, self.vcar
        brow = self.brow

        self.dma('pool', brow.t[:, 0, :], self.browd[l], [], brow.rows(0, 1), 'brow')

        s0 = self.wload(l, WIN0 + 0)
        for h in range(8):
            bk = self.proj(s0, h * 64, 64, N)
            self.act(ha.t[0:64, QT0 + h, 0:N], PS[bk][0:64, 0:N], AF.Identity, [self.psr[bk]], ha.rows(QT0 + h, QT0 + h + 1),
                     bias=self.par(l, 'bq', h, rows=64))
        s1 = self.wload(l, WIN0 + 1)
        for g in range(2):
            bk = self.proj(s1, g * 64, 64, N)
            self.act(kT.t[0:64, g, 0:N], PS[bk][0:64, 0:N], AF.Identity, [self.psr[bk]], kT.rows(g, g + 1),
                     bias=self.par(l, 'bk', g, rows=64))
            if last:
                self.act(self.kfin.t[0:64, g, :], PS[bk][0:64, 384:512], AF.Identity, [self.psr[bk]], self.kfin.rows(g, g + 1),
                         bias=self.par(l, 'bk', g, rows=64))
        if last:
            self.dma('pool', self.p_k[l], self.kfin.t[0:64, :, :], self.kfin.rows(0, 2), [], 'out')
        bv = self.bank()
        for blk in range(4):
            self.mm(PS[bv][:, blk * 128: blk * 128 + 128],
                    [(xb.t[:, kc, blk * 128: blk * 128 + 128], wr.t[:, s1, kc * 512 + 128: kc * 512 + 256]) for kc in range(8)],
                    [wr.res[s1]] + xb.rows(0, 8), bv)
        self.tt(vtok.t[:, 0, :].rearrange('p (b d) -> p b d', b=4), PS[bv][:, :].rearrange('p (b d) -> p b d', b=4),
                bc(brow.t[:, 0, 0:128].unsqueeze(1), [128, 4, 128]), ALU.add, [self.psr[bv]] + brow.rows(0, 1), vtok.rows(0, 1))
        if last:
            self.tt(self.vfin.t[:, 0, :], PS[bv][:, 384:512], brow.t[:, 0, 0:128], ALU.add, [self.psr[bv]] + brow.rows(0, 1), self.vfin.rows(0, 1))
            self.dma('pool', self.p_v[l], self.vfin.t[:, 0, :], self.vfin.rows(0, 1), [], 'out')
        bdt = self.bank(hold=True)
        self.mm(PS[bdt][0:8, 0:N], [(wr.t[:, s1, kc * 512 + 256: kc * 512 + 264], xb.t[:, kc, 0:N]) for kc in range(8)],
                [wr.res[s1]] + xb.rows(0, 8), bdt)

        stop = self.cfg.get('stop', 9)
        if stop <= 1:
            self.release(bdt)
            return
        BIAS = cb.t[:, 0, CB_BIAS:CB_BIAS + 2048]
        def att_s(qb, g):
            kinds = ([0] if (qb > 0 or ti > 0) else []) + [1]
            pp = self.rotate('att', 2)
            ptrow = {}
            for kind in kinds:
                if kind == 0 and qb == 0:
                    klhs = kcar.t[0:64, l * 2 + g, :]
                    kres = kcar.rows(l * 2 + g, l * 2 + g + 1)
                else:
                    kb = qb - 1 + kind
                    klhs = kT.t[0:64, g, kb * 128: kb * 128 + 128]
                    kres = kT.rows(g, g + 1)
                bS = self.bank()
                self.mm(PS[bS][:, :], [(klhs, ha.t[0:64, QT0 + 4 * g: QT0 + 4 * g + 4, qb * 128: qb * 128 + 128])],
                        kres + ha.rows(QT0 + 4 * g, QT0 + 4 * g + 4), bS)
                sb = 2 * pp + kind
                self.stt(fa.t[:, sb, :], PS[bS][:, :], 0.125, BIAS[:, (kind * 8 + 4 * g) * 128:(kind * 8 + 4 * g + 4) * 128],
                         ALU.mult, ALU.add, [self.psr[bS]], fa.rows(sb, sb + 1))
                pr = PT0 + 2 * pp + kind
                self.act(ha.t[:, pr, :], fa.t[:, sb, :], AF.Exp, fa.rows(sb, sb + 1), ha.rows(pr, pr + 1))
                ptrow[kind] = pr
            return kinds, pp, ptrow

        def att_pv(qb, g, ctx):
            kinds, pp, ptrow = ctx
            bO = self.bank()
            bD = self.bank()
            for p2 in range(2):
                pairs = []
                rr = []
                for kind in kinds:
                    if kind == 0 and qb == 0:
                        vl = vcar.t[:, l, g * 64: g * 64 + 64]
                        rr += vcar.rows(l, l + 1)
                    else:
                        vb = qb - 1 + kind
                        vl = vtok.t[:, 0, vb * 128 + g * 64: vb * 128 + g * 64 + 64]
                        rr += vtok.rows(0, 1)
                    pt = ha.t[:, ptrow[kind], :].rearrange('p (h t) -> p h t', h=4)[:, p2::2, :]
                    pairs.append((vl, pt))
                    rr += ha.rows(ptrow[kind], ptrow[kind] + 1)
                self.mm(PS[bO][p2 * 64: p2 * 64 + 64, 0:256], pairs, rr, bO)
            self.mm(PS[bD][:, :], [(onesb, ha.t[:, ptrow[kind], :]) for kind in kinds],
                    [r for kind in kinds for r in ha.rows(ptrow[kind], ptrow[kind] + 1)], bD)
            dn = 4 + pp
            self.tt(fa.t[:, dn, :].rearrange('p (h t) -> p h t', h=4), PS[bD][:, :].rearrange('p (h t) -> p h t', h=4),
                    bc(self.par(l, 'esink', 4 * g, n=4).unsqueeze(2), [128, 4, 128]), ALU.add, [self.psr[bD]], fa.rows(dn, dn + 1))
            self.act(fa.t[:, dn, :], fa.t[:, dn, :], AF.Ln, fa.rows(dn, dn + 1), fa.rows(dn, dn + 1))
            self.act(fa.t[:, dn, :], fa.t[:, dn, :], AF.Exp, fa.rows(dn, dn + 1), fa.rows(dn, dn + 1), scale=-1.0)
            for p2 in range(2):
                self.tt(ha.t[p2 * 64: p2 * 64 + 64, YA0 + 2 * g: YA0 + 2 * g + 2, qb * 128: qb * 128 + 128],
                        PS[bO][p2 * 64: p2 * 64 + 64, 0:256].rearrange('p (i t) -> p i t', i=2),
                        fa.t[p2 * 64: p2 * 64 + 64, dn, :].rearrange('p (h t) -> p h t', h=4)[:, p2::2, :],
                        ALU.mult, [self.psr[bO]] + fa.rows(dn, dn + 1), ha.rows(YA0 + 2 * g, YA0 + 2 * g + 2))

        steps = [(qb, g) for qb in range(4) for g in range(2)]
        ctx = att_s(*steps[0])
        for i, st in enumerate(steps):
            nxt = att_s(*steps[i + 1]) if i + 1 < len(steps) else None
            att_pv(st[0], st[1], ctx)
            ctx = nxt
        self.cpv(kcar.t[0:64, 2 * l: 2 * l + 2, :], kT.t[0:64, :, 384:512], kT.rows(0, 2), kcar.rows(2 * l, 2 * l + 2))
        self.cpv(vcar.t[:, l, :], vtok.t[:, 0, 384:512], vtok.rows(0, 1), vcar.rows(l, l + 1))

        if stop <= 2:
            self.release(bdt)
            return
        ZS0, ACT0 = 0, 4
        s2 = self.wload(l, WIN0 + 2)
        for c in range(4):
            bk = self.proj(s2, c * 128, 128, N)
            self.act(fa.t[:, ZS0 + c, :], PS[bk][:, :], AF.Silu, [self.psr[bk]], fa.rows(ZS0 + c, ZS0 + c + 1), bias=self.par(l, 'bz', c))
        s3 = self.wload(l, WIN0 + 3)
        s4 = self.wload(l, WIN0 + 4)
        xh, convc = self.xh, self.convc
        for c in range(6):
            bk = self.proj(s3 if c < 4 else s4, (c % 4) * 128, 128, N)
            pp = self.rotate('xh', 2)
            self.act(xh.t[:, pp, 3:515], PS[bk][:, :], AF.Identity, [self.psr[bk]], xh.rows(pp, pp + 1), bias=self.par(l, 'bxbc', c))
            cc = l * 6 + c
            self.cpv(xh.t[:, pp, 0:3], convc.t[:, 0, cc * 3: cc * 3 + 3], convc.rows(0, 1), xh.rows(pp, pp + 1))
            ar = 10 + self.rotate('cacc', 2)
            self.ts(fa.t[:, ar, :], xh.t[:, pp, 0:512], self.par(l, 'convw', c * 4), None, ALU.mult, None, xh.rows(pp, pp + 1), fa.rows(ar, ar + 1))
            for w in range(1, 4):
                self.stt(fa.t[:, ar, :], xh.t[:, pp, w:w + 512], self.par(l, 'convw', c * 4 + w), fa.t[:, ar, :], ALU.mult, ALU.add,
                         xh.rows(pp, pp + 1) + fa.rows(ar, ar + 1), fa.rows(ar, ar + 1))
            self.cpv(convc.t[:, 0, cc * 3: cc * 3 + 3], xh.t[:, pp, 512:515], xh.rows(pp, pp + 1), convc.rows(0, 1))
            self.act(fa.t[:, ACT0 + c, :], fa.t[:, ar, :], AF.Silu, fa.rows(ar, ar + 1), fa.rows(ACT0 + c, ACT0 + c + 1), bias=self.par(l, 'convb', c))
        if last:
            self.dma('pool', self.p_conv[l], convc.t[:, 0, l * 18: l * 18 + 18], convc.rows(0, 1), [], 'out')
        DT, DA, AT, RH = 12, 13, 14, 15
        self.act(fa.t[0:8, DT, :], PS[bdt][0:8, :], AF.Exp, [self.psr[bdt]], fa.rows(DT, DT + 1), bias=self.par(l, 'dtbt', 0, rows=8))
        self.release(bdt)
        self.act(fa.t[0:8, DT, :], fa.t[0:8, DT, :], AF.Ln, fa.rows(DT, DT + 1), fa.rows(DT, DT + 1), bias=1.0)
        self.ts(fa.t[0:8, DA, :], fa.t[0:8, DT, :], self.par(l, 'A', 0, rows=8), None, ALU.mult, None, fa.rows(DT, DT + 1), fa.rows(DA, DA + 1))
        for c in range(4):
            self.E('dve', lambda h, c=c: h.tensor_tensor_scan(out=fa.t[0:8, AT, c * 128:(c + 1) * 128], data0=cf.t[0:8, 0, CF_ONE:CF_ONE + 128],
                                                             data1=fa.t[0:8, DA, c * 128:(c + 1) * 128], initial=0.0, op0=ALU.mult, op1=ALU.add),
                   fa.rows(DA, DA + 1), fa.rows(AT, AT + 1))
        atdt, sm = self.atdt, self.sm
        bk = self.bank()
        for blk in range(4):
            self.mm(PS[bk][:, blk * 16: blk * 16 + 8], [(fa.t[0:8, AT, blk * 128: blk * 128 + 128], identf[0:8, 0:8])], fa.rows(AT, AT + 1), bk)
            self.mm(PS[bk][:, blk * 16 + 8: blk * 16 + 16], [(fa.t[0:8, DT, blk * 128: blk * 128 + 128], identf[0:8, 0:8])], fa.rows(DT, DT + 1), bk)
        self.cpv(atdt.t[:, 0, 0:64], PS[bk][:, 0:64], [self.psr[bk]], atdt.rows(0, 1))
        atv = atdt.t[:, 0, 0:64].rearrange('p (c k) -> p c k', c=4)
        self.tt(sm.t[0:8, 0, 0:32].rearrange('p (c h) -> p c h', c=4),
                bc(fa.t[0:8, AT, 127:512:128].unsqueeze(2), [8, 4, 8]), bc(identf[0:8, 0:8].unsqueeze(1), [8, 4, 8]), ALU.mult,
                fa.rows(AT, AT + 1), sm.rows(0, 1))
        bk = self.bank()
        self.mm(PS[bk][:, 0:32], [(onesf[0:8, :], sm.t[0:8, 0, 0:32])], sm.rows(0, 1), bk)
        self.cpv(sm.t[:, 1, 0:32], PS[bk][:, 0:32], [self.psr[bk]], sm.rows(1, 2))
        aend = sm.t[:, 1, 0:32].rearrange('p (c h) -> p c h', c=4)
        wtok = sm.t[:, 2, 0:32].rearrange('p (c h) -> p c h', c=4)
        self.tt(wtok, aend, atv[:, :, 0:8], ALU.subtract, sm.rows(1, 2) + atdt.rows(0, 1), sm.rows(2, 3))
        self.act(sm.t[:, 2, 0:32], sm.t[:, 2, 0:32], AF.Exp, sm.rows(2, 3), sm.rows(2, 3))
        self.tt(wtok, wtok, atv[:, :, 8:16], ALU.mult, sm.rows(2, 3) + atdt.rows(0, 1), sm.rows(2, 3))
        for g in range(2):
            self.act(sm.t[g * 64: g * 64 + 64, 3, 0:16].rearrange('p (c j) -> p c j', c=4), aend[g * 64: g * 64 + 64, :, 4 * g: 4 * g + 4],
                     AF.Exp, sm.rows(1, 2), sm.rows(3, 4))
        Fdec = sm.t[:, 3, 0:16].rearrange('p (c j) -> p c j', c=4)
        XST0, BTOK, BC0, SHD, WT0, BW0 = 2, 6, 0, 7, 9, 25
        for blk in range(4):
            bk = self.bank()
            for c in range(4):
                self.mm(PS[bk][:, c * 128: c * 128 + 128], [(fa.t[:, ACT0 + c, blk * 128: blk * 128 + 128], identf)], fa.rows(ACT0 + c, ACT0 + c + 1), bk)
            self.cpa(ha.t[:, XST0 + blk, :], PS[bk][:, :], [self.psr[bk]], ha.rows(XST0 + blk, XST0 + blk + 1))
        bk = self.bank()
        for blk in range(4):
            self.mm(PS[bk][:, blk * 128: blk * 128 + 128], [(fa.t[:, ACT0 + 4, blk * 128: blk * 128 + 128], identf)], fa.rows(ACT0 + 4, ACT0 + 5), bk)
        self.cpa(ha.t[:, BTOK, :], PS[bk][:, :], [self.psr[bk]], ha.rows(BTOK, BTOK + 1))
        self.cpa(ha.t[:, BC0, :], fa.t[:, ACT0 + 4, :], fa.rows(ACT0 + 4, ACT0 + 5), ha.rows(BC0, BC0 + 1))
        self.cpa(ha.t[:, BC0 + 1, :], fa.t[:, ACT0 + 5, :], fa.rows(ACT0 + 5, ACT0 + 6), ha.rows(BC0 + 1, BC0 + 2))
        for c in range(4):
            for g in range(2):
                self.tt(ha.t[:, BW0 + c, g * 256: g * 256 + 256].rearrange('p (j n) -> p j n', j=4),
                        bc(ha.t[:, BTOK, c * 128 + g * 64: c * 128 + g * 64 + 64].unsqueeze(1), [128, 4, 64]),
                        bc(wtok[:, c, 4 * g: 4 * g + 4].unsqueeze(2), [128, 4, 64]), ALU.mult,
                        ha.rows(BTOK, BTOK + 1) + sm.rows(2, 3), ha.rows(BW0 + c, BW0 + c + 1))
        bS = [self.bank(hold=True), self.bank(hold=True)]
        for c in range(4):
            for h in range(8):
                g, j = divmod(h, 4)
                self.mm(PS[bS[c // 2]][g * 64: g * 64 + 64, (c % 2) * 256 + j * 64:(c % 2) * 256 + j * 64 + 64],
                        [(ha.t[:, BW0 + c, h * 64: h * 64 + 64], ha.t[:, XST0 + c, h * 64: h * 64 + 64])],
                        ha.rows(BW0 + c, BW0 + c + 1) + ha.rows(XST0 + c, XST0 + c + 1), bS[c // 2])
        hT = self.hT
        shd = ha.t[:, SHD:SHD + 2, :].rearrange('p r (c x) -> p (r c) x', c=2)
        hst = hT.t[:, l, :]
        for c in range(4):
            self.cpa(shd[:, c, :], hst, hT.rows(l, l + 1), ha.rows(SHD, SHD + 2))
            self.tt(hst.rearrange('p (j x) -> p j x', j=4), hst.rearrange('p (j x) -> p j x', j=4),
                    bc(Fdec[:, c, :].unsqueeze(2), [128, 4, 64]), ALU.mult, hT.rows(l, l + 1) + sm.rows(3, 4), hT.rows(l, l + 1))
            self.tt(hst, hst, PS[bS[c // 2]][:, (c % 2) * 256:(c % 2) * 256 + 256], ALU.add, hT.rows(l, l + 1) + [self.psr[bS[c // 2]]], hT.rows(l, l + 1))
        self.release(bS[0])
        self.release(bS[1])
        if last:
            self.dma('pool', self.p_ssm[l], hst, hT.rows(l, l + 1), [], 'out')
        bG = [self.bank(hold=True), self.bank(hold=True)]
        for g in range(2):
            for c in range(4):
                self.mm(PS[bG[g]][:, c * 128: c * 128 + 128],
                        [(ha.t[g * 64: g * 64 + 64, BC0, c * 128: c * 128 + 128], ha.t[g * 64: g * 64 + 64, BC0 + 1, c * 128: c * 128 + 128])],
                        ha.rows(BC0, BC0 + 2), bG[g])
        tri = cf.t[:, 0, CF_TRI:CF_TRI + 128]
        bY = None
        ybanks = []
        def ssd_pre(h):
            g, j = divmod(h, 4)
            self.ts(fa.t[0:8, RH, :], fa.t[0:8, AT, :], identf[0:8, h:h + 1], None, ALU.mult, None, fa.rows(AT, AT + 1), fa.rows(RH, RH + 1))
            ba = self.bank()
            self.mm(PS[ba][:, :], [(onesf[0:8, :], fa.t[0:8, RH, :])], fa.rows(RH, RH + 1), ba)
            pp = self.rotate('ssdh', 2)
            DF, EA = 16 + pp, 18 + pp
            dfv = fa.t[:, DF, :].rearrange('p (c t) -> p c t', c=4)
            self.tt(dfv, PS[ba][:, :].rearrange('p (c t) -> p c t', c=4), bc(atv[:, :, h:h + 1], [128, 4, 128]), ALU.subtract,
                    [self.psr[ba]] + atdt.rows(0, 1), fa.rows(DF, DF + 1))
            self.tt(dfv, dfv, bc(tri.unsqueeze(1), [128, 4, 128]), ALU.add, fa.rows(DF, DF + 1), fa.rows(DF, DF + 1))
            self.act(fa.t[:, DF, :], fa.t[:, DF, :], AF.Exp, fa.rows(DF, DF + 1), fa.rows(DF, DF + 1))
            self.act(fa.t[:, EA, :], PS[ba][:, :], AF.Exp, [self.psr[ba]], fa.rows(EA, EA + 1))
            self.tt(dfv, dfv, bc(atv[:, :, 8 + h:9 + h], [128, 4, 128]), ALU.mult, fa.rows(DF, DF + 1) + atdt.rows(0, 1), fa.rows(DF, DF + 1))
            wt = WT0 + pp
            self.tt(ha.t[:, wt, :], fa.t[:, DF, :], PS[bG[g]][:, :], ALU.mult, fa.rows(DF, DF + 1) + [self.psr[bG[g]]], ha.rows(wt, wt + 1))
            cs = (11, 24)[pp]
            self.tt(ha.t[g * 64: g * 64 + 64, cs, :], ha.t[g * 64: g * 64 + 64, BC0 + 1, :], fa.t[g * 64: g * 64 + 64, EA, :], ALU.mult,
                    ha.rows(BC0 + 1, BC0 + 2) + fa.rows(EA, EA + 1), ha.rows(cs, cs + 1))
            return wt, cs

        pre = ssd_pre(0)
        for h in range(8):
            g, j = divmod(h, 4)
            wt, cs = pre
            if h + 1 < 8:
                pre = ssd_pre(h + 1)
            if h % 2 == 0:
                bY = self.bank(hold=True)
            for c in range(4):
                self.mm(PS[bY][(h % 2) * 64:(h % 2) * 64 + 64, c * 128: c * 128 + 128],
                        [(ha.t[:, XST0 + c, h * 64: h * 64 + 64], ha.t[:, wt, c * 128: c * 128 + 128]),
                         (shd[g * 64: g * 64 + 64, c, j * 64: j * 64 + 64], ha.t[g * 64: g * 64 + 64, cs, c * 128: c * 128 + 128])],
                        ha.rows(XST0 + c, XST0 + c + 1) + ha.rows(wt, wt + 1) + ha.rows(SHD, SHD + 2) + ha.rows(cs, cs + 1), bY)
            if h % 2 == 1:
                ybanks.append(bY)
        self.release(bG[0])
        self.release(bG[1])
        yb = ybanks
        self.ssd_post(l, N,
                      lambda i: PS[yb[i]][:, :], lambda i: [self.psr[yb[i]]],
                      lambda i: fa.t[:, ACT0 + i, :], lambda i: fa.rows(ACT0 + i, ACT0 + i + 1),
                      lambda i: fa.t[:, ZS0 + i, :], lambda i: fa.rows(ZS0 + i, ZS0 + i + 1),
                      lambda i: ha.t[:, YM0 + i, :], lambda i: ha.rows(YM0 + i, YM0 + i + 1))
        for b in yb:
            self.release(b)

        if stop <= 3:
            return
        QS0, F0, KH0, G0, E0 = 0, 4, 8, 12, 16
        QB0, KB0, KTOK0, ITOK0, ATM0 = 0, 4, 8, 24, 28
        s5 = self.wload(l, WIN0 + 5)
        s6 = self.wload(l, WIN0 + 6)
        s7 = self.wload(l, WIN0 + 7)

        def hg_proj(h):
            bk = self.proj(s5, h * 128, 128, N)
            self.act(fa.t[:, QS0 + h, :], PS[bk][:, :], AF.Silu, [self.psr[bk]], fa.rows(QS0 + h, QS0 + h + 1), bias=self.par(l, 'bhq', h))
            bk = self.proj(s6, h * 128, 128, N)
            self.act(fa.t[:, F0 + h, :], PS[bk][:, :], AF.Sigmoid, [self.psr[bk]], fa.rows(F0 + h, F0 + h + 1), bias=self.par(l, 'bhf', h))

        def hi_proj(blk):
            bk = self.bank()
            self.mm(PS[bk][:, :],
                    [(xb.t[:, kc, blk * 128: blk * 128 + 128], wr.t[:, s7, kc * 512: kc * 512 + 512]) for kc in range(8)],
                    [wr.res[s7]] + xb.rows(0, 8), bk)
            self.tt(ha.t[:, ITOK0 + blk, :], PS[bk][:, :], brow.t[:, 0, 128:640], ALU.add, [self.psr[bk]] + brow.rows(0, 1), ha.rows(ITOK0 + blk, ITOK0 + blk + 1))
        hg_proj(0)
        dfac = self.dfac
        for h in range(4):
            if h + 1 < 4:
                hg_proj(h + 1)
            hi_proj(h)
            fr, kr, gr, er, qr = fa.rows(F0 + h, F0 + h + 1), fa.rows(KH0 + h, KH0 + h + 1), fa.rows(G0 + h, G0 + h + 1), fa.rows(E0 + h, E0 + h + 1), fa.rows(QS0 + h, QS0 + h + 1)
            self.ts(fa.t[:, F0 + h, :], fa.t[:, F0 + h, :], self.par(l, 'omlb', h), self.par(l, 'lb', h), ALU.mult, ALU.add, fr, fr)
            self.ts(fa.t[:, KH0 + h, :], fa.t[:, F0 + h, :], -1.0, 1.0, ALU.mult, ALU.add, fr, kr)
            self.act(fa.t[:, F0 + h, :], fa.t[:, F0 + h, :], AF.Ln, fr, fr)
            self.E('dve', lambda hh, h=h: hh.tensor_tensor_scan(out=fa.t[:, G0 + h, :], data0=cf.t[:, 0, CF_ONE512:CF_ONE512 + 512],
                                                               data1=fa.t[:, F0 + h, :], initial=0.0, op0=ALU.mult, op1=ALU.add), fr, gr)
            Gv = fa.t[:, G0 + h, :]
            self.tt(fa.t[:, E0 + h, :].rearrange('p (c t) -> p c t', c=8), Gv.rearrange('p (c t) -> p c t', c=8),
                    bc(fa.t[:, G0 + h, 31:512:64].unsqueeze(2), [128, 8, 64]), ALU.subtract, gr, er)
            self.act(fa.t[:, F0 + h, :], fa.t[:, E0 + h, :], AF.Exp, er, fr)
            self.act(fa.t[:, E0 + h, :], fa.t[:, E0 + h, :], AF.Exp, er, er, scale=-1.0)
            self.tt(ha.t[:, QB0 + h, :], fa.t[:, QS0 + h, :], fa.t[:, F0 + h, :], ALU.mult, qr + fr, ha.rows(QB0 + h, QB0 + h + 1))
            self.tt(ha.t[:, KB0 + h, :], fa.t[:, KH0 + h, :], fa.t[:, E0 + h, :], ALU.mult, kr + er, ha.rows(KB0 + h, KB0 + h + 1))
            dr = dfac.rows(0, 1)
            d = dfac.t[:, 0, :].rearrange('p (k h c) -> p k h c', k=3, h=4)
            Gend = fa.t[:, G0 + h, 63:512:64]
            Gref = fa.t[:, G0 + h, 31:512:64]
            self.tt(d[:, 0, h, 1:8], Gend[:, 1:8], Gend[:, 0:7], ALU.subtract, gr, dr)
            self.cpv(d[:, 0, h, 0:1], Gend[:, 0:1], gr, dr)
            self.tt(d[:, 1, h, :], Gend, Gref, ALU.subtract, gr, dr)
            self.tt(d[:, 2, h, 1:8], Gref[:, 1:8], Gend[:, 0:7], ALU.subtract, gr, dr)
            self.cpv(d[:, 2, h, 0:1], Gref[:, 0:1], gr, dr)
        self.act(dfac.t[:, 0, :], dfac.t[:, 0, :], AF.Exp, dfac.rows(0, 1), dfac.rows(0, 1))
        dv = dfac.t[:, 0, :].rearrange('p (k h c) -> p k h c', k=3, h=4)
        hgm = cb.t[:, 0, CB_HGM:CB_HGM + 64]
        for hp in range(2):
            bA = self.bank()
            for h2 in range(2):
                h = hp * 2 + h2
                for c in range(8):
                    self.mm(PS[bA][(c % 2) * 64:(c % 2) * 64 + 64, h2 * 256 + (c // 2) * 64: h2 * 256 + (c // 2) * 64 + 64],
                            [(ha.t[:, KB0 + h, c * 64: c * 64 + 64], ha.t[:, QB0 + h, c * 64: c * 64 + 64])],
                            ha.rows(KB0 + h, KB0 + h + 1) + ha.rows(QB0 + h, QB0 + h + 1), bA)
            self.tt(ha.t[:, ATM0 + hp, :].rearrange('p (a t) -> p a t', a=8), PS[bA][:, :].rearrange('p (a t) -> p a t', a=8),
                    bc(hgm.unsqueeze(1), [128, 8, 64]), ALU.mult, [self.psr[bA]], ha.rows(ATM0 + hp, ATM0 + hp + 1))
        for blk in range(4):
            bk = self.bank()
            for h in range(4):
                self.mm(PS[bk][:, h * 128: h * 128 + 128], [(ha.t[:, KB0 + h, blk * 128: blk * 128 + 128], identb)], ha.rows(KB0 + h, KB0 + h + 1), bk)
            self.cpa(ha.t[:, KTOK0 + blk, :], PS[bk][:, :], [self.psr[bk]], ha.rows(KTOK0 + blk, KTOK0 + blk + 1))
        hgS = self.hgS
        Sv = hgS.t[:, l, :].rearrange('p (h v) -> p h v', h=4)
        Sr = hgS.rows(l, l + 1)
        ub = []
        for c in range(8):
            blk, pb = c // 2, (c % 2) * 64
            bU = self.bank(hold=True)
            for h in range(4):
                self.mm(PS[bU][:, h * 128: h * 128 + 128],
                        [(ha.t[pb:pb + 64, KTOK0 + blk, h * 128: h * 128 + 128], ha.t[pb:pb + 64, ITOK0 + blk, h * 128: h * 128 + 128])],
                        ha.rows(KTOK0 + blk, KTOK0 + blk + 1) + ha.rows(ITOK0 + blk, ITOK0 + blk + 1), bU)
            ub.append(bU)
            if c >= 3:
                self._hg_chain(c - 3, ub, Sv, Sr, dv)
        for c in range(5, 8):
            self._hg_chain(c, ub, Sv, Sr, dv)
        if last:
            self.dma('pool', self.p_hg[l], hgS.t[:, l, :], Sr, [], 'out')
        HGS0 = 8
        s8 = self.wload(l, WIN0 + 8)
        for h in range(4):
            bk = self.proj(s8, h * 128, 128, N)
            self.act(fa.t[:, HGS0 + h, :], PS[bk][:, :], AF.Silu, [self.psr[bk]], fa.rows(HGS0 + h, HGS0 + h + 1), bias=self.par(l, 'bhg', h))
        for h in range(4):
            bO = self.bank(hold=True)
            hp, h2 = divmod(h, 2)
            for c in range(8):
                blk, pb = c // 2, (c % 2) * 64
                self.mm(PS[bO][:, c * 64: c * 64 + 64],
                        [(ha.t[pb:pb + 64, ITOK0 + blk, h * 128: h * 128 + 128],
                          ha.t[pb:pb + 64, ATM0 + hp, h2 * 256 + (c // 2) * 64: h2 * 256 + (c // 2) * 64 + 64]),
                         (ha.t[:, 4 + c, h * 128: h * 128 + 128], ha.t[:, QB0 + h, c * 64: c * 64 + 64])],
                        ha.rows(ITOK0 + blk, ITOK0 + blk + 1) + ha.rows(ATM0 + hp, ATM0 + hp + 1) + ha.rows(4 + c, 5 + c) + ha.rows(QB0 + h, QB0 + h + 1), bO)
            self.hgrn_post(l, N, h, PS[bO][:, :], [self.psr[bO]], fa.t[:, HGS0 + h, :], fa.rows(HGS0 + h, HGS0 + h + 1),
                           ha.t[:, YH0 + h, :], ha.rows(YH0 + h, YH0 + h + 1))
            self.release(bO)

    def _hg_chain(self, c, ub, Sv, Sr, dv):
        fa, ha, PS = self.fa, self.ha, self.ps
        bU = ub[c]
        shr = ha.rows(4 + c, 5 + c)
        dr = self.dfac.rows(0, 1)
        self.tt(ha.t[:, 4 + c, :].rearrange('p (h v) -> p h v', h=4), Sv, bc(dv[:, 2, :, c:c + 1], [128, 4, 128]), ALU.mult, Sr + dr, shr)
        self.tt(fa.t[:, 23, :].rearrange('p (h v) -> p h v', h=4), PS[bU][:, :].rearrange('p (h v) -> p h v', h=4),
                bc(dv[:, 1, :, c:c + 1], [128, 4, 128]), ALU.mult, [self.psr[bU]] + dr, fa.rows(23, 24))
        self.release(bU)
        self.tt(Sv, Sv, bc(dv[:, 0, :, c:c + 1], [128, 4, 128]), ALU.mult, Sr + dr, Sr)
        self.tt(Sv, Sv, fa.t[:, 23, :].rearrange('p (h v) -> p h v', h=4), ALU.add, Sr + fa.rows(23, 24), Sr)

    def mixer_sample(self, l):
        N = NS
        fa, ha, PS, wr, xb = self.fa, self.ha, self.ps, self.wring, self.xb
        cf, cb = self.cf, self.cb
        identf = cf.t[:, 0, CF_ID:CF_ID + 128]
        onesf = cf.t[:, 0, CF_ONE:CF_ONE + 128]
        onesb = cb.t[:, 0, CB_ONE:CB_ONE + 128]
        QT0, PT0, YA0, YM0, YH0 = 0, 8, 12, 16, 20
        brow = self.brow
        self.dma('pool', brow.t[:, 0, :], self.browd[l], [], brow.rows(0, 1), 'brow')

        s0 = self.wload(l, WIN0 + 0)
        for h in range(8):
            bk = self.proj(s0, h * 64, 64, N)
            self.act(ha.t[0:64, QT0 + h, 0:N], PS[bk][0:64, 0:N], AF.Identity, [self.psr[bk]], ha.rows(QT0 + h, QT0 + h + 1),
                     bias=self.par(l, 'bq', h, rows=64))
        s1 = self.wload(l, WIN0 + 1)
        kT, kfin = self.kT, self.kfin
        for g in range(2):
            bk = self.proj(s1, g * 64, 64, N)
            self.act(kT.t[0:64, g, 0:N], PS[bk][0:64, 0:N], AF.Identity, [self.psr[bk]], kT.rows(g, g + 1), bias=self.par(l, 'bk', g, rows=64))
            self.act(kfin.t[0:64, g, 0:N], PS[bk][0:64, 0:N], AF.Identity, [self.psr[bk]], kfin.rows(g, g + 1), bias=self.par(l, 'bk', g, rows=64))
        self.dma('pool', self.s_knew[l], kfin.t[0:64, :, 0:N], kfin.rows(0, 2), [], 'out')
        bv = self.bank()
        self.mm(PS[bv][0:N, 0:128],
                [(xb.t[:, kc, 0:N], wr.t[:, s1, kc * 512 + 128: kc * 512 + 256]) for kc in range(8)],
                [wr.res[s1]] + xb.rows(0, 8), bv)
        vfin, vtok = self.vfin, self.vtok
        self.tt(vfin.t[0:N, 0, :], PS[bv][0:N, 0:128], brow.t[0:N, 0, 0:128], ALU.add, [self.psr[bv]] + brow.rows(0, 1), vfin.rows(0, 1))
        self.tt(vtok.t[0:N, 0, 0:128], PS[bv][0:N, 0:128], brow.t[0:N, 0, 0:128], ALU.add, [self.psr[bv]] + brow.rows(0, 1), vtok.rows(0, 1))
        self.dma('pool', self.s_vnew[l], vfin.t[0:N, 0, :], vfin.rows(0, 1), [], 'out')
        self.dma('pool', self.s_kshift[l], self.kc_nat[l, :, 1:128, :], [], [], 'out')
        self.dma('pool', self.s_vshift[l], self.vc_nat[l, :, 1:128, :], [], [], 'out')
        sK = self.wload(l, 0, dst_fn=lambda t, s: t[0:64, s, :], src=self.kc_T[l])
        sV = self.wload(l, 0, dst_fn=lambda t, s: t[:, s, 0:2048].rearrange('p (b d) -> p b d', b=NS),
                        src=self.vc_nat[l].rearrange('b k d -> k b d'))
        bS = self.bank()
        for b in range(NS):
            for g in range(2):
                self.mm(PS[bS][:, b * 8 + 4 * g: b * 8 + 4 * g + 4],
                        [(wr.t[0:64, sK, (b * 2 + g) * 128:(b * 2 + g) * 128 + 128], ha.t[0:64, QT0 + 4 * g: QT0 + 4 * g + 4, b:b + 1])],
                        [wr.res[sK]] + ha.rows(QT0 + 4 * g, QT0 + 4 * g + 4), bS)
        BIAS = cb.t[:, 0, CB_BIAS:CB_BIAS + 2048].rearrange('p (k h t) -> p k h t', k=2, h=8)
        self.stt(fa.t[:, 0, 0:128].rearrange('p (b h) -> p b h', b=NS), PS[bS][:, 0:128].rearrange('p (b h) -> p b h', b=NS), 0.125,
                 bc(BIAS[:, 0, :, 0:1].rearrange('p h o -> p o h'), [128, NS, 8]), ALU.mult, ALU.add, [self.psr[bS]], fa.rows(0, 1))
        self.act(ha.t[:, PT0, 0:128], fa.t[:, 0, 0:128], AF.Exp, fa.rows(0, 1), ha.rows(PT0, PT0 + 1))
        for g in range(2):
            self.tt(fa.t[0:64, 1, g * 64: g * 64 + 64].rearrange('p (j b) -> p j b', j=4), ha.t[0:64, QT0 + 4 * g: QT0 + 4 * g + 4, 0:N],
                    bc(kT.t[0:64, g, 0:N].unsqueeze(1), [64, 4, N]), ALU.mult, ha.rows(QT0 + 4 * g, QT0 + 4 * g + 4) + kT.rows(g, g + 1), fa.rows(1, 2))
        bN = self.bank()
        self.mm(PS[bN][0:N, 0:128], [(onesf[0:64, 0:N], fa.t[0:64, 1, 0:128])], fa.rows(1, 2), bN)
        self.act(fa.t[0:N, 2, 0:128], PS[bN][0:N, 0:128], AF.Exp, [self.psr[bN]], fa.rows(2, 3), scale=0.125)
        self.tt(ha.t[0:N, PT0 + 1, 0:128].rearrange('p (b h) -> p b h', b=NS), fa.t[0:N, 2, 0:128].rearrange('p (h b) -> p b h', h=8),
                bc(identf[0:N, 0:N].unsqueeze(2), [N, NS, 8]), ALU.mult, fa.rows(2, 3), ha.rows(PT0 + 1, PT0 + 2))
        bO = self.bank()
        for b in range(NS):
            for g in range(2):
                for p2 in range(2):
                    c0 = (g * NS + b) * 2
                    h0 = b * 8 + 4 * g + p2
                    self.mm(PS[bO][p2 * 64: p2 * 64 + 64, c0:c0 + 2],
                            [(wr.t[:, sV, b * 128 + g * 64: b * 128 + g * 64 + 64], ha.t[:, PT0, h0:h0 + 3:2]),
                             (vtok.t[0:N, 0, g * 64: g * 64 + 64], ha.t[0:N, PT0 + 1, h0:h0 + 3:2])],
                            [wr.res[sV]] + ha.rows(PT0, PT0 + 2) + vtok.rows(0, 1), bO)
        bD = self.bank()
        self.mm(PS[bD][:, 0:128], [(onesb, ha.t[:, PT0, 0:128]), (onesb[0:N, :], ha.t[0:N, PT0 + 1, 0:128])], ha.rows(PT0, PT0 + 2), bD)
        self.tt(fa.t[:, 3, 0:128].rearrange('p (b h) -> p b h', b=NS), PS[bD][:, 0:128].rearrange('p (b h) -> p b h', b=NS),
                bc(self.par(l, 'esink', 0, n=8).unsqueeze(1), [128, NS, 8]), ALU.add, [self.psr[bD]], fa.rows(3, 4))
        self.act(fa.t[:, 3, 0:128], fa.t[:, 3, 0:128], AF.Ln, fa.rows(3, 4), fa.rows(3, 4))
        self.act(fa.t[:, 3, 0:128], fa.t[:, 3, 0:128], AF.Exp, fa.rows(3, 4), fa.rows(3, 4), scale=-1.0)
        for p2 in range(2):
            for g in range(2):
                self.tt(ha.t[p2 * 64: p2 * 64 + 64, YA0 + 2 * g: YA0 + 2 * g + 2, 0:N],
                        PS[bO][p2 * 64: p2 * 64 + 64, g * 2 * NS:(g + 1) * 2 * NS].rearrange('p (b i) -> p i b', i=2),
                        fa.t[p2 * 64: p2 * 64 + 64, 3, 0:128].rearrange('p (b h) -> p h b', h=8)[:, 4 * g + p2: 4 * g + p2 + 3: 2, :],
                        ALU.mult, [self.psr[bO]] + fa.rows(3, 4), ha.rows(YA0 + 2 * g, YA0 + 2 * g + 2))

        ZS, ACTR, DTR, DECR, DTXR, YSR, CVT = 4, 5, 6, 7, 8, 9, 10
        s2 = self.wload(l, WIN0 + 2)
        for c in range(4):
            bk = self.proj(s2, c * 128, 128, N)
            self.act(fa.t[:, ZS, c * N:(c + 1) * N], PS[bk][:, 0:N], AF.Silu, [self.psr[bk]], fa.rows(ZS, ZS + 1), bias=self.par(l, 'bz', c))
        xhs = self.xhs
        xv = xhs.t[:, 0, :].rearrange('p (c b w) -> p c b w', c=6, b=NS)
        self.dma('pool', xv[:, :, :, 0:3], self.conv_fm[l], [], xhs.rows(0, 1), 'ld')
        s3 = self.wload(l, WIN0 + 3)
        s4 = self.wload(l, WIN0 + 4)
        for c in range(6):
            bk = self.proj(s3 if c < 4 else s4, (c % 4) * 128, 128, N)
            self.act(xv[:, c, :, 3], PS[bk][:, 0:N], AF.Identity, [self.psr[bk]], xhs.rows(0, 1), bias=self.par(l, 'bxbc', c))
        self.dma('pool', self.s_conv[l], xv[:, :, :, 1:4], xhs.rows(0, 1), [], 'out')
        for c in range(6):
            self.tt(fa.t[:, CVT, c * 64:(c + 1) * 64].rearrange('p (b w) -> p b w', b=NS), xv[:, c, :, :],
                    bc(self.par(l, 'convw', c * 4, n=4).unsqueeze(1), [128, NS, 4]), ALU.mult, xhs.rows(0, 1), fa.rows(CVT, CVT + 1))
        self.E('dve', lambda h: h.tensor_reduce(out=fa.t[:, CVT + 1, 0:96], in_=fa.t[:, CVT, 0:384].rearrange('p (x w) -> p x w', w=4), axis=AX.X, op=ALU.add),
               fa.rows(CVT, CVT + 1), fa.rows(CVT + 1, CVT + 2))
        for c in range(6):
            self.act(fa.t[:, ACTR, c * N:(c + 1) * N], fa.t[:, CVT + 1, c * N:(c + 1) * N], AF.Silu, fa.rows(CVT + 1, CVT + 2), fa.rows(ACTR, ACTR + 1),
                     bias=self.par(l, 'convb', c))
        sd = self.wload(l, DTREP)
        for c in range(4):
            bk = self.proj(sd, c * 128, 128, N)
            self.act(fa.t[:, DTR, c * N:(c + 1) * N], PS[bk][:, 0:N], AF.Exp, [self.psr[bk]], fa.rows(DTR, DTR + 1), bias=self.par(l, 'dtbtrep', c))
        self.act(fa.t[:, DTR, 0:64], fa.t[:, DTR, 0:64], AF.Ln, fa.rows(DTR, DTR + 1), fa.rows(DTR, DTR + 1), bias=1.0)
        for c in range(4):
            self.act(fa.t[:, DECR, c * N:(c + 1) * N], fa.t[:, DTR, c * N:(c + 1) * N], AF.Exp, fa.rows(DTR, DTR + 1), fa.rows(DECR, DECR + 1),
                     scale=self.par(l, 'Arep', c))
        self.tt(fa.t[:, DTXR, 0:64], fa.t[:, DTR, 0:64], fa.t[:, ACTR, 0:64], ALU.mult, fa.rows(DTR, DTR + 1) + fa.rows(ACTR, ACTR + 1), fa.rows(DTXR, DTXR + 1))
        HS, T2, BB, CB_, DG = 12, 16, 20, 22, 0
        hsv = fa.t[:, HS:HS + 4, :].rearrange('p r (a n) -> p (r a) n', n=64)
        hsr = fa.rows(HS, HS + 4)
        dgv = fa.t[:, DG:DG + 2, :].rearrange('p r (b n) -> p (r b) n', n=64)
        bbv = fa.t[:, BB:BB + 2, :].rearrange('p r (b n) -> p (r b) n', n=64)
        cbv = fa.t[:, CB_:CB_ + 2, :].rearrange('p r (b n) -> p (r b) n', n=64)
        t2v = fa.t[:, T2:T2 + 4, :].rearrange('p r (a n) -> p (r a) n', n=64)
        for g in range(2):
            for cp in range(2):
                c = 2 * g + cp
                self.dma('pool', hsv[:, cp * NS:(cp + 1) * NS, :], self.ssm_in[l, :, 2 * c: 2 * c + 2].rearrange('b h p n -> (h p) b n'), [], hsr, 'ld')
            for (row, dst) in ((4, BB), (5, CB_)):
                self.tt(dgv, bc(fa.t[:, ACTR, row * N:(row + 1) * N].unsqueeze(2), [128, NS, 64]),
                        bc(identf[:, g * 64: g * 64 + 64].unsqueeze(1), [128, NS, 64]), ALU.mult, fa.rows(ACTR, ACTR + 1), fa.rows(DG, DG + 2))
                for half in range(2):
                    bk = self.bank()
                    self.mm(PS[bk][:, :], [(onesf, fa.t[:, DG + half, :])], fa.rows(DG, DG + 2), bk)
                    self.cpa(fa.t[:, dst + half, :], PS[bk][:, :], [self.psr[bk]], fa.rows(dst + half, dst + half + 1))
            for cp in range(2):
                c = 2 * g + cp
                sl = slice(cp * NS, (cp + 1) * NS)
                self.tt(hsv[:, sl, :], hsv[:, sl, :], bc(fa.t[:, DECR, c * N:(c + 1) * N].unsqueeze(2), [128, NS, 64]), ALU.mult,
                        hsr + fa.rows(DECR, DECR + 1), hsr)
                self.tt(t2v[:, sl, :], bbv, bc(fa.t[:, DTXR, c * N:(c + 1) * N].unsqueeze(2), [128, NS, 64]), ALU.mult,
                        fa.rows(BB, BB + 2) + fa.rows(DTXR, DTXR + 1), fa.rows(T2, T2 + 4))
                self.tt(hsv[:, sl, :], hsv[:, sl, :], t2v[:, sl, :], ALU.add, hsr + fa.rows(T2, T2 + 4), hsr)
                self.dma('pool', self.s_ssm[l, :, 2 * c: 2 * c + 2].rearrange('b h p n -> (h p) b n'), hsv[:, sl, :], hsr, [], 'out')
                self.tt(t2v[:, sl, :], hsv[:, sl, :], cbv, ALU.mult, hsr + fa.rows(CB_, CB_ + 2), fa.rows(T2, T2 + 4))
                self.E('dve', lambda h, c=c, sl=sl: h.tensor_reduce(out=fa.t[:, YSR, c * N:(c + 1) * N], in_=t2v[:, sl, :], axis=AX.X, op=ALU.add),
                       fa.rows(T2, T2 + 4), fa.rows(YSR, YSR + 1))
        self.ssd_post(l, N,
                      lambda i: fa.t[:, YSR, i * N:(i + 1) * N], lambda i: fa.rows(YSR, YSR + 1),
                      lambda i: fa.t[:, ACTR, i * N:(i + 1) * N], lambda i: fa.rows(ACTR, ACTR + 1),
                      lambda i: fa.t[:, ZS, i * N:(i + 1) * N], lambda i: fa.rows(ZS, ZS + 1),
                      lambda i: ha.t[:, YM0 + i, 0:N], lambda i: ha.rows(YM0 + i, YM0 + i + 1))

        QSR, FR, KHR, HGR, ITK = 4, 5, 6, 7, 8
        s5 = self.wload(l, WIN0 + 5)
        for h in range(4):
            bk = self.proj(s5, h * 128, 128, N)
            self.act(fa.t[:, QSR, h * N:(h + 1) * N], PS[bk][:, 0:N], AF.Silu, [self.psr[bk]], fa.rows(QSR, QSR + 1), bias=self.par(l, 'bhq', h))
        s6 = self.wload(l, WIN0 + 6)
        for h in range(4):
            bk = self.proj(s6, h * 128, 128, N)
            self.act(fa.t[:, FR, h * N:(h + 1) * N], PS[bk][:, 0:N], AF.Sigmoid, [self.psr[bk]], fa.rows(FR, FR + 1), bias=self.par(l, 'bhf', h))
            self.ts(fa.t[:, FR, h * N:(h + 1) * N], fa.t[:, FR, h * N:(h + 1) * N], self.par(l, 'omlb', h), self.par(l, 'lb', h), ALU.mult, ALU.add,
                    fa.rows(FR, FR + 1), fa.rows(FR, FR + 1))
        self.ts(fa.t[:, KHR, 0:64], fa.t[:, FR, 0:64], -1.0, 1.0, ALU.mult, ALU.add, fa.rows(FR, FR + 1), fa.rows(KHR, KHR + 1))
        s8 = self.wload(l, WIN0 + 8)
        for h in range(4):
            bk = self.proj(s8, h * 128, 128, N)
            self.act(fa.t[:, HGR, h * N:(h + 1) * N], PS[bk][:, 0:N], AF.Silu, [self.psr[bk]], fa.rows(HGR, HGR + 1), bias=self.par(l, 'bhg', h))
        s7 = self.wload(l, WIN0 + 7)
        bk = self.bank()
        self.mm(PS[bk][0:N, :],
                [(xb.t[:, kc, 0:N], wr.t[:, s7, kc * 512: kc * 512 + 512]) for kc in range(8)],
                [wr.res[s7]] + xb.rows(0, 8), bk)
        self.tt(fa.t[0:N, ITK, :], PS[bk][0:N, :], brow.t[0:N, 0, 128:640], ALU.add, [self.psr[bk]] + brow.rows(0, 1), fa.rows(ITK, ITK + 1))
        bO = self.bank(hold=True)
        HS, T2, DG = 12, 16, 0
        for grp in range(4):
            hsr = fa.rows(HS, HS + 4)
            self.dma('pool', fa.t[:, HS:HS + 4, :].rearrange('p b (h v) -> p b h v', h=4),
                     self.hg_in[l, 4 * grp: 4 * grp + 4].rearrange('b h k v -> k b h v'), [], hsr, 'ld')
            self.tt(fa.t[0:N, DG:DG + 4, :], bc(fa.t[0:N, ITK, :].unsqueeze(1), [N, 4, 512]),
                    bc(identf[0:N, 4 * grp: 4 * grp + 4].unsqueeze(2), [N, 4, 512]), ALU.mult, fa.rows(ITK, ITK + 1), fa.rows(DG, DG + 4))
            for bb in range(4):
                b = 4 * grp + bb
                bk = self.bank()
                self.mm(PS[bk][:, :], [(onesf[0:N, :], fa.t[0:N, DG + bb, :])], fa.rows(DG, DG + 4), bk)
                khb = fa.t[:, KHR, b:64:N]
                fb = fa.t[:, FR, b:64:N]
                sv = fa.t[:, HS + bb, :].rearrange('p (h v) -> p h v', h=4)
                sr = fa.rows(HS + bb, HS + bb + 1)
                tv = fa.t[:, T2 + bb, :].rearrange('p (h v) -> p h v', h=4)
                tr_ = fa.rows(T2 + bb, T2 + bb + 1)
                self.tt(tv, PS[bk][:, :].rearrange('p (h v) -> p h v', h=4), bc(khb.unsqueeze(2), [128, 4, 128]), ALU.mult,
                        [self.psr[bk]] + fa.rows(KHR, KHR + 1), tr_)
                self.tt(sv, sv, bc(fb.unsqueeze(2), [128, 4, 128]), ALU.mult, sr + fa.rows(FR, FR + 1), sr)
                self.tt(sv, sv, tv, ALU.add, sr + tr_, sr)
                for h in range(4):
                    self.mm(PS[bO][:, h * N + b: h * N + b + 1],
                            [(fa.t[:, HS + bb, h * 128: h * 128 + 128], fa.t[:, QSR, h * N + b: h * N + b + 1])],
                            sr + fa.rows(QSR, QSR + 1), bO)
            self.dma('pool', self.s_hg[l, 4 * grp: 4 * grp + 4].rearrange('b h k v -> k b h v'),
                     fa.t[:, HS:HS + 4, :].rearrange('p b (h v) -> p b h v', h=4), hsr, [], 'out')
        for h in range(4):
            self.hgrn_post(l, N, h, PS[bO][:, h * N:(h + 1) * N], [self.psr[bO]], fa.t[:, HGR, h * N:(h + 1) * N], fa.rows(HGR, HGR + 1),
                           ha.t[:, YH0 + h, 0:N], ha.rows(YH0 + h, YH0 + h + 1))
        self.release(bO)

    def setup_params(self):
        depth = self.depth
        pr = self.pt.rows(0, 1)
        parts = self.cfg.get('setup_parts', (1, 2, 3))
        for l in range(depth if 1 in parts else 0):
            for nm, src, rows, n in (('A', 'alog', 8, 1), ('Arep', 'alogrep', 128, 4)):
                self.act(self.par(l, nm, 0, rows, n), self.par(l, src, 0, rows, n), AF.Exp, pr, pr)
                self.ts(self.par(l, nm, 0, rows, n), self.par(l, nm, 0, rows, n), -1.0, None, ALU.mult, None, pr, pr)
            self.act(self.par(l, 'esink', 0, 128, 8), self.par(l, 'sink', 0, 128, 8), AF.Exp, pr, pr)
            self.tt(self.par(l, 'dtbt', 0, 8, 1), self.par(l, 'bdt', 0, 8, 1), self.par(l, 'dtb', 0, 8, 1), ALU.add, pr, pr)
            self.tt(self.par(l, 'dtbtrep', 0, 128, 4), self.par(l, 'bdtrep', 0, 128, 4), self.par(l, 'dtbrep', 0, 128, 4), ALU.add, pr, pr)
        lg = self.lbl.t[:, 0, 0:4 * depth]
        lr = self.lbl.rows(0, 1)
        lr1 = self.lbl.rows(1, 2)
        lgv = lg.rearrange('p (h d) -> p h d', h=4)
        if 2 not in parts:
            return
        self.act(lg, lg, AF.Exp, lr, lr)
        self.E('dve', lambda h: h.tensor_reduce(out=self.lbl.t[:, 1, 0:4], in_=lgv, axis=AX.X, op=ALU.add), lr, lr1)
        self.act(self.lbl.t[:, 1, 0:4], self.lbl.t[:, 1, 0:4], AF.Ln, lr1, lr1)
        self.act(self.lbl.t[:, 1, 0:4], self.lbl.t[:, 1, 0:4], AF.Exp, lr1, lr1, scale=-1.0)
        self.tt(lgv, lgv, bc(self.lbl.t[:, 1, 0:4].unsqueeze(2), [128, 4, depth]), ALU.mult, lr + lr1, lr)
        for l in range(depth):
            lbp = self.par(l, 'lb', 0, 128, 4)
            if l == 0:
                self.E('dve', lambda h, lbp=lbp: h.memset(lbp, 0.0), [], pr)
            else:
                self.tt(lbp, self.par(l - 1, 'lb', 0, 128, 4), lgv[:, :, l], ALU.add, pr + lr, pr)
            self.ts(self.par(l, 'omlb', 0, 128, 4), lbp, -1.0, 1.0, ALU.mult, ALU.add, pr, pr)
        if 3 not in parts:
            return
        ga, gd = Res('g_pa'), Res('g_pd')
        ga.w = self.pt.res[0].w
        gd.w = self.lbl.res[0].w
        self.tr.gates += [ga, gd]

    def build(self):
        nc, es, tr = self.nc, self.es, self.tr
        T, depth = self.T, self.depth
        self.xin = self.din('xin', [128, 8, T])
        self.wts = self.din('wts', [depth, NB_LAYER, 128, WBLK])
        self.wbf = self.nc.dram_tensor('wbf', [depth, NB_LAYER, 128, WBLK], BF16).ap()
        self.cvt = {}
        self.pard = self.din('par', [128, depth * PW])
        self.lbld = self.din('lblog', [128, 4 * depth])
        self.browd = self.din('brow', [depth, 128, 640])
        self.cfd = self.din('cf', [128, CFW])
        self.cbd = self.din('cb', [128, CBW])
        self.yout = self.dout('yout', [128, 8, T])
        self.p_k = self.dout('p_k', [depth, 64, 2, 128])
        self.p_v = self.dout('p_v', [depth, 128, 128])
        self.p_conv = self.dout('p_conv', [depth, 128, 18])
        self.p_ssm = self.dout('p_ssm', [depth, 128, 256])
        self.p_hg = self.dout('p_hg', [depth, 128, 512])
        if self.sample:
            self.xs_in = self.din('xs_in', [128, 8, NS])
            self.kc_T = self.din('kc_T', [depth, 64, NS * 2 * 128])
            self.kc_nat = self.din('kc_nat', [depth, NS, 128, 128])
            self.vc_nat = self.din('vc_nat', [depth, NS, 128, 128])
            self.conv_fm = self.din('conv_fm', [depth, 128, 6, NS, 3])
            self.ssm_in = self.din('ssm_in', [depth, NS, 8, 64, 64])
            self.hg_in = self.din('hg_in', [depth, NS, 4, 128, 128])
            self.ys_out = self.dout('ys_out', [128, 8, NS])
            self.s_knew = self.dout('s_knew', [depth, 64, 2, NS])
            self.s_vnew = self.dout('s_vnew', [depth, NS, 128])
            self.s_kshift = self.dout('s_kshift', [depth, NS, 127, 128])
            self.s_vshift = self.dout('s_vshift', [depth, NS, 127, 128])
            self.s_conv = self.dout('s_conv', [depth, 128, 6, NS, 3])
            self.s_ssm = self.dout('s_ssm', [depth, NS, 8, 64, 64])
            self.s_hg = self.dout('s_hg', [depth, NS, 4, 128, 128])

        self.nslots = 4
        self.wring = Arena(nc, es, 'wring', self.nslots, WBLK, BF16)
        self.wpin = Arena(nc, es, 'wpin', 3, WBLK, BF16)
        self.xres = Arena(nc, es, 'xres', 8, 512, F32)
        self.xb = Arena(nc, es, 'xb', 8, 512, BF16)
        self.fa = Arena(nc, es, 'fa', 24, 512, F32)
        self.ha = Arena(nc, es, 'ha', 32, 512, BF16)
        self.pt = Arena(nc, es, 'pt', 1, depth * PW, F32)
        self.lbl = Arena(nc, es, 'lbl', 2, 4 * depth, F32)
        self.cf = Arena(nc, es, 'cf', 1, CFW, F32, const=True)
        self.cb = Arena(nc, es, 'cb', 1, CBW, BF16, const=True)
        self.kT = Arena(nc, es, 'kT', 2, 512, BF16)
        self.vtok = Arena(nc, es, 'vtok', 1, 512, BF16)
        self.kcar = Arena(nc, es, 'kcar', 2 * depth, 128, BF16)
        self.vcar = Arena(nc, es, 'vcar', depth, 128, BF16)
        self.kfin = Arena(nc, es, 'kfin', 2, 128, F32)
        self.vfin = Arena(nc, es, 'vfin', 1, 128, F32)
        self.brow = Arena(nc, es, 'brw', 1, 640, F32)
        self.xh = Arena(nc, es, 'xh', 2, 515, F32)
        self.convc = Arena(nc, es, 'convc', 1, depth * 18, F32)
        self.atdt = Arena(nc, es, 'atdt', 1, 64, F32)
        self.sm = Arena(nc, es, 'sm', 4, 32, F32)
        self.dfac = Arena(nc, es, 'dfac', 1, 96, F32)
        self.hT = Arena(nc, es, 'hT', depth, 256, F32)
        self.hgS = Arena(nc, es, 'hgS', depth, 512, F32)
        self.xhs = Arena(nc, es, 'xhs', 1, 6 * NS * 4, F32)
        self.ps = [es.enter_context(nc.psum_tensor(f'ps{i}', [128, 512], F32)) for i in range(8)]
        self.psr = [Res(f'ps{i}') for i in range(8)]

        g0 = Res('g_cf')
        g1 = Res('g_cb')
        tr.emit('sp', lambda h: h.dma_start(out=self.cf.t[:, 0, :], in_=self.cfd), writes=[g0], dma='cst')
        tr.emit('pool', lambda h: h.dma_start(out=self.cb.t[:, 0, :], in_=self.cbd, max_dma_last_dim=4096), writes=[g1], dma='cstb')
        tr.gates += [g0, g1]
        self.dma('pool', self.pt.t[:, 0, :], self.pard, [], self.pt.rows(0, 1), 'cst')
        self.dma('pool', self.lbl.t[:, 0, :], self.lbld, [], self.lbl.rows(0, 1), 'cst')
        if self.cfg.get('stop', 9) > -2:
            self.setup_params()
        self.E('dve', lambda h: h.memset(self.convc.t[:, 0, :], 0.0), [], self.convc.rows(0, 1))
        for l in range(depth):
            self.E('dve', lambda h, l=l: h.memset(self.hT.t[:, l, :], 0.0), [], self.hT.rows(l, l + 1))
            self.E('dve', lambda h, l=l: h.memset(self.hgS.t[:, l, :], 0.0), [], self.hgS.rows(l, l + 1))

        ntile = self.nt + (1 if self.sample else 0)
        for ti in range(ntile):
            prompt = ti < self.nt
            N = 512 if prompt else NS
            t0 = ti * 512
            if prompt:
                self.dma('pool', self.xres.t[:, :, :], self.xin[:, :, t0:t0 + 512], [], self.xres.rows(0, 8), 'xin')
            else:
                self.dma('pool', self.xres.t[:, :, 0:NS], self.xs_in, [], self.xres.rows(0, 8), 'xin')
            for c in range(8):
                self.cpa(self.xb.t[:, c, 0:N], self.xres.t[:, c, 0:N], self.xres.rows(c, c + 1), self.xb.rows(c, c + 1))
            for l in range(depth):
                self.ffn(l, FF1, N)
                self.layernorm(l, 'ln1_g', 'ln1_b', N, 0.5)
                stop = self.cfg.get('stop', 9)
                if stop > 0:
                    if prompt:
                        self.mixer_prompt(l, ti)
                    else:
                        self.mixer_sample(l)
                if stop > 0 or stop == -1:
                    self.merge(l, N)
                    self.layernorm(l, 'ln2_g', 'ln2_b', N, 1.0)
                self.ffn(l, FF2, N)
                self.layernorm(l, 'ln3_g', 'ln3_b', N, 0.5)
            if prompt:
                self.dma('pool', self.yout[:, :, t0:t0 + 512], self.xres.t[:, :, :], self.xres.rows(0, 8), [], 'out')
            else:
                self.dma('pool', self.ys_out, self.xres.t[:, :, 0:NS], self.xres.rows(0, 8), [], 'out')
        tr.replay(nc, es)
        return nc


def host_prepare(inp, cfg):
    depth = cfg['depth']
    T = cfg['n_tiles'] * 512
    par, lblog, brow = layout_params(inp, depth)
    cf, cb = layout_consts()
    wts = np.stack([layout_layer_weights(inp, l) for l in range(depth)])
    xp = np.asarray(inp['x_prompt'], np.float32)
    BP = xp.shape[0]
    in_maps = []
    assert BP <= len(PROMPT_CORES)
    for c in range(NCORES):
        if c in PROMPT_CORES[:BP]:
            xT = np.ascontiguousarray(xp[PROMPT_CORES.index(c), :T].T)
            xin = np.ascontiguousarray(xT.reshape(8, 128, T).transpose(1, 0, 2))
        else:
            xin = np.zeros((128, 8, T), np.float32)
        m = {'xin': xin, 'wts': wts, 'par': par, 'lblog': lblog, 'brow': brow, 'cf': cf, 'cb': cb}
        if cfg.get('sample', True):
            sl = slice(c * NS, (c + 1) * NS)
            xs = np.asarray(inp['x_sample'], np.float32)[sl, 0, :]
            m['xs_in'] = np.ascontiguousarray(xs.T.reshape(8, 128, NS).transpose(1, 0, 2))
            kc = np.asarray(inp['cache_swa_k'], np.float32)[:depth, sl]
            m['kc_T'] = np.ascontiguousarray(kc.transpose(0, 4, 1, 3, 2)).reshape(depth, 64, NS * 2 * 128)
            m['kc_nat'] = np.ascontiguousarray(kc.reshape(depth, NS, 128, 128))
            m['vc_nat'] = np.ascontiguousarray(np.asarray(inp['cache_swa_v'], np.float32)[:depth, sl].reshape(depth, NS, 128, 128))
            cv = np.asarray(inp['state_conv'], np.float32)[:depth, sl]
            m['conv_fm'] = np.ascontiguousarray(cv.reshape(depth, NS, 3, 6, 128).transpose(0, 4, 3, 1, 2))
            m['ssm_in'] = np.ascontiguousarray(np.asarray(inp['state_ssm'], np.float32)[:depth, sl])
            m['hg_in'] = np.ascontiguousarray(np.asarray(inp['state_hgrn'], np.float32)[:depth, sl])
        in_maps.append(m)
    return in_maps


def assemble(res, cfg, BP):
    depth = cfg['depth']
    T = cfg['n_tiles'] * 512
    f = np.float32
    full = res
    res = [full[PROMPT_CORES[b]] for b in range(BP)]
    y_prompt = np.stack([res[b]['yout'].transpose(1, 0, 2).reshape(1024, T).T for b in range(BP)]).astype(f)
    p_k = np.stack([np.stack([res[b]['p_k'][l].transpose(2, 1, 0) for b in range(BP)]) for l in range(depth)]).astype(f)
    p_v = np.stack([np.stack([res[b]['p_v'][l].reshape(128, 2, 64) for b in range(BP)]) for l in range(depth)]).astype(f)
    p_conv = np.stack([np.stack([res[b]['p_conv'][l].reshape(128, 6, 3).transpose(2, 1, 0).reshape(3, 768) for b in range(BP)])
                       for l in range(depth)]).astype(f)
    p_ssm = np.stack([np.stack([res[b]['p_ssm'][l].reshape(2, 64, 4, 64).transpose(0, 2, 3, 1).reshape(8, 64, 64) for b in range(BP)])
                      for l in range(depth)]).astype(f)
    p_hg = np.stack([np.stack([res[b]['p_hg'][l].reshape(128, 4, 128).transpose(1, 0, 2) for b in range(BP)]) for l in range(depth)]).astype(f)
    outs = [y_prompt, None, p_k, p_v, p_conv, p_ssm, p_hg]
    if cfg.get('sample', True):
        res = full
        ys = np.concatenate([res[c]['ys_out'].transpose(1, 0, 2).reshape(1024, NS).T for c in range(NCORES)], axis=0)[:, None, :].astype(f)
        outs[1] = ys
        sk = np.zeros((depth, NCORES * NS, 128, 2, 64), f)
        sv = np.zeros((depth, NCORES * NS, 128, 2, 64), f)
        sc = np.zeros((depth, NCORES * NS, 3, 768), f)
        for c in range(NCORES):
            sl = slice(c * NS, (c + 1) * NS)
            r = res[c]
            sk[:, sl, :127] = r['s_kshift'].reshape(depth, NS, 127, 2, 64)
            sk[:, sl, 127] = r['s_knew'].transpose(0, 3, 2, 1)
            sv[:, sl, :127] = r['s_vshift'].reshape(depth, NS, 127, 2, 64)
            sv[:, sl, 127] = r['s_vnew'].reshape(depth, NS, 2, 64)
            sc[:, sl] = r['s_conv'].transpose(0, 3, 4, 2, 1).reshape(depth, NS, 3, 768)
        s_ssm = np.concatenate([res[c]['s_ssm'] for c in range(NCORES)], axis=1).astype(f)
        s_hg = np.concatenate([res[c]['s_hg'] for c in range(NCORES)], axis=1).astype(f)
        outs += [sk, sv, sc, s_ssm, s_hg]
    return outs


def run(inp, cfg):
    in_maps = host_prepare(inp, cfg)
    b = Builder(cfg)
    nc = b.build()
    res = run_bass_kernel_spmd(nc, in_maps, core_ids=list(range(NCORES)))
    return assemble(res.results, cfg, np.asarray(inp['x_prompt']).shape[0])


def kernel(**inputs):
    cfg = {'n_tiles': 8, 'depth': 4, 'sample': True}
    return tuple(run(inputs, cfg))
```
